# Optimizing a Trainium2 kernel written in Bass

```python
import math
import jax, jax.numpy as jnp
from jax import lax
import numpy as np

D_MODEL = 1024
BATCH = 8
SEQ = 4096
DEPTH = 2

HEAD_DIM = 64
MOBA_HEADS = D_MODEL // (2 * HEAD_DIM)
MOBA_BLOCK = 256
MOBA_TOPK = 3
NSA_HEADS = D_MODEL // (2 * HEAD_DIM)
NSA_KV_GROUPS = 2
NSA_HEADS_PER_GROUP = NSA_HEADS // NSA_KV_GROUPS
NSA_CMP_BLOCK = 32
NSA_CMP_STRIDE = 16
NSA_CMP_HIDDEN = 256
NSA_SLC_BLOCK = 64
NSA_TOPN = 16
NSA_WINDOW = 512
NSA_FORCE_SCORE = 1e6
FOX_HEADS = D_MODEL // HEAD_DIM
D_FF = 4 * D_MODEL
REL_BUCKETS = 32
REL_MAX_DISTANCE = 1024
QUERY_BLOCK = 128
SEQ_ALIGN = MOBA_BLOCK
RMS_EPS = 1e-5
NEG_INF = -1e30

MOBA_W = MOBA_HEADS * HEAD_DIM
NSA_W = NSA_HEADS * HEAD_DIM
NSA_KV_W = NSA_KV_GROUPS * HEAD_DIM
EVEN_SPLITS = (MOBA_W, MOBA_W, MOBA_W, NSA_W) + (NSA_KV_W,) * 6 + (3 * NSA_HEADS,)
EVEN_IN = sum(EVEN_SPLITS)
FOX_W = FOX_HEADS * HEAD_DIM
ODD_SPLITS = (FOX_W, FOX_W, FOX_W, FOX_HEADS)
ODD_IN = sum(ODD_SPLITS)
N_BIAS_HEADS = MOBA_HEADS + NSA_HEADS

kernel_name = 'moba_nsa_fox_hybrid_trunk'


def _split(t, sizes):
    offs = [int(o) for o in np.cumsum(sizes)[:-1]]
    return jnp.split(t, offs, axis=-1)


def rmsnorm(x, g):
    xf = x.astype(jnp.float32)
    y = xf * lax.rsqrt(jnp.mean(xf * xf, axis=-1, keepdims=True) + RMS_EPS)
    return (y * g.astype(jnp.float32)).astype(x.dtype)


def masked_softmax(logits, mask):
    logits = jnp.where(mask, logits.astype(jnp.float32), NEG_INF)
    p = jax.nn.softmax(logits, axis=-1)
    return p * jnp.any(mask, axis=-1, keepdims=True)


def rel_bucket(dist):
    n = jnp.maximum(dist, 0)
    max_exact = REL_BUCKETS // 2
    nf = jnp.maximum(n, 1).astype(jnp.float32)
    large = max_exact + (jnp.log(nf / max_exact) / math.log(REL_MAX_DISTANCE / max_exact)
                         * (REL_BUCKETS - max_exact)).astype(jnp.int32)
    large = jnp.minimum(large, REL_BUCKETS - 1)
    return jnp.where(n < max_exact, n, large)


def moba_nsa_mixer(h, w_in, w_out, rel_bias, cmp_pos_k, cmp_pos_v,
                   cmp_k_w1, cmp_k_w2, cmp_v_w1, cmp_v_w2):
    B, S, _ = h.shape
    S_pad = -(-S // SEQ_ALIGN) * SEQ_ALIGN
    hp = jnp.pad(h, ((0, 0), (0, S_pad - S), (0, 0)))
    mq, mk, mv, nq, kc, vc, ksl, vsl, kwn, vwn, gz = _split(hp @ w_in, EVEN_SPLITS)
    scale = HEAD_DIM ** -0.5
    Q = QUERY_BLOCK
    G, J = NSA_KV_GROUPS, NSA_HEADS_PER_GROUP
    n_qc = S_pad // Q
    f32 = jnp.float32

    n_mb = S_pad // MOBA_BLOCK
    k_moba = min(MOBA_TOPK, n_mb)
    mq = mq.reshape(B, S_pad, MOBA_HEADS, HEAD_DIM).transpose(0, 2, 1, 3)
    mk = mk.reshape(B, n_mb, MOBA_BLOCK, MOBA_HEADS, HEAD_DIM).transpose(0, 3, 1, 2, 4)
    mv = mv.reshape(B, n_mb, MOBA_BLOCK, MOBA_HEADS, HEAD_DIM).transpose(0, 3, 1, 2, 4)
    k_mean = jnp.mean(mk, axis=3)

    nq = nq.reshape(B, S_pad, NSA_HEADS, HEAD_DIM).transpose(0, 2, 1, 3)
    n_cmp = (S_pad - NSA_CMP_BLOCK) // NSA_CMP_STRIDE + 1
    cmp_idx = np.arange(n_cmp)[:, None] * NSA_CMP_STRIDE + np.arange(NSA_CMP_BLOCK)[None, :]
    cmp_end = jnp.asarray(cmp_idx[:, -1])

    def compress(raw, pos, w1, w2):
        blocks = raw.reshape(B, S_pad, G, HEAD_DIM)[:, cmp_idx] + pos[:, None, :]
        blocks = blocks.transpose(0, 3, 1, 2, 4).reshape(B, G, n_cmp, NSA_CMP_BLOCK * HEAD_DIM)
        return jax.nn.silu(blocks @ w1) @ w2

    k_cmp = compress(kc, cmp_pos_k, cmp_k_w1, cmp_k_w2)
    v_cmp = compress(vc, cmp_pos_v, cmp_v_w1, cmp_v_w2)
    n_sb = S_pad // NSA_SLC_BLOCK
    n_sel = min(NSA_TOPN, n_sb)
    k_slc = ksl.reshape(B, n_sb, NSA_SLC_BLOCK, G, HEAD_DIM).transpose(0, 3, 1, 2, 4)
    v_slc = vsl.reshape(B, n_sb, NSA_SLC_BLOCK, G, HEAD_DIM).transpose(0, 3, 1, 2, 4)
    pad_w = ((0, 0), (0, 0), (NSA_WINDOW, 0), (0, 0))
    k_win = jnp.pad(kwn.reshape(B, S_pad, G, HEAD_DIM).transpose(0, 2, 1, 3), pad_w)
    v_win = jnp.pad(vwn.reshape(B, S_pad, G, HEAD_DIM).transpose(0, 2, 1, 3), pad_w)
    gates = jax.nn.sigmoid(gz.reshape(B, S_pad, NSA_HEADS, 3).transpose(0, 2, 1, 3))
    ci = np.arange(n_cmp)[:, None] * NSA_CMP_STRIDE
    sj = np.arange(n_sb)[None, :] * NSA_SLC_BLOCK
    overlap = jnp.asarray(((ci < sj + NSA_SLC_BLOCK) & (ci + NSA_CMP_BLOCK > sj)).astype(np.float32))

    table = rel_bias.T
    tb_moba = table[:MOBA_HEADS]
    tb_nsa = table[MOBA_HEADS:].reshape(G, J, REL_BUCKETS)
    h_m = jnp.arange(MOBA_HEADS)[:, None, None]
    g_i = jnp.arange(G)[:, None, None]
    g5 = jnp.arange(G)[:, None, None, None, None]
    j5 = jnp.arange(J)[None, :, None, None, None]

    def query_block(bc):
        b, c = bc
        q0 = c * Q
        t = q0 + jnp.arange(Q)
        qm = lax.dynamic_slice_in_dim(mq[b], q0, Q, axis=1)
        mk_b, mv_b = mk[b], mv[b]
        blk = q0 // MOBA_BLOCK
        route = jnp.einsum('hqd,hnd->hqn', qm, k_mean[b]).astype(f32)
        route = jnp.where(jnp.arange(n_mb) < blk, route, NEG_INF)
        _, sel = lax.top_k(route, k_moba)
        sel_ok = sel < blk
        k_sel = mk_b[h_m, sel]
        v_sel = mv_b[h_m, sel]
        k_own = lax.dynamic_index_in_dim(mk_b, blk, axis=1, keepdims=False)
        v_own = lax.dynamic_index_in_dim(mv_b, blk, axis=1, keepdims=False)
        d_sel = t[:, None, None] - (sel[..., None] * MOBA_BLOCK + jnp.arange(MOBA_BLOCK))
        d_own = t[:, None] - (blk * MOBA_BLOCK + jnp.arange(MOBA_BLOCK))[None, :]
        s_sel = (jnp.einsum('hqd,hqnkd->hqnk', qm, k_sel).astype(f32) * scale
                 + tb_moba[h_m[..., None], rel_bucket(d_sel)])
        s_own = (jnp.einsum('hqd,hkd->hqk', qm, k_own).astype(f32) * scale
                 + tb_moba[:, rel_bucket(d_own)])
        n_k = k_moba * MOBA_BLOCK
        m_sel = jnp.broadcast_to(sel_ok[..., None], d_sel.shape).reshape(MOBA_HEADS, Q, n_k)
        m_own = jnp.broadcast_to((d_own >= 0)[None], s_own.shape)
        p = masked_softmax(jnp.concatenate([s_sel.reshape(MOBA_HEADS, Q, n_k), s_own], axis=-1),
                           jnp.concatenate([m_sel, m_own], axis=-1))
        o_a = (jnp.einsum('hqnk,hqnkd->hqd', p[..., :n_k].reshape(MOBA_HEADS, Q, k_moba, MOBA_BLOCK), v_sel)
               + jnp.einsum('hqk,hkd->hqd', p[..., n_k:], v_own))
        qn = lax.dynamic_slice_in_dim(nq[b], q0, Q, axis=1).reshape(G, J, Q, HEAD_DIM)
        d_cmp = t[:, None] - cmp_end[None, :]
        s_cmp = (jnp.einsum('gjqd,gnd->gjqn', qn, k_cmp[b]).astype(f32) * scale
                 + tb_nsa[:, :, rel_bucket(d_cmp)])
        p_cmp = masked_softmax(s_cmp, (d_cmp >= 0)[None, None])
        o_cmp = jnp.einsum('gjqn,gnd->gjqd', p_cmp, v_cmp[b])
        imp = jnp.einsum('gjqn,nm->gqm', p_cmp, overlap)
        s_blk = t // NSA_SLC_BLOCK
        jb = jnp.arange(n_sb)[None, :]
        forced = (jb == 0) | (jb == s_blk[:, None]) | (jb == s_blk[:, None] - 1)
        imp = jnp.where(forced, imp + NSA_FORCE_SCORE, jnp.where(jb <= s_blk[:, None], imp, NEG_INF))
        _, sidx = lax.top_k(imp, n_sel)
        k_s = k_slc[b][g_i, sidx]
        v_s = v_slc[b][g_i, sidx]
        d_s = t[:, None, None] - (sidx[..., None] * NSA_SLC_BLOCK + jnp.arange(NSA_SLC_BLOCK))
        m_s = (d_s >= 0) & (sidx <= s_blk[:, None])[..., None]
        s_s = (jnp.einsum('gjqd,gqnkd->gjqnk', qn, k_s).astype(f32) * scale
               + tb_nsa[g5, j5, rel_bucket(d_s)[:, None]])
        n_ks = n_sel * NSA_SLC_BLOCK
        p_s = masked_softmax(s_s.reshape(G, J, Q, n_ks), m_s.reshape(G, 1, Q, n_ks))
        o_slc = jnp.einsum('gjqnk,gqnkd->gjqd', p_s.reshape(G, J, Q, n_sel, NSA_SLC_BLOCK), v_s)
        k_w = lax.dynamic_slice_in_dim(k_win[b], q0, Q + NSA_WINDOW, axis=1)
        v_w = lax.dynamic_slice_in_dim(v_win[b], q0, Q + NSA_WINDOW, axis=1)
        pos_w = q0 - NSA_WINDOW + jnp.arange(Q + NSA_WINDOW)
        d_w = t[:, None] - pos_w[None, :]
        m_w = (d_w >= 0) & (d_w < NSA_WINDOW) & (pos_w >= 0)[None, :]
        s_w = (jnp.einsum('gjqd,gkd->gjqk', qn, k_w).astype(f32) * scale
               + tb_nsa[:, :, rel_bucket(d_w)])
        p_w = masked_softmax(s_w, m_w[None, None])
        o_win = jnp.einsum('gjqk,gkd->gjqd', p_w, v_w)
        g = lax.dynamic_slice_in_dim(gates[b], q0, Q, axis=1).astype(f32)
        o_b = (g[..., 0:1] * o_cmp.reshape(NSA_HEADS, Q, HEAD_DIM)
               + g[..., 1:2] * o_slc.reshape(NSA_HEADS, Q, HEAD_DIM)
               + g[..., 2:3] * o_win.reshape(NSA_HEADS, Q, HEAD_DIM))
        return jnp.concatenate([o_a.astype(f32), o_b.astype(f32)], axis=0)

    b_idx = jnp.repeat(jnp.arange(B), n_qc)
    c_idx = jnp.tile(jnp.arange(n_qc), B)
    o = lax.map(query_block, (b_idx, c_idx))
    o = o.reshape(B, n_qc, MOBA_HEADS + NSA_HEADS, Q, HEAD_DIM).transpose(0, 1, 3, 2, 4)
    o = o.reshape(B, S_pad, MOBA_W + NSA_W)[:, :S]
    return o.astype(h.dtype) @ w_out


def fox_mixer(h, w_in, b_forget, w_out):
    B, S, _ = h.shape
    Q = QUERY_BLOCK
    scale = HEAD_DIM ** -0.5
    q, k, v, fz = _split(h @ w_in, ODD_SPLITS)
    q = q.reshape(B, S, FOX_HEADS, HEAD_DIM).transpose(0, 2, 1, 3)
    k = k.reshape(B, S, FOX_HEADS, HEAD_DIM).transpose(0, 2, 1, 3)
    v = v.reshape(B, S, FOX_HEADS, HEAD_DIM).transpose(0, 2, 1, 3)
    log_f = jax.nn.log_sigmoid((fz + b_forget).astype(jnp.float32))
    cum = jnp.cumsum(log_f, axis=1).transpose(0, 2, 1)
    key_pos = jnp.arange(S)

    def block(c):
        q0 = c * Q
        qc = lax.dynamic_slice_in_dim(q, q0, Q, axis=2)
        cq = lax.dynamic_slice_in_dim(cum, q0, Q, axis=2)
        t = q0 + jnp.arange(Q)
        s = (jnp.einsum('bhqd,bhkd->bhqk', qc, k).astype(jnp.float32) * scale
             + cq[..., None] - cum[:, :, None, :])
        p = masked_softmax(s, key_pos[None, :] <= t[:, None])
        return jnp.einsum('bhqk,bhkd->bhqd', p, v).astype(jnp.float32)

    o = lax.map(block, jnp.arange(S // Q))
    o = o.transpose(1, 0, 3, 2, 4).reshape(B, S, FOX_W)
    return o.astype(h.dtype) @ w_out


def sqrelu_mlp(h, w1, w2):
    a = jax.nn.relu(h @ w1)
    return (a * a) @ w2


def setup_inputs(seed: int = 0) -> dict:
    key = jax.random.key(seed)
    ks = jax.random.split(key, 20)
    f32 = jnp.float32

    def nrm(k, shape, scale):
        return jax.random.normal(k, shape, f32) * scale

    n_even = (DEPTH + 1) // 2
    n_odd = DEPTH // 2
    cmp_in = NSA_CMP_BLOCK * HEAD_DIM
    return {
        'x': nrm(ks[0], (BATCH, SEQ, D_MODEL), 1.0),
        'rel_bias': nrm(ks[1], (REL_BUCKETS, N_BIAS_HEADS), 0.5),
        'mix_norm': 1.0 + nrm(ks[2], (DEPTH, D_MODEL), 0.02),
        'mlp_norm': 1.0 + nrm(ks[3], (DEPTH, D_MODEL), 0.02),
        'even_w_in': nrm(ks[4], (n_even, D_MODEL, EVEN_IN), D_MODEL ** -0.5),
        'even_w_out': nrm(ks[5], (n_even, MOBA_W + NSA_W, D_MODEL), (MOBA_W + NSA_W) ** -0.5),
        'cmp_pos_k': nrm(ks[6], (n_even, NSA_CMP_BLOCK, HEAD_DIM), 0.2),
        'cmp_pos_v': nrm(ks[7], (n_even, NSA_CMP_BLOCK, HEAD_DIM), 0.2),
        'cmp_k_w1': nrm(ks[8], (n_even, cmp_in, NSA_CMP_HIDDEN), cmp_in ** -0.5),
        'cmp_k_w2': nrm(ks[9], (n_even, NSA_CMP_HIDDEN, HEAD_DIM), NSA_CMP_HIDDEN ** -0.5),
        'cmp_v_w1': nrm(ks[10], (n_even, cmp_in, NSA_CMP_HIDDEN), cmp_in ** -0.5),
        'cmp_v_w2': nrm(ks[11], (n_even, NSA_CMP_HIDDEN, HEAD_DIM), NSA_CMP_HIDDEN ** -0.5),
        'odd_w_in': nrm(ks[12], (n_odd, D_MODEL, ODD_IN), D_MODEL ** -0.5),
        'odd_b_forget': 3.0 + nrm(ks[13], (n_odd, FOX_HEADS), 0.5),
        'odd_w_out': nrm(ks[14], (n_odd, FOX_W, D_MODEL), FOX_W ** -0.5),
        'mlp_w1': nrm(ks[15], (DEPTH, D_MODEL, D_FF), D_MODEL ** -0.5),
        'mlp_w2': nrm(ks[16], (DEPTH, D_FF, D_MODEL), D_FF ** -0.5),
        'final_norm': 1.0 + nrm(ks[17], (D_MODEL,), 0.02),
    }


def reference(x, rel_bias, mix_norm, mlp_norm, even_w_in, even_w_out, cmp_pos_k, cmp_pos_v,
              cmp_k_w1, cmp_k_w2, cmp_v_w1, cmp_v_w2, odd_w_in, odd_b_forget, odd_w_out,
              mlp_w1, mlp_w2, final_norm):
    h = x
    for layer in range(DEPTH):
        hn = rmsnorm(h, mix_norm[layer])
        i = layer // 2
        if layer % 2 == 0:
            h = h + moba_nsa_mixer(hn, even_w_in[i], even_w_out[i], rel_bias,
                                   cmp_pos_k[i], cmp_pos_v[i], cmp_k_w1[i], cmp_k_w2[i],
                                   cmp_v_w1[i], cmp_v_w2[i])
        else:
            h = h + fox_mixer(hn, odd_w_in[i], odd_b_forget[i], odd_w_out[i])
        h = h + sqrelu_mlp(rmsnorm(h, mlp_norm[layer]), mlp_w1[layer], mlp_w2[layer])
    return rmsnorm(h, final_norm)
```

```python
import contextlib
import numpy as np
import concourse.bass as bass
import concourse.mybir as mybir
from concourse.bass_utils import run_bass_kernel_spmd

F32 = mybir.dt.float32
BF16 = mybir.dt.bfloat16
AF = mybir.ActivationFunctionType
ALU = mybir.AluOpType
AX = mybir.AxisListType

D = 1024
S = 4096
NT = 32
DFF = 4096
EVEN_IN = 2840
ODD_IN = 3088
LS = 4608
LC = 8320
PEN = 30000.0
EPS = 1e-5


class Res:
    __slots__ = ("w", "r")

    def __init__(self):
        self.w = {}
        self.r = {}


def RL(n):
    return [Res() for _ in range(n)]


class Ctx:
    def __init__(self, nc, st):
        self.nc = nc
        self.engs = {"pe": nc.tensor, "act": nc.scalar, "dve": nc.vector, "pool": nc.gpsimd, "sp": nc.sync}
        self.sem = {k: st.enter_context(nc.semaphore("sem_" + k)) for k in ["pe", "act", "dve", "pool"]}
        self.cnt = {k: 0 for k in self.sem}
        self.seen = {k: {} for k in self.engs}
        self.hist = {}
        self.rings = {}
        for q, n in [("sp", 12), ("pool", 8), ("act", 4)]:
            self.rings[q] = dict(sems=[st.enter_context(nc.semaphore(f"dq_{q}_{i}")) for i in range(n)], i=0)
        self.last = {}
        self.flip = 0
        self.ninst = 0

    def _semfor(self, key):
        if isinstance(key, tuple):
            return self.rings[key[1]]["sems"][key[2]]
        return self.sem[key]

    def _wait(self, e, key, val):
        s = self.seen[e]
        if s.get(key, 0) >= val:
            return
        self.engs[e].wait_ge(self._semfor(key), val)
        self.ninst += 1
        s[key] = val
        snap = self.hist.get((key, val))
        if snap:
            for k2, v2 in snap.items():
                if s.get(k2, 0) < v2:
                    s[k2] = v2

    def _deps(self, e, reads, writes):
        for r in reads:
            for key, val in list(r.w.items()):
                self._wait(e, key, val)
        for w in writes:
            for key, val in list(w.w.items()):
                if key != e:
                    self._wait(e, key, val)
            for key, val in list(w.r.items()):
                if key != e:
                    self._wait(e, key, val)

    def op(self, e, fn, reads=(), writes=()):
        self._deps(e, reads, writes)
        inst = fn()
        self.cnt[e] += 1
        c = self.cnt[e]
        inst.then_inc(self.sem[e], 1)
        self.ninst += 1
        self.hist[(e, c)] = dict(self.seen[e])
        for r in reads:
            r.r[e] = c
        for w in writes:
            w.w = {e: c}
            w.r = {}
        return inst

    def dma(self, q, out, in_, reads=(), writes=(), **kw):
        ring = self.rings[q]
        i = ring["i"]
        n = len(ring["sems"])
        slot = i % n
        val = 16 * (i // n + 1)
        key = ("d", q, slot)
        if val > 16:
            self._wait(q, key, val - 16)
        for r in reads:
            for k2, v2 in list(r.w.items()):
                self._wait(q, k2, v2)
        for w in writes:
            for k2, v2 in list(w.w.items()):
                if k2 != q and not (isinstance(k2, tuple) and k2[1] == q):
                    self._wait(q, k2, v2)
            for k2, v2 in list(w.r.items()):
                if k2 != q:
                    self._wait(q, k2, v2)
        inst = self.engs[q].dma_start(out=out, in_=in_, **kw)
        inst.then_inc(ring["sems"][slot], 16)
        self.ninst += 1
        ring["i"] += 1
        self.hist[(key, val)] = dict(self.seen[q])
        self.last[key] = val
        for r in reads:
            r.r[key] = val
        for w in writes:
            w.w[key] = val
            w.r = {}

    def barrier(self):
        targets = [(k, self.cnt[k]) for k in self.sem if self.cnt[k] > 0] + list(self.last.items())
        for e in self.engs:
            for key, val in targets:
                if key == e:
                    continue
                self._wait(e, key, val)

    def mm(self, out, lhsT, rhs, start, stop, reads, writes):
        return self.op("pe", lambda: self.nc.tensor.matmul(out, lhsT, rhs, start=start, stop=stop), reads, writes)

    def tr(self, out, in_, ident, reads, writes):
        return self.op("pe", lambda: self.nc.tensor.transpose(out, in_, ident), reads, writes)

    def act(self, out, in_, func, reads, writes, **kw):
        return self.op("act", lambda: self.nc.scalar.activation(out=out, in_=in_, func=func, **kw), reads, writes)

    def evac(self, out, in_, reads, writes, scale=None, eng=None):
        if eng is None:
            self.flip ^= 1
            eng = "act" if self.flip else "dve"
        if eng == "act":
            if scale is None:
                return self.op("act", lambda: self.nc.scalar.copy(out=out, in_=in_), reads, writes)
            return self.op("act", lambda: self.nc.scalar.mul(out=out, in_=in_, mul=scale), reads, writes)
        if scale is None:
            return self.op("dve", lambda: self.nc.vector.tensor_copy(out=out, in_=in_), reads, writes)
        return self.op("dve", lambda: self.nc.vector.tensor_scalar(out=out, in0=in_, scalar1=scale, scalar2=None,
                                                                   op0=ALU.mult), reads, writes)

    def dve(self, fn, reads, writes):
        return self.op("dve", fn, reads, writes)


class Phase:
    uid = 0

    def __init__(self, ctx):
        self.ctx = ctx
        self.st = contextlib.ExitStack()

    def __enter__(self):
        self.st.__enter__()
        return self

    def __exit__(self, *a):
        self.ctx.barrier()
        return self.st.__exit__(*a)

    def sb(self, name, shape, dt):
        Phase.uid += 1
        return self.st.enter_context(self.ctx.nc.sbuf_tensor(f"{name}_u{Phase.uid}", list(shape), dt))


def dram_rows(t, r0, nr, ncols, c0=0, rowlen=None):
    return t.ap()[r0:r0 + nr, c0:c0 + ncols]


def build(nc, upto=99, dbg=False):
    st = contextlib.ExitStack()
    with st:
        _build(nc, st, upto, dbg)
    return nc


def _build(nc, st, upto, dbg):
    C = Ctx(nc, st)
    ein = lambda name, shape: nc.dram_tensor(name, list(shape), F32, kind="ExternalInput")
    x = ein("x", [S, D])
    rel_bias = ein("rel_bias", [32, 16])
    mix_norm = ein("mix_norm", [2, D])
    mlp_norm = ein("mlp_norm", [2, D])
    even_w_in = ein("even_w_in", [D, EVEN_IN])
    even_w_out = ein("even_w_out", [D, D])
    cmp_pos_k = ein("cmp_pos_k", [32, 64])
    cmp_pos_v = ein("cmp_pos_v", [32, 64])
    cmp_k_w1 = ein("cmp_k_w1", [2048, 256])
    cmp_k_w2 = ein("cmp_k_w2", [256, 64])
    cmp_v_w1 = ein("cmp_v_w1", [2048, 256])
    cmp_v_w2 = ein("cmp_v_w2", [256, 64])
    odd_w_in = ein("odd_w_in", [D, ODD_IN])
    odd_b_forget = ein("odd_b_forget", [16, 1])
    odd_w_out = ein("odd_w_out", [D, D])
    mlp_w1 = ein("mlp_w1", [2 * D, DFF])
    mlp_w2 = ein("mlp_w2", [2 * DFF, D])
    final_norm = ein("final_norm", [1, D])
    c_ident = ein("c_ident", [128, 128])
    c_ohc = ein("c_ohc", [33, LS])
    c_ohw = ein("c_ohw", [33, LS])
    c_ohcmp = ein("c_ohcmp", [33, LC])
    c_cm16 = ein("c_cm16", [128, 512])
    c_own16 = ein("c_own16", [128, 512])
    c_force = ein("c_force", [128, 2048])
    c_overlap = ein("c_overlap", [256, 64])
    c_ohb64 = ein("c_ohb64", [64, S])
    c_ohb16 = ein("c_ohb16", [16, S])
    c_tri = ein("c_tri", [128, 128])
    c_ones = ein("c_ones", [128, S])

    out = nc.dram_tensor("out", [S, D], F32, kind="ExternalOutput")
    dbgset = dbg if isinstance(dbg, (set, list, tuple)) else None
    def scr(name, shape, dt):
        ext = (name in dbgset) if dbgset is not None else bool(dbg)
        return nc.dram_tensor(name, list(shape), dt, kind="ExternalOutput" if ext else "Internal")
    FT0 = scr("FT0", [16 * 128, S], BF16)
    TM0 = scr("TM0", [S, 768], BF16)
    STRC = scr("STRC", [16 * 128, LS], BF16)
    STRW = scr("STRW", [8 * 128, LS], BF16)
    STRK = scr("STRK", [8 * 128, LC], BF16)
    PENM = scr("PENM", [8 * 16, S], BF16)
    PENS = scr("PENS", [2 * 64, S], BF16)
    KCMP = scr("KCMP", [2 * 64, 256], BF16)
    VCMP = scr("VCMP", [2 * 256, 64], BF16)
    OATT = scr("OATT", [S, D], BF16)
    H1 = scr("H1", [S, D], F32)
    H2 = scr("H2", [S, D], F32)
    FT1 = scr("FT1", [16 * 128, S], BF16)
    TM1 = scr("TM1", [S, D], BF16)
    CUMP = scr("CUMP", [6 * 16, S], BF16)
    H3 = scr("H3", [S, D], F32)
    B_OHB64 = scr("B_OHB64", [64, S], BF16)
    B_OHB16 = scr("B_OHB16", [16, S], BF16)
    B_ONES = scr("B_ONES", [8, S], BF16)
    B_OVL = scr("B_OVL", [256, 64], BF16)

    gsb = lambda name, shape, dt: st.enter_context(nc.sbuf_tensor(name, list(shape), dt))
    banks = [st.enter_context(nc.psum_tensor(f"bank{i}", [128, 512], F32)) for i in range(8)]
    bres = RL(8)
    identf = gsb("identf", [128, 128], F32)
    identb = gsb("identb", [128, 128], BF16)
    gates = gsb("gates", [128, NT, 24], F32)
    r_const = Res()
    r_gates = Res()

    C.dma("sp", identf[:], c_ident.ap(), writes=[r_const])
    C.dve(lambda: nc.vector.tensor_copy(out=identb[:], in_=identf[:]), [r_const], [r_const])
    C.dma("pool", B_OHB64.ap(), c_ohb64.ap())
    C.dma("pool", B_OHB16.ap(), c_ohb16.ap())
    C.dma("pool", B_ONES.ap(), c_ones.ap()[0:8, :])
    C.dma("pool", B_OVL.ap(), c_overlap.ap())

    def bankbf(i):
        return banks[i][:, :].bitcast(BF16)

    def load_w_bf16(ph, name, src, r0, nk, ncols):
        w = ph.sb(name, [128, nk, ncols], BF16)
        res = Res()
        for k in range(nk):
            c0 = 0
            while c0 < ncols:
                c1 = min(ncols, c0 + 2048)
                C.dma("pool", w[:, k, c0:c1], src.ap()[r0 + k * 128:r0 + (k + 1) * 128, c0:c1], writes=[res])
                c0 = c1
        return w, res

    def bcast_row(ph, name, src, row):
        g = ph.sb(name, [128, D], F32)
        res = Res()
        C.dma("sp", g[:], bass.AP(src, row * D, [[0, 128], [1, D]]), writes=[res])
        return g, res

    def rmsnorm_T(ph, tag, xt, r_x, g, r_g, hn, r_hn, small, r_small, hnT_dst, r_dst, bank_i):
        junk, ss, rstd = small
        C.dve(lambda: nc.vector.scalar_tensor_tensor(out=junk[:], in0=xt, scalar=1.0, in1=xt, op0=ALU.mult,
                                                     op1=ALU.mult, accum_out=ss[:]), [r_x], [r_small])
        C.dve(lambda: nc.vector.tensor_scalar(out=rstd[:], in0=ss[:], scalar1=1.0 / D, scalar2=EPS, op0=ALU.mult,
                                              op1=ALU.add), [r_small], [r_small])
        C.act(rstd[:], rstd[:], AF.Sqrt, [r_small], [r_small])
        C.dve(lambda: nc.vector.reciprocal(out=rstd[:], in_=rstd[:]), [r_small], [r_small])
        C.dve(lambda: nc.vector.scalar_tensor_tensor(out=hn[:], in0=xt, scalar=rstd[:, 0:1], in1=g[:], op0=ALU.mult,
                                                     op1=ALU.mult), [r_x, r_small, r_g], [r_hn])
        bb = bankbf(bank_i)
        for k in range(8):
            C.tr(bb[:, k * 128:(k + 1) * 128], hn[:, k * 128:(k + 1) * 128], identb[:], [r_hn, r_const],
                 [bres[bank_i]])
        C.evac(hnT_dst, bb.rearrange("p (k t) -> p k t", k=8), [bres[bank_i]], [r_dst])

    if upto >= 0:
        with Phase(C) as ph:
            tba = ph.sb("tba", [65, 16], F32)
            tbh = ph.sb("tbh", [65, 16], BF16)
            tbrep = ph.sb("tbrep", [65, 16, 128], BF16)
            ones65 = ph.sb("ones65", [65, 128], BF16)
            ohc = ph.sb("ohc", [65, LS], BF16)
            ohw = ph.sb("ohw", [65, LS], BF16)
            ohk = ph.sb("ohk", [65, LC], BF16)
            r0 = Res()
            rt = Res()
            C.dve(lambda: nc.vector.memset(tbh[:], -PEN), [], [rt])
            C.dve(lambda: nc.vector.memset(ones65[:], 1.0), [], [rt])
            C.dma("sp", tba[0:32, :], rel_bias.ap(), writes=[rt])
            C.dma("sp", tba[32:64, :], rel_bias.ap(), writes=[rt])
            for (dst_t, src_t, L) in [(ohc, c_ohc, LS), (ohw, c_ohw, LS), (ohk, c_ohcmp, LC)]:
                c0 = 0
                while c0 < L:
                    c1 = min(L, c0 + 2048)
                    C.dma("pool", dst_t[0:32, c0:c1], src_t.ap()[0:32, c0:c1], writes=[r0])
                    C.dma("pool", dst_t[32:64, c0:c1], src_t.ap()[0:32, c0:c1], writes=[r0])
                    C.dma("pool", dst_t[64:65, c0:c1], src_t.ap()[32:33, c0:c1], writes=[r0])
                    c0 = c1
            C.dve(lambda: nc.vector.tensor_copy(out=tbh[0:64, :], in_=tba[0:64, :]), [rt], [rt])
            C.dve(lambda: nc.vector.tensor_tensor(out=tba[32:64, :], in0=tba[32:64, :], in1=tbh[32:64, :],
                                                  op=ALU.subtract), [rt], [rt])
            C.dve(lambda: nc.vector.tensor_copy(out=tbh[32:64, :], in_=tba[32:64, :]), [rt], [rt])
            r1 = Res()
            for h in range(16):
                C.dve(lambda h=h: nc.vector.tensor_scalar(out=tbrep[:, h, :], in0=ones65[:], scalar1=tbh[:, h:h + 1],
                                                          scalar2=None, op0=ALU.mult), [rt], [r1])
            stg = [ph.sb(f"stg{i}", [128, LC], BF16) for i in range(2)]
            rstg = RL(2)
            jobs = [(ohc, LS, h, STRC, h) for h in range(16)]
            jobs += [(ohw, LS, 8 + hn, STRW, hn) for hn in range(8)]
            jobs += [(ohk, LC, 8 + hn, STRK, hn) for hn in range(8)]
            bi = 0
            for ji, (oh, L, trow, dst, di) in enumerate(jobs):
                sg, rs = stg[ji % 2], rstg[ji % 2]
                c0 = 0
                while c0 < L:
                    c1 = min(L, c0 + 512)
                    b = bi % 4
                    bi += 1
                    C.mm(banks[b][:, 0:c1 - c0], tbrep[:, trow, :], oh[:, c0:c1], True, True, [r0, r1], [bres[b]])
                    C.act(sg[:, c0:c1], banks[b][:, 0:c1 - c0], AF.Exp, [bres[b]], [rs])
                    c0 = c1
                C.dma("pool", dst.ap()[di * 128:(di + 1) * 128, :], sg[:, 0:L], reads=[rs])
    if upto < 1:
        return _finish(C, nc)

    def proj_phase(layer, src_h, w_in_t, ncols, gsrc, fm_list, tm_groups, fm_special, tm_handler, pre=None):
        with Phase(C) as ph:
            win, r_win = load_w_bf16(ph, "win", w_in_t, 0, 8, ncols)
            g, r_g = bcast_row(ph, "gmix", gsrc, layer)
            xts = [ph.sb(f"xt{i}", [128, D], F32) for i in range(3)]
            r_xt = RL(3)
            hns = [ph.sb(f"hn{i}", [128, D], BF16) for i in range(2)]
            r_hn = RL(2)
            junk = ph.sb("junk", [128, D], BF16)
            smalls = [(junk, ph.sb(f"ss{i}", [128, 1], F32), ph.sb(f"rstd{i}", [128, 1], F32)) for i in range(2)]
            r_sm = RL(2)
            hnT = [ph.sb(f"hnT{i}", [128, 8, 512], BF16) for i in range(2)]
            r_hnT = [RL(4) for _ in range(2)]
            fsb = [ph.sb(f"fsb{i}", [128, 512], BF16) for i in range(4)]
            r_fsb = RL(4)
            tsb = [ph.sb(f"tsb{i}", [128, 1024], BF16) for i in range(2)]
            r_tsb = RL(2)
            env = dict(ph=ph)
            if pre is not None:
                pre(env)

            def ld(tt):
                i = tt % 3
                C.dma("sp", xts[i][:], src_h.ap()[tt * 128:(tt + 1) * 128, :], writes=[r_xt[i]])

            ld(0)
            ld(1)
            fi = 0
            for c in range(8):
                hb = c % 2
                for i in range(4):
                    tt = 4 * c + i
                    if tt + 2 < NT:
                        ld(tt + 2)
                    rmsnorm_T(ph, "p", xts[tt % 3][:], r_xt[tt % 3], g, r_g, hns[tt % 2], r_hn[tt % 2],
                              smalls[tt % 2], r_sm[tt % 2], hnT[hb][:, :, i * 128:(i + 1) * 128], r_hnT[hb][i],
                              6 + (tt % 2))
                for m, (col0, width, scale, dst) in enumerate(fm_list):
                    b = m % 3
                    for k in range(8):
                        C.mm(banks[b][0:width, :], win[:, k, col0:col0 + width], hnT[hb][:, k, :], k == 0, k == 7,
                             [r_win] + r_hnT[hb], [bres[b]])
                    if dst is None:
                        fm_special(env, c, banks[b], bres[b])
                        continue
                    f = fi % 4
                    fi += 1
                    C.evac(fsb[f][0:width, :], banks[b][0:width, :], [bres[b]], [r_fsb[f]], scale=scale)
                    C.dma("pool", dst[:, c * 512:(c + 1) * 512], fsb[f][0:width, :], reads=[r_fsb[f]])
                for i in range(4):
                    tt = 4 * c + i
                    for gi, grp in enumerate(tm_groups):
                        b = 3 + (2 * i + gi) % 3
                        o = 0
                        for (col0, width) in grp:
                            for k in range(8):
                                C.mm(banks[b][:, o:o + width], hnT[hb][:, k, i * 128:(i + 1) * 128],
                                     win[:, k, col0:col0 + width], k == 0, k == 7, [r_win, r_hnT[hb][i]], [bres[b]])
                            o += width
                        tm_handler(env, tt, gi, banks[b], bres[b], tsb[tt % 2], r_tsb[tt % 2])

    fm0_cols = [0, 128, 256, 384, 512, 640, 768, 896, 1536, 1664, 1792, 1920, 2048, 2176, 2304, 2560]
    fm0 = []
    for m, col0 in enumerate(fm0_cols):
        isq = m < 4 or 8 <= m < 12
        fm0.append((col0, 128, 0.125 if isq else None, FT0.ap()[m * 128:(m + 1) * 128, :]))
    tm0_groups = [[(1024, 512)], [(2432, 128), (2688, 128), (2816, 24)]]
    gtmp = gsb("gtmp", [128, 24], F32)
    r_gtmp = Res()

    def tm0_handler(env, tt, gi, bank, rb, tsb, r_tsb):
        if gi == 0:
            C.evac(tsb[:, 0:512], bank[:, 0:512], [rb], [r_tsb])
        else:
            C.evac(tsb[:, 512:768], bank[:, 0:256], [rb], [r_tsb])
            C.act(gates[:, tt, :], bank[:, 256:280], AF.Sigmoid, [rb], [r_gates])
            C.dma("pool", TM0.ap()[tt * 128:(tt + 1) * 128, :], tsb[:, 0:768], reads=[r_tsb])

    proj_phase(0, x, even_w_in, EVEN_IN, mix_norm, fm0, tm0_groups, None, tm0_handler)
    if upto < 2:
        return _finish(C, nc)

    def alloc_attn(ph, nkmax, vwmax, wlen):
        sets = []
        for i in range(2):
            sets.append(dict(QA=ph.sb(f"QA{i}", [128, S], BF16), KA=ph.sb(f"KA{i}", [128, nkmax], BF16),
                             VA=ph.sb(f"VA{i}", [128, nkmax // 128, vwmax], BF16),
                             W=ph.sb(f"W{i}", [128, wlen], BF16), r=Res(),
                             fb=ph.sb(f"fb{i}", [128, 2], F32), rfb=Res()))
        A = dict(sets=sets, P=[ph.sb(f"P{i}", [128, 512], BF16) for i in range(6)], rP=RL(6), pi=0,
                 den=[ph.sb(f"den{i}", [128, 2], F32) for i in range(8)], rden=RL(8), di=0, defer=[], mi=0)
        return A

    def load_job(aset, job):
        r = aset["r"]
        deps = job.get("deps", [])
        for (row0, nrows, src) in job["qa"]:
            C.dma("sp", aset["QA"][row0:row0 + nrows, :], src, reads=deps, writes=[r])
        for (row0, nrows, ncols, src) in job["ka"]:
            C.dma("sp", aset["KA"][row0:row0 + nrows, 0:ncols], src, reads=deps, writes=[r])
        vsrc, nkt, vcols = job["va"]
        C.dma("sp", aset["VA"][:, 0:nkt, 0:vcols], vsrc.rearrange("(t p) c -> p t c", p=128), reads=deps, writes=[r])
        for (col0, ncols, tensor, off, pstep) in job["w"]:
            C.dma("sp", aset["W"][:, col0:col0 + ncols], bass.AP(tensor, off, [[pstep, 128], [1, ncols]]), writes=[r])

    def compute_job(A, aset, job):
        KQ, VW = job["KQ"], job["VW"]
        r = aset["r"]
        QA, KA, VA = aset["QA"], aset["KA"], aset["VA"]
        P, rP = A["P"], A["rP"]
        if job.get("far"):
            C.act(aset["fb"][:, 0:1], aset["W"][:, 4479:4480], AF.Ln, [r], [aset["rfb"]])
        for qc in range(8):
            tiles = job["tiles"](qc)
            first, last = {}, {}
            for idx, (kt, jlo, jhi) in enumerate(tiles):
                for j in range(jlo, jhi + 1):
                    first.setdefault(j, idx)
                    last[j] = idx

            def emit_pv(pd):
                idx, kt, jlo, jhi, p = pd
                for j in range(jlo, jhi + 1):
                    C.mm(banks[3 + j][:, 0:VW], P[p][:, j * 128:(j + 1) * 128], VA[:, kt, 0:VW], first[j] == idx,
                         last[j] == idx, [rP[p], r], [bres[3 + j]])

            pend = []
            first_pv = [True]

            def do_pv(pd):
                if first_pv[0]:
                    first_pv[0] = False
                    for f in A["defer"]:
                        f()
                    A["defer"] = []
                emit_pv(pd)

            for idx, (kt, jlo, jhi) in enumerate(tiles):
                sbi = idx % 3
                c0, c1 = jlo * 128, (jhi + 1) * 128
                C.mm(banks[sbi][:, c0:c1], KA[0:KQ, kt * 128:(kt + 1) * 128], QA[0:KQ, qc * 512 + c0:qc * 512 + c1],
                     True, True, [r], [bres[sbi]])
                p = A["pi"] % 6
                A["pi"] += 1
                if job.get("far") and 512 * qc - 128 * kt >= 1152:
                    C.act(P[p][:, c0:c1], banks[sbi][:, c0:c1], AF.Exp, [bres[sbi], aset["rfb"]], [rP[p]],
                          bias=aset["fb"][:, 0:1])
                else:
                    C.act(P[p][:, c0:c1], banks[sbi][:, c0:c1], AF.Exp, [bres[sbi]], [rP[p]])
                    for (a0, a1, wap, rw) in job["wmul"](aset, kt, qc, jlo, jhi):
                        A["mi"] += 1
                        if job.get("pool_share") and A["mi"] % 3 == 0:
                            C.op("pool", lambda a0=a0, a1=a1, wap=wap, p=p: nc.gpsimd.tensor_tensor(
                                out=P[p][:, a0:a1], in0=P[p][:, a0:a1], in1=wap, op=ALU.mult), [rP[p], rw], [rP[p]])
                        else:
                            C.dve(lambda a0=a0, a1=a1, wap=wap, p=p: nc.vector.tensor_tensor(
                                out=P[p][:, a0:a1], in0=P[p][:, a0:a1], in1=wap, op=ALU.mult), [rP[p], rw], [rP[p]])
                if len(pend) >= 2:
                    do_pv(pend.pop(0))
                pend.append((idx, kt, jlo, jhi, p))
            for pd in pend:
                do_pv(pd)
            for j in range(4):
                A["defer"].append(lambda j=j, qc=qc: job["fin"](A, qc * 4 + j, banks[3 + j], bres[3 + j]))
            if job.get("hook"):
                job["hook"](qc)
        for f in A["defer"]:
            f()
        A["defer"] = []

    def run_jobs(A, jobs):
        for i, job in enumerate(jobs):
            if i == 0 or job.get("late"):
                load_job(A["sets"][i % 2], job)
            if i + 1 < len(jobs) and not jobs[i + 1].get("late"):
                load_job(A["sets"][(i + 1) % 2], jobs[i + 1])
            compute_job(A, A["sets"][i % 2], job)
            if job.get("after"):
                job["after"]()

    def rden_of(A, bank, rb, col):
        d = A["di"] % 8
        A["di"] += 1
        den, rd = A["den"][d], A["rden"][d]
        C.dve(lambda: nc.vector.tensor_scalar(out=den[:, 0:1], in0=bank[:, col:col + 1], scalar1=1e-30, scalar2=None,
                                              op0=ALU.max), [rb], [rd])
        C.dve(lambda: nc.vector.reciprocal(out=den[:, 0:1], in_=den[:, 0:1]), [rd], [rd])
        return den, rd

    def causal_tiles(qc):
        return [(kt, max(0, kt - 4 * qc), 3) for kt in range(4 * qc + 4)]

    def window_tiles(qc):
        out_ = []
        for kt in range(max(0, 4 * qc - 4), 4 * qc + 4):
            rel = kt - 4 * qc
            out_.append((kt, max(0, rel), min(3, rel + 4)))
        return out_

    def strip_wmul(aset, kt, qc, jlo, jhi):
        j0 = 512 * qc - 128 * kt + 384
        c0, c1 = jlo * 128, (jhi + 1) * 128
        return [(c0, c1, aset["W"][:, j0 + c0:j0 + c1], aset["r"])]

    def ft_rows(FT, row0, n=64):
        return FT.ap()[row0:row0 + n, :]

    if upto >= 2:
        with Phase(C) as ph:
            A = alloc_attn(ph, S, 65, 4480)
            for aset in A["sets"]:
                C.dve(lambda aset=aset: nc.vector.memset(aset["VA"][:, :, 64:65], 1.0), [], [aset["r"]])
            cm = ph.sb("cm", [128, 512], F32)
            own = ph.sb("own", [128, 512], F32)
            r_c2 = Res()
            C.dma("sp", cm[:], c_cm16.ap(), writes=[r_c2])
            C.dma("sp", own[:], c_own16.ap(), writes=[r_c2])
            kTp = [ph.sb(f"kTp{i}", [64, S], BF16) for i in range(2)]
            qTp = [ph.sb(f"qTp{i}", [64, S], BF16) for i in range(2)]
            r_kq = RL(2)
            kmf = ph.sb("kmf", [64, 16], F32)
            kmb = ph.sb("kmb", [64, 16], BF16)
            rm = ph.sb("rm", [128, 512], F32)
            sel = ph.sb("sel", [128, 512], F32)
            penf = ph.sb("penf", [128, 512], F32)
            m8 = ph.sb("m8", [128, NT, 8], F32)
            penT = [ph.sb(f"penT{i}", [16, S], BF16) for i in range(2)]
            r_penT = RL(2)
            r_prep = Res()
            r_rm, r_m8, r_selm = Res(), Res(), Res()
            r_penm = RL(8)

            def prep_load(h):
                i = h % 2
                C.dma("sp", kTp[i][:], ft_rows(FT0, (4 + h // 2) * 128 + (h % 2) * 64), writes=[r_kq[i]])
                C.dma("sp", qTp[i][:], ft_rows(FT0, (h // 2) * 128 + (h % 2) * 64), writes=[r_kq[i]])

            prep_load(0)
            for h in range(8):
                i = h % 2
                if h + 1 < 8:
                    prep_load(h + 1)
                C.dve(lambda: nc.vector.tensor_reduce(out=kmf[:], in_=kTp[i][:, :].rearrange("p (n k) -> p n k", k=256),
                                                      axis=AX.X, op=ALU.add), [r_kq[i]], [r_prep])
                C.dve(lambda: nc.vector.tensor_scalar(out=kmb[:], in0=kmf[:], scalar1=1.0 / 256, scalar2=None,
                                                      op0=ALU.mult), [r_prep], [r_prep])
                for qt in range(NT):
                    C.mm(banks[6][:, qt * 16:(qt + 1) * 16], qTp[i][:, qt * 128:(qt + 1) * 128], kmb[:], True, True,
                         [r_kq[i], r_prep], [bres[6]])
                C.dve(lambda: nc.vector.tensor_tensor(out=rm[:], in0=banks[6][:, :], in1=cm[:], op=ALU.add),
                      [bres[6], r_c2], [r_rm])
                for qt in range(NT):
                    C.dve(lambda qt=qt: nc.vector.max(out=m8[:, qt, :], in_=rm[:, qt * 16:(qt + 1) * 16]),
                          [r_rm], [r_m8])
                for qt in range(NT):
                    C.dve(lambda qt=qt: nc.vector.tensor_scalar(out=sel[:, qt * 16:(qt + 1) * 16],
                                                                in0=rm[:, qt * 16:(qt + 1) * 16],
                                                                scalar1=m8[:, qt, 2:3], scalar2=None, op0=ALU.is_ge),
                          [r_rm, r_m8], [r_selm])
                C.dve(lambda: nc.vector.tensor_tensor(out=sel[:], in0=sel[:], in1=own[:], op=ALU.max),
                      [r_selm, r_c2], [r_prep])
                C.dve(lambda: nc.vector.tensor_scalar(out=penf[:], in0=sel[:], scalar1=PEN, scalar2=-PEN, op0=ALU.mult,
                                                      op1=ALU.add), [r_prep], [r_prep])
                for g4 in range(8):
                    for u in range(4):
                        qt = 4 * g4 + u
                        C.tr(banks[7][0:16, u * 128:(u + 1) * 128], penf[:, qt * 16:(qt + 1) * 16], identf[:],
                             [r_prep, r_const], [bres[7]])
                    C.evac(penT[i][:, g4 * 512:(g4 + 1) * 512], banks[7][0:16, :], [bres[7]], [r_penT[i]])
                C.dma("pool", PENM.ap()[h * 16:(h + 1) * 16, :], penT[i][:], reads=[r_penT[i]], writes=[r_penm[h]])

            osb = [ph.sb(f"osb{i}", [128, NT, 64], BF16) for i in range(2)]
            r_osb = RL(2)

            def simple_fin(oi):
                def fin(A, qt, bank, rb):
                    den, rd = rden_of(A, bank, rb, 64)
                    C.act(osb[oi][:, qt, :], bank[:, 0:64], AF.Copy, [rb, rd], [r_osb[oi]], scale=den[:, 0:1])
                return fin

            def simple_after(oi, dst, col0):
                def after():
                    C.dma("pool", dst.ap()[:, col0:col0 + 64].rearrange("(t p) c -> p t c", p=128), osb[oi][:],
                          reads=[r_osb[oi]])
                return after

            jobs = []
            for h in range(8):
                jobs.append(dict(
                    qa=[(0, 64, ft_rows(FT0, (h // 2) * 128 + (h % 2) * 64)), (64, 16, PENM.ap()[h * 16:(h + 1) * 16, :])],
                    ka=[(0, 64, S, ft_rows(FT0, (4 + h // 2) * 128 + (h % 2) * 64)), (64, 16, S, B_OHB16.ap())],
                    va=(TM0.ap()[:, h * 64:(h + 1) * 64], NT, 64),
                    w=[(0, 4480, STRC, h * 128 * LS + 128, LS - 1)],
                    deps=[r_penm[h]], KQ=80, VW=65, tiles=causal_tiles, wmul=strip_wmul, far=True, pool_share=True,
                    fin=simple_fin(h % 2), after=simple_after(h % 2, OATT, h * 64)))
            run_jobs(A, jobs)
    if upto < 3:
        return _finish(C, nc)

    with Phase(C) as ph:
        w1s = ph.sb("w1s", [64, 32, 256], BF16)
        w2s = ph.sb("w2s", [128, 2, 64], BF16)
        posf = ph.sb("posf", [64, 32], F32)
        pos2 = ph.sb("pos2", [64, 32, 2], BF16)
        rawT = ph.sb("rawT", [64, S], BF16)
        cb = ph.sb("cb", [128, 2], F32)
        HT = ph.sb("HT", [128, 2, 256], BF16)
        kcs = ph.sb("kcs", [64, 256], BF16)
        vcs = ph.sb("vcs", [128, 2, 64], BF16)
        r_w, r_raw, r_cb, r_HT, r_o = Res(), Res(), Res(), Res(), Res()
        C.dve(lambda: nc.vector.memset(HT[:], 0.0), [], [r_HT])
        for kv in range(2):
            w1src = [cmp_k_w1, cmp_v_w1][kv]
            w2src = [cmp_k_w2, cmp_v_w2][kv]
            possrc = [cmp_pos_k, cmp_pos_v][kv]
            for l0 in range(0, 32, 8):
                C.dma("pool", w1s[:, l0:l0 + 8, :],
                      w1src.ap()[l0 * 64:(l0 + 8) * 64, :].rearrange("(l d) h -> d l h", d=64), writes=[r_w])
            C.dma("pool", w2s[:], w2src.ap().rearrange("(t p) c -> p t c", p=128), writes=[r_w])
            C.dma("sp", posf[:], possrc.ap().rearrange("l d -> d l"), writes=[r_w], allow_slow_non_contiguous=True)
            for u in range(2):
                C.dve(lambda u=u: nc.vector.tensor_copy(out=pos2[:, :, u], in_=posf[:]), [r_w], [r_w])
            for ht in range(2):
                for l in range(32):
                    C.mm(banks[6][:, ht * 2:ht * 2 + 2], w1s[:, l, ht * 128:(ht + 1) * 128], pos2[:, l, :], l == 0,
                         l == 31, [r_w], [bres[6]])
            C.evac(cb[:, 0:1], banks[6][:, 0:1], [bres[6]], [r_cb], eng="dve")
            C.evac(cb[:, 1:2], banks[6][:, 2:3], [bres[6]], [r_cb], eng="dve")
            for g in range(2):
                C.dma("sp", rawT[:], ft_rows(FT0, (12 + kv) * 128 + g * 64), writes=[r_raw])
                rv = rawT[:, :].rearrange("p (n s) -> p n s", s=16)
                for ht in range(2):
                    for l in range(32):
                        rhs = rv[:, 0:255, l] if l < 16 else rv[:, 1:256, l - 16]
                        C.mm(banks[ht][:, 0:255], w1s[:, l, ht * 128:(ht + 1) * 128], rhs, l == 0, l == 31,
                             [r_w, r_raw], [bres[ht]])
                    C.act(HT[:, ht, 0:255], banks[ht][:, 0:255], AF.Silu, [bres[ht], r_cb], [r_HT],
                          bias=cb[:, ht:ht + 1])
                if kv == 0:
                    for ht in range(2):
                        C.mm(banks[2][0:64, 0:256], w2s[:, ht, :], HT[:, ht, :], ht == 0, ht == 1, [r_w, r_HT],
                             [bres[2]])
                    C.evac(kcs[:], banks[2][0:64, 0:256], [bres[2]], [r_o])
                    C.dma("pool", KCMP.ap()[g * 64:(g + 1) * 64, :], kcs[:], reads=[r_o])
                else:
                    for nt in range(2):
                        for ht in range(2):
                            C.mm(banks[3 + nt][:, 0:64], HT[:, ht, nt * 128:(nt + 1) * 128], w2s[:, ht, :], ht == 0,
                                 ht == 1, [r_w, r_HT], [bres[3 + nt]])
                        C.evac(vcs[:, nt, :], banks[3 + nt][:, 0:64], [bres[3 + nt]], [r_o])
                    C.dma("pool", VCMP.ap()[g * 256:(g + 1) * 256, :].rearrange("(t p) c -> p t c", p=128), vcs[:],
                          reads=[r_o])
    if upto < 4:
        return _finish(C, nc)

    with Phase(C) as ph:
        A = alloc_attn(ph, S, 129, 8192)
        for aset in A["sets"]:
            C.dve(lambda aset=aset: nc.vector.memset(aset["VA"][:, :, 64:65], 1.0), [], [aset["r"]])
            C.dma("sp", aset["VA"][:, 0:2, 65:129], B_OVL.ap().rearrange("(t p) c -> p t c", p=128),
                  writes=[aset["r"]])
        oacc = ph.sb("oacc", [128, NT, 512], F32)
        imps = [ph.sb(f"imp{g}", [128, NT, 64], F32) for g in range(2)]
        r_imps = RL(2)
        force = ph.sb("force", [128, NT, 64], F32)
        r_oacc, r_force = Res(), Res()
        C.dma("sp", force[:].rearrange("p t m -> p (t m)"), c_force.ap(), writes=[r_force])
        sg = [ph.sb(f"sg{i}", [128, 2], F32) for i in range(4)]
        r_sg = RL(4)
        sgi = [0]
        r_pens = RL(2)

        def nsa_fin(hn, branch):
            def fin(A, qt, bank, rb):
                den, rd = rden_of(A, bank, rb, 64)
                k = sgi[0] % 4
                sgi[0] += 1
                C.dve(lambda: nc.vector.tensor_tensor(out=sg[k][:, 0:1], in0=den[:, 0:1],
                                                      in1=gates[:, qt, 3 * hn + branch:3 * hn + branch + 1],
                                                      op=ALU.mult), [rd, r_gates], [r_sg[k]])
                oslice = oacc[:, qt, hn * 64:(hn + 1) * 64]
                imp, r_imp = imps[hn // 4], r_imps[hn // 4]
                if branch == 0:
                    C.dve(lambda: nc.vector.tensor_scalar(out=oslice, in0=bank[:, 0:64], scalar1=sg[k][:, 0:1],
                                                          scalar2=None, op0=ALU.mult), [rb, r_sg[k]], [r_oacc])
                    if hn % 4 == 0:
                        C.dve(lambda: nc.vector.tensor_scalar(out=imp[:, qt, :], in0=bank[:, 65:129],
                                                              scalar1=den[:, 0:1], scalar2=None, op0=ALU.mult),
                              [rb, rd], [r_imp])
                    else:
                        C.dve(lambda: nc.vector.scalar_tensor_tensor(out=imp[:, qt, :], in0=bank[:, 65:129],
                                                                     scalar=den[:, 0:1], in1=imp[:, qt, :],
                                                                     op0=ALU.mult, op1=ALU.add),
                              [rb, rd, r_imp], [r_imp])
                else:
                    C.dve(lambda: nc.vector.scalar_tensor_tensor(out=oslice, in0=bank[:, 0:64], scalar=sg[k][:, 0:1],
                                                                 in1=oslice, op0=ALU.mult, op1=ALU.add),
                          [rb, r_sg[k], r_oacc], [r_oacc])
            return fin

        def cmp_tiles(qc):
            return [(0, 0, 3)] + ([(1, 0, 3)] if qc >= 4 else [])

        def cmp_wmul(aset, kt, qc, jlo, jhi):
            return [(0, 512, aset["W"][:, kt * 4096 + qc * 512:kt * 4096 + (qc + 1) * 512], aset["r"])]

        i2 = ph.sb("i2", [128, 4, 64], F32)
        i3 = ph.sb("i3", [128, 4, 64], F32)
        m8a = ph.sb("m8a", [128, 4, 8], F32)
        m8b = ph.sb("m8b", [128, 4, 8], F32)
        sel2 = ph.sb("sel2", [128, 4, 64], F32)
        pnf = [ph.sb(f"pnf{i}", [128, 4, 64], F32) for i in range(2)]
        r_i2, r_i3, r_m8a, r_m8b, r_s2 = Res(), Res(), Res(), Res(), Res()
        r_pnf = RL(2)
        penTs = ph.sb("penTs", [64, S], BF16)
        r_penTs = Res()

        def sel_piece(g, k):
            pb = k % 2
            C.dve(lambda: nc.vector.tensor_tensor(out=i2[:], in0=imps[g][:, 4 * k:4 * k + 4, :],
                                                  in1=force[:, 4 * k:4 * k + 4, :], op=ALU.add),
                  [r_imps[g], r_force], [r_i2])
            for u in range(4):
                C.dve(lambda u=u: nc.vector.max(out=m8a[:, u, :], in_=i2[:, u, :]), [r_i2], [r_m8a])
            for u in range(4):
                C.dve(lambda u=u: nc.vector.match_replace(out=i3[:, u, :], in_to_replace=m8a[:, u, :],
                                                          in_values=i2[:, u, :], imm_value=-3.0e38),
                      [r_i2, r_m8a], [r_i3])
            for u in range(4):
                C.dve(lambda u=u: nc.vector.max(out=m8b[:, u, :], in_=i3[:, u, :]), [r_i3], [r_m8b])
            for u in range(4):
                C.dve(lambda u=u: nc.vector.tensor_scalar(out=sel2[:, u, :], in0=i2[:, u, :], scalar1=m8b[:, u, 7:8],
                                                          scalar2=None, op0=ALU.is_ge), [r_i2, r_m8b], [r_s2])
            C.dve(lambda: nc.vector.tensor_scalar(out=pnf[pb][:], in0=sel2[:], scalar1=PEN, scalar2=-PEN, op0=ALU.mult,
                                                  op1=ALU.add), [r_s2], [r_pnf[pb]])
            for u in range(4):
                C.tr(banks[7][0:64, u * 128:(u + 1) * 128], pnf[pb][:, u, :], identf[:], [r_pnf[pb], r_const],
                     [bres[7]])
            C.evac(penTs[:, k * 512:(k + 1) * 512], banks[7][0:64, :], [bres[7]], [r_penTs])
            if k == 7:
                C.dma("pool", PENS.ap()[g * 64:(g + 1) * 64, :], penTs[:], reads=[r_penTs], writes=[r_pens[g]])

        def win_hook(g, j):
            def hook(qc):
                if qc == 3:
                    sel_piece(g, 2 * j)
                elif qc == 7:
                    sel_piece(g, 2 * j + 1)
            return hook

        jobs_cmp, jobs_win, jobs_slc = [], [], []
        for g in range(2):
            for j in range(4):
                hn = 4 * g + j
                qrows = ft_rows(FT0, (8 + hn // 2) * 128 + (hn % 2) * 64)
                jobs_cmp.append(dict(
                    qa=[(0, 64, qrows)], ka=[(0, 64, 256, KCMP.ap()[g * 64:(g + 1) * 64, :])],
                    va=(VCMP.ap()[g * 256:(g + 1) * 256, :], 2, 64),
                    w=[(nt * 4096, 4096, STRK, hn * 128 * LC + 4081 - 2048 * nt, LC - 16) for nt in range(2)],
                    KQ=64, VW=129, tiles=cmp_tiles, wmul=cmp_wmul, fin=nsa_fin(hn, 0)))
                jobs_win.append(dict(
                    qa=[(0, 64, qrows)], ka=[(0, 64, S, ft_rows(FT0, 15 * 128 + g * 64))],
                    va=(TM0.ap()[:, 640 + g * 64:640 + (g + 1) * 64], NT, 64),
                    w=[(0, 4480, STRW, hn * 128 * LS + 128, LS - 1)],
                    KQ=64, VW=65, tiles=window_tiles, wmul=strip_wmul, fin=nsa_fin(hn, 2), hook=win_hook(g, j),
                    pool_share=True))
                jobs_slc.append(dict(
                    qa=[(0, 64, qrows), (64, 64, PENS.ap()[g * 64:(g + 1) * 64, :])],
                    ka=[(0, 64, S, ft_rows(FT0, 14 * 128 + g * 64)), (64, 64, S, B_OHB64.ap())],
                    va=(TM0.ap()[:, 512 + g * 64:512 + (g + 1) * 64], NT, 64),
                    w=[(0, 4480, STRC, (8 + hn) * 128 * LS + 128, LS - 1)],
                    deps=[r_pens[g]], KQ=128, VW=65, tiles=causal_tiles, wmul=strip_wmul, fin=nsa_fin(hn, 1),
                    far=True, pool_share=True))
        run_jobs(A, jobs_cmp + jobs_win + jobs_slc)
        ob = [ph.sb(f"ob{i}", [128, 4, 512], BF16) for i in range(2)]
        r_ob = RL(2)
        for q8 in range(8):
            i = q8 % 2
            C.evac(ob[i][:], oacc[:, q8 * 4:(q8 + 1) * 4, :], [r_oacc], [r_ob[i]])
            C.dma("pool", OATT.ap()[q8 * 512:(q8 + 1) * 512, 512:1024].rearrange("(t p) c -> p t c", p=128), ob[i][:],
                  reads=[r_ob[i]])
    if upto < 5:
        return _finish(C, nc)

    def outproj_phase(wsrc, hsrc, hdst):
        with Phase(C) as ph:
            wout, r_wout = load_w_bf16(ph, "wout", wsrc, 0, 8, D)
            ot = [ph.sb(f"ot{i}", [128, D], BF16) for i in range(2)]
            ht = [ph.sb(f"ht{i}", [128, D], F32) for i in range(3)]
            oT = [ph.sb(f"oT{i}", [128, 8, 128], BF16) for i in range(2)]
            r_ot, r_ht, r_oT = RL(2), RL(3), RL(2)

            def ld(tt):
                C.dma("sp", ot[tt % 2][:], OATT.ap()[tt * 128:(tt + 1) * 128, :], writes=[r_ot[tt % 2]])
                C.dma("sp", ht[tt % 3][:], hsrc.ap()[tt * 128:(tt + 1) * 128, :], writes=[r_ht[tt % 3]])

            ld(0)
            for tt in range(NT):
                if tt + 1 < NT:
                    ld(tt + 1)
                i = tt % 2
                bb = bankbf(6 + i)
                for k in range(8):
                    C.tr(bb[:, k * 128:(k + 1) * 128], ot[i][:, k * 128:(k + 1) * 128], identb[:], [r_ot[i], r_const],
                         [bres[6 + i]])
                C.evac(oT[i][:], bb.rearrange("p (k t) -> p k t", k=8), [bres[6 + i]], [r_oT[i]])
                for half in range(2):
                    b = 2 * i + half
                    for k in range(8):
                        C.mm(banks[b][:, :], oT[i][:, k, :], wout[:, k, half * 512:(half + 1) * 512], k == 0, k == 7,
                             [r_oT[i], r_wout], [bres[b]])
                    hs = ht[tt % 3][:, half * 512:(half + 1) * 512]
                    C.dve(lambda hs=hs, b=b: nc.vector.tensor_tensor(out=hs, in0=banks[b][:, :], in1=hs, op=ALU.add),
                          [bres[b], r_ht[tt % 3]], [r_ht[tt % 3]])
                C.dma("pool", hdst.ap()[tt * 128:(tt + 1) * 128, :], ht[tt % 3][:], reads=[r_ht[tt % 3]])

    def mlp_phase(layer, hsrc, hdst, final):
        with Phase(C) as ph:
            w1, r_w1 = load_w_bf16(ph, "w1", mlp_w1, layer * D, 8, DFF)
            w2, r_w2 = load_w_bf16(ph, "w2", mlp_w2, layer * DFF, 32, D)
            g, r_g = bcast_row(ph, "gmlp", mlp_norm, layer)
            if final:
                gf, r_gf = bcast_row(ph, "gfin", final_norm, 0)
            ht = [ph.sb(f"mh{i}", [128, D], F32) for i in range(3)]
            r_ht = RL(3)
            hn = [ph.sb(f"mhn{i}", [128, D], BF16) for i in range(2)]
            r_hn = RL(2)
            junk = ph.sb("mjunk", [128, D], BF16)
            smalls = [(junk, ph.sb(f"mss{i}", [128, 1], F32), ph.sb(f"mrs{i}", [128, 1], F32)) for i in range(2)]
            r_sm = RL(2)
            hnT = [ph.sb(f"mhnT{i}", [128, 8, 128], BF16) for i in range(2)]
            r_hnT = RL(2)
            aT = ph.sb("aT", [128, 32, 128], BF16)
            r_aT = RL(32)
            rl = [ph.sb(f"rl{i}", [128, 128], F32) for i in range(4)]
            r_rl = RL(4)
            if final:
                fo = [ph.sb(f"fo{i}", [128, D], F32) for i in range(2)]
                r_fo = RL(2)
                fss = [ph.sb(f"fss{i}", [128, 2], F32) for i in range(2)]
                r_fss = RL(2)

            def ld(tt):
                C.dma("sp", ht[tt % 3][:], hsrc.ap()[tt * 128:(tt + 1) * 128, :], writes=[r_ht[tt % 3]])

            ld(0)
            ld(1)
            for tt in range(NT):
                if tt + 2 < NT:
                    ld(tt + 2)
                i = tt % 2
                h3 = tt % 3
                rmsnorm_T(ph, "m", ht[h3][:], r_ht[h3], g, r_g, hn[i], r_hn[i], smalls[i], r_sm[i], hnT[i][:],
                          r_hnT[i], 6 + i)
                for f in range(32):
                    b = f % 4
                    for k in range(8):
                        C.mm(banks[b][:, 0:128], w1[:, k, f * 128:(f + 1) * 128], hnT[i][:, k, :], k == 0, k == 7,
                             [r_w1, r_hnT[i]], [bres[b]])
                    C.act(rl[b][:], banks[b][:, 0:128], AF.Relu, [bres[b]], [r_rl[b]])
                    C.dve(lambda f=f, b=b: nc.vector.tensor_tensor(out=aT[:, f, :], in0=rl[b][:], in1=rl[b][:],
                                                                   op=ALU.mult), [r_rl[b]], [r_aT[f]])
                for half in range(2):
                    b = 4 + half
                    for f in range(32):
                        C.mm(banks[b][:, :], aT[:, f, :], w2[:, f, half * 512:(half + 1) * 512], f == 0, f == 31,
                             [r_aT[f], r_w2], [bres[b]])
                    hs = ht[h3][:, half * 512:(half + 1) * 512]
                    C.dve(lambda hs=hs, b=b: nc.vector.tensor_tensor(out=hs, in0=banks[b][:, :], in1=hs, op=ALU.add),
                          [bres[b], r_ht[h3]], [r_ht[h3]])
                if not final:
                    C.dma("pool", hdst.ap()[tt * 128:(tt + 1) * 128, :], ht[h3][:], reads=[r_ht[h3]])
                else:
                    ss, rstd = fss[i][:, 0:1], fss[i][:, 1:2]
                    C.dve(lambda: nc.vector.scalar_tensor_tensor(out=junk[:], in0=ht[h3][:], scalar=1.0, in1=ht[h3][:],
                                                                 op0=ALU.mult, op1=ALU.mult, accum_out=ss),
                          [r_ht[h3]], [r_fss[i]])
                    C.dve(lambda: nc.vector.tensor_scalar(out=rstd, in0=ss, scalar1=1.0 / D, scalar2=EPS, op0=ALU.mult,
                                                          op1=ALU.add), [r_fss[i]], [r_fss[i]])
                    C.act(rstd, rstd, AF.Sqrt, [r_fss[i]], [r_fss[i]])
                    C.dve(lambda: nc.vector.reciprocal(out=rstd, in_=rstd), [r_fss[i]], [r_fss[i]])
                    C.dve(lambda: nc.vector.scalar_tensor_tensor(out=fo[i][:], in0=ht[h3][:], scalar=rstd, in1=gf[:],
                                                                 op0=ALU.mult, op1=ALU.mult),
                          [r_ht[h3], r_fss[i], r_gf], [r_fo[i]])
                    C.dma("pool", hdst.ap()[tt * 128:(tt + 1) * 128, :], fo[i][:], reads=[r_fo[i]])

    outproj_phase(even_w_out, x, H1)
    if upto < 6:
        return _finish(C, nc)
    mlp_phase(0, H1, H2, False)
    if upto < 7:
        return _finish(C, nc)

    SPD = nc.dram_tensor("SPD", [16, S], F32, kind="ExternalOutput" if (dbgset is None and dbg) or (dbgset and "SPD" in dbgset) else "Internal")
    fm1 = []
    for m in range(8):
        fm1.append((m * 128, 128, 0.125, FT1.ap()[m * 128:(m + 1) * 128, :]))
    for m in range(8):
        fm1.append((1024 + m * 128, 128, None, FT1.ap()[1024 + m * 128:1024 + (m + 1) * 128, :]))
    fm1.append((3072, 16, None, None))
    tm1_groups = [[(2048, 512)], [(2560, 512)]]

    def pre1(env):
        ph = env["ph"]
        env["nb"] = ph.sb("nb", [16, 1], F32)
        env["e16"] = ph.sb("e16", [16, 512], F32)
        env["sp16"] = [ph.sb(f"sp16_{i}", [16, 512], F32) for i in range(2)]
        env["r_nb"], env["r_e"], env["r_sp"] = Res(), Res(), RL(2)
        C.dma("sp", env["nb"][:], odd_b_forget.ap(), writes=[env["r_nb"]])
        C.dve(lambda: nc.vector.tensor_scalar(out=env["nb"][:], in0=env["nb"][:], scalar1=-1.0, scalar2=None,
                                              op0=ALU.mult), [env["r_nb"]], [env["r_nb"]])

    def fm1_special(env, c, bank, rb):
        i = c % 2
        C.act(env["e16"][:], bank[0:16, :], AF.Exp, [rb, env["r_nb"]], [env["r_e"]], scale=-1.0, bias=env["nb"][:, 0:1])
        C.act(env["sp16"][i][:], env["e16"][:], AF.Ln, [env["r_e"]], [env["r_sp"][i]], bias=1.0)
        C.dma("pool", SPD.ap()[:, c * 512:(c + 1) * 512], env["sp16"][i][:], reads=[env["r_sp"][i]])

    def tm1_handler(env, tt, gi, bank, rb, tsb, r_tsb):
        C.evac(tsb[:, gi * 512:(gi + 1) * 512], bank[:, 0:512], [rb], [r_tsb])
        if gi == 1:
            C.dma("pool", TM1.ap()[tt * 128:(tt + 1) * 128, :], tsb[:, 0:1024], reads=[r_tsb])

    proj_phase(1, H2, odd_w_in, ODD_IN, mix_norm, fm1, tm1_groups, fm1_special, tm1_handler, pre=pre1)
    with Phase(C) as ph:
        Cm = ph.sb("Cm", [16, S], F32)
        spb = ph.sb("spb", [16, S], F32)
        ones16 = ph.sb("ones16", [16, S], F32)
        parts = ph.sb("parts", [16, 6, S], BF16)
        r_c = Res()
        C.dma("sp", spb[:], SPD.ap(), writes=[r_c])
        C.dve(lambda: nc.vector.memset(ones16[:], 1.0), [], [r_c])
        C.dve(lambda: nc.vector.tensor_tensor_scan(out=Cm[:], data0=ones16[:], data1=spb[:], initial=0.0, op0=ALU.mult,
                                                   op1=ALU.add), [r_c], [r_c])
        C.dve(lambda: nc.vector.tensor_copy(out=parts[:, 0, :], in_=Cm[:]), [r_c], [r_c])
        C.dve(lambda: nc.vector.tensor_tensor(out=spb[:], in0=Cm[:], in1=parts[:, 0, :], op=ALU.subtract), [r_c], [r_c])
        C.dve(lambda: nc.vector.tensor_copy(out=parts[:, 1, :], in_=spb[:]), [r_c], [r_c])
        C.dve(lambda: nc.vector.tensor_tensor(out=spb[:], in0=spb[:], in1=parts[:, 1, :], op=ALU.subtract), [r_c], [r_c])
        C.dve(lambda: nc.vector.tensor_copy(out=parts[:, 2, :], in_=spb[:]), [r_c], [r_c])
        for u in range(3):
            C.dve(lambda u=u: nc.vector.tensor_scalar(out=parts[:, 3 + u, :], in0=parts[:, u, :], scalar1=-1.0,
                                                      scalar2=None, op0=ALU.mult), [r_c], [r_c])
        for u in range(6):
            C.dma("pool", CUMP.ap()[u * 16:(u + 1) * 16, :], parts[:, u, :], reads=[r_c])
    if upto < 8:
        return _finish(C, nc)

    with Phase(C) as ph:
        A = alloc_attn(ph, S, 65, 128)
        tri = ph.sb("tri", [128, 128], BF16)
        r_tri = Res()
        C.dma("pool", tri[:], c_tri.ap(), writes=[r_tri])
        for aset in A["sets"]:
            C.dve(lambda aset=aset: nc.vector.memset(aset["VA"][:, :, 64:65], 1.0), [], [aset["r"]])
        osb = [ph.sb(f"fosb{i}", [128, NT, 64], BF16) for i in range(2)]
        r_osb = RL(2)

        def fox_fin(oi):
            def fin(A, qt, bank, rb):
                den, rd = rden_of(A, bank, rb, 64)
                C.act(osb[oi][:, qt, :], bank[:, 0:64], AF.Copy, [rb, rd], [r_osb[oi]], scale=den[:, 0:1])
            return fin

        def fox_after(oi, col0):
            def after():
                C.dma("pool", OATT.ap()[:, col0:col0 + 64].rearrange("(t p) c -> p t c", p=128), osb[oi][:],
                      reads=[r_osb[oi]])
            return after

        def fox_wmul(aset, kt, qc, jlo, jhi):
            if kt >= 4 * qc:
                return [(jlo * 128, jlo * 128 + 128, tri[:], r_tri)]
            return []

        jobs = []
        for h in range(16):
            jobs.append(dict(
                qa=[(0, 64, ft_rows(FT1, h * 64)), (64, 3, bass.AP(CUMP, (48 + h) * S, [[16 * S, 3], [1, S]])),
                    (67, 3, B_ONES.ap()[0:3, :])],
                ka=[(0, 64, S, ft_rows(FT1, 1024 + h * 64)), (64, 3, S, B_ONES.ap()[0:3, :]),
                    (67, 3, S, bass.AP(CUMP, h * S, [[16 * S, 3], [1, S]]))],
                va=(TM1.ap()[:, h * 64:(h + 1) * 64], NT, 64), w=[],
                KQ=70, VW=65, tiles=causal_tiles, wmul=fox_wmul, fin=fox_fin(h % 2), after=fox_after(h % 2, h * 64)))
        run_jobs(A, jobs)
    if upto < 9:
        return _finish(C, nc)
    outproj_phase(odd_w_out, H2, H3)
    if upto < 10:
        return _finish(C, nc)
    mlp_phase(1, H3, out, True)
    _finish(C, nc)


def _finish(C, nc):
    C.barrier()


def rel_bucket_np(n):
    n = np.maximum(n, 0)
    nf = np.maximum(n, 1).astype(np.float32)
    large = 16 + (np.log(nf / np.float32(16)) / np.float32(np.log(1024 / 16)) * np.float32(16)).astype(np.int32)
    large = np.minimum(large, 31)
    return np.where(n < 16, n, large)


def make_consts():
    c = {}
    c["c_ident"] = np.eye(128, dtype=np.float32)
    r = np.arange(LS) - 512
    oh = np.zeros((33, LS), np.float32)
    b = rel_bucket_np(r)
    valid = r >= 0
    oh[b[valid], np.nonzero(valid)[0]] = 1.0
    oh[32, ~valid] = 1.0
    c["c_ohc"] = oh
    ohw = np.zeros((33, LS), np.float32)
    validw = (r >= 0) & (r < 512)
    ohw[b[validw], np.nonzero(validw)[0]] = 1.0
    ohw[32, ~validw] = 1.0
    c["c_ohw"] = ohw
    r2 = np.arange(LC) - 4112
    b2 = rel_bucket_np(r2)
    ohk = np.zeros((33, LC), np.float32)
    v2 = r2 >= 0
    ohk[b2[v2], np.nonzero(v2)[0]] = 1.0
    ohk[32, ~v2] = 1.0
    c["c_ohcmp"] = ohk
    n = np.arange(16)[None, :]
    blk = (np.arange(32) // 2)[:, None]
    cm = np.where(n < blk, 0.0, -1e30).astype(np.float32)
    own = (n >= blk).astype(np.float32)
    c["c_cm16"] = np.tile(cm.reshape(1, 512), (128, 1))
    c["c_own16"] = np.tile(own.reshape(1, 512), (128, 1))
    t = (np.arange(32)[None, :, None] * 128 + np.arange(128)[:, None, None])
    sblk = t // 64
    m = np.arange(64)[None, None, :]
    forced = (m == 0) | (m == sblk) | (m == sblk - 1)
    f = np.where(forced, 1e6, np.where(m <= sblk, 0.0, -1e30)).astype(np.float32)
    c["c_force"] = f.reshape(128, 2048)
    ci = np.arange(256)[:, None] * 16
    sj = np.arange(64)[None, :] * 64
    ov = ((ci < sj + 64) & (ci + 32 > sj)).astype(np.float32)
    ov[255, :] = 0.0
    c["c_overlap"] = ov
    tok = np.arange(S)[None, :]
    c["c_ohb64"] = (tok // 64 == np.arange(64)[:, None]).astype(np.float32)
    c["c_ohb16"] = (tok // 256 == np.arange(16)[:, None]).astype(np.float32)
    c["c_tri"] = (np.arange(128)[None, :] >= np.arange(128)[:, None]).astype(np.float32)
    c["c_ones"] = np.ones((128, S), np.float32)
    return c


_CACHE = {}


def make_in_maps(inputs, cores):
    consts = make_consts()
    f = lambda a: np.ascontiguousarray(np.asarray(a, dtype=np.float32))
    shared = {
        "rel_bias": f(inputs["rel_bias"]),
        "mix_norm": f(inputs["mix_norm"]),
        "mlp_norm": f(inputs["mlp_norm"]),
        "even_w_in": f(inputs["even_w_in"][0]),
        "even_w_out": f(inputs["even_w_out"][0]),
        "cmp_pos_k": f(inputs["cmp_pos_k"][0]),
        "cmp_pos_v": f(inputs["cmp_pos_v"][0]),
        "cmp_k_w1": f(inputs["cmp_k_w1"][0]),
        "cmp_k_w2": f(inputs["cmp_k_w2"][0]),
        "cmp_v_w1": f(inputs["cmp_v_w1"][0]),
        "cmp_v_w2": f(inputs["cmp_v_w2"][0]),
        "odd_w_in": f(inputs["odd_w_in"][0]),
        "odd_b_forget": f(np.asarray(inputs["odd_b_forget"]).reshape(16, 1)),
        "odd_w_out": f(inputs["odd_w_out"][0]),
        "mlp_w1": f(np.asarray(inputs["mlp_w1"]).reshape(2 * D, DFF)),
        "mlp_w2": f(np.asarray(inputs["mlp_w2"]).reshape(2 * DFF, D)),
        "final_norm": f(np.asarray(inputs["final_norm"]).reshape(1, D)),
    }
    shared.update(consts)
    xs = np.asarray(inputs["x"], dtype=np.float32)
    maps = []
    for b in cores:
        m = dict(shared)
        m["x"] = np.ascontiguousarray(xs[b])
        maps.append(m)
    return maps


def kernel(**inputs):
    if "nc" not in _CACHE:
        nc = bass.Bass("TRN2", target_bir_lowering=False)
        build(nc)
        _CACHE["nc"] = nc
    nc = _CACHE["nc"]
    maps = make_in_maps(inputs, list(range(8)))
    res = run_bass_kernel_spmd(nc, maps, core_ids=list(range(8)))
    return np.stack([np.asarray(r["out"], dtype=np.float32) for r in res.results], axis=0)
```

```python
import contextlib
import numpy as np
import concourse.bass as bass
import concourse.mybir as mybir
from concourse.bass_utils import run_bass_kernel_spmd

F32 = mybir.dt.float32
BF16 = mybir.dt.bfloat16
AF = mybir.ActivationFunctionType
ALU = mybir.AluOpType
AX = mybir.AxisListType

D = 1024
S = 4096
NT = 32
DFF = 4096
EVEN_IN = 2840
ODD_IN = 3088
LS = 4608
LC = 8320
PEN = 30000.0
EPS = 1e-5


class Res:
    __slots__ = ("w", "r")

    def __init__(self):
        self.w = {}
        self.r = {}


def RL(n):
    return [Res() for _ in range(n)]


class Ctx:
    def __init__(self, nc, st):
        self.nc = nc
        self.engs = {"pe": nc.tensor, "act": nc.scalar, "dve": nc.vector, "pool": nc.gpsimd, "sp": nc.sync}
        self.sem = {k: st.enter_context(nc.semaphore("sem_" + k)) for k in ["pe", "act", "dve", "pool"]}
        self.cnt = {k: 0 for k in self.sem}
        self.seen = {k: {} for k in self.engs}
        self.hist = {}
        self.rings = {}
        for q, n in [("sp", 12), ("pool", 8), ("act", 4)]:
            self.rings[q] = dict(sems=[st.enter_context(nc.semaphore(f"dq_{q}_{i}")) for i in range(n)], i=0)
        self.last = {}
        self.flip = 0
        self.ninst = 0

    def _semfor(self, key):
        if isinstance(key, tuple):
            return self.rings[key[1]]["sems"][key[2]]
        return self.sem[key]

    def _wait(self, e, key, val):
        s = self.seen[e]
        if s.get(key, 0) >= val:
            return
        self.engs[e].wait_ge(self._semfor(key), val)
        self.ninst += 1
        s[key] = val
        snap = self.hist.get((key, val))
        if snap:
            for k2, v2 in snap.items():
                if s.get(k2, 0) < v2:
                    s[k2] = v2

    def _deps(self, e, reads, writes):
        for r in reads:
            for key, val in list(r.w.items()):
                self._wait(e, key, val)
        for w in writes:
            for key, val in list(w.w.items()):
                if key != e:
                    self._wait(e, key, val)
            for key, val in list(w.r.items()):
                if key != e:
                    self._wait(e, key, val)

    def op(self, e, fn, reads=(), writes=()):
        self._deps(e, reads, writes)
        inst = fn()
        self.cnt[e] += 1
        c = self.cnt[e]
        inst.then_inc(self.sem[e], 1)
        self.ninst += 1
        self.hist[(e, c)] = dict(self.seen[e])
        for r in reads:
            r.r[e] = c
        for w in writes:
            w.w = {e: c}
            w.r = {}
        return inst

    def dma(self, q, out, in_, reads=(), writes=(), **kw):
        ring = self.rings[q]
        i = ring["i"]
        n = len(ring["sems"])
        slot = i % n
        val = 16 * (i // n + 1)
        key = ("d", q, slot)
        if val > 16:
            self._wait(q, key, val - 16)
        for r in reads:
            for k2, v2 in list(r.w.items()):
                self._wait(q, k2, v2)
        for w in writes:
            for k2, v2 in list(w.w.items()):
                if k2 != q and not (isinstance(k2, tuple) and k2[1] == q):
                    self._wait(q, k2, v2)
            for k2, v2 in list(w.r.items()):
                if k2 != q:
                    self._wait(q, k2, v2)
        inst = self.engs[q].dma_start(out=out, in_=in_, **kw)
        inst.then_inc(ring["sems"][slot], 16)
        self.ninst += 1
        ring["i"] += 1
        self.hist[(key, val)] = dict(self.seen[q])
        self.last[key] = val
        for r in reads:
            r.r[key] = val
        for w in writes:
            w.w[key] = val
            w.r = {}

    def barrier(self):
        targets = [(k, self.cnt[k]) for k in self.sem if self.cnt[k] > 0] + list(self.last.items())
        for e in self.engs:
            for key, val in targets:
                if key == e:
                    continue
                self._wait(e, key, val)

    def mm(self, out, lhsT, rhs, start, stop, reads, writes):
        return self.op("pe", lambda: self.nc.tensor.matmul(out, lhsT, rhs, start=start, stop=stop), reads, writes)

    def tr(self, out, in_, ident, reads, writes):
        return self.op("pe", lambda: self.nc.tensor.transpose(out, in_, ident), reads, writes)

    def act(self, out, in_, func, reads, writes, **kw):
        return self.op("act", lambda: self.nc.scalar.activation(out=out, in_=in_, func=func, **kw), reads, writes)

    def evac(self, out, in_, reads, writes, scale=None, eng=None):
        if eng is None:
            self.flip ^= 1
            eng = "act" if self.flip else "dve"
        if eng == "act":
            if scale is None:
                return self.op("act", lambda: self.nc.scalar.copy(out=out, in_=in_), reads, writes)
            return self.op("act", lambda: self.nc.scalar.mul(out=out, in_=in_, mul=scale), reads, writes)
        if scale is None:
            return self.op("dve", lambda: self.nc.vector.tensor_copy(out=out, in_=in_), reads, writes)
        return self.op("dve", lambda: self.nc.vector.tensor_scalar(out=out, in0=in_, scalar1=scale, scalar2=None,
                                                                   op0=ALU.mult), reads, writes)

    def dve(self, fn, reads, writes):
        return self.op("dve", fn, reads, writes)


class Phase:
    uid = 0

    def __init__(self, ctx):
        self.ctx = ctx
        self.st = contextlib.ExitStack()

    def __enter__(self):
        self.st.__enter__()
        return self

    def __exit__(self, *a):
        self.ctx.barrier()
        return self.st.__exit__(*a)

    def sb(self, name, shape, dt):
        Phase.uid += 1
        return self.st.enter_context(self.ctx.nc.sbuf_tensor(f"{name}_u{Phase.uid}", list(shape), dt))


def dram_rows(t, r0, nr, ncols, c0=0, rowlen=None):
    return t.ap()[r0:r0 + nr, c0:c0 + ncols]


def build(nc, upto=99, dbg=False):
    st = contextlib.ExitStack()
    with st:
        _build(nc, st, upto, dbg)
    return nc


def _build(nc, st, upto, dbg):
    C = Ctx(nc, st)
    ein = lambda name, shape: nc.dram_tensor(name, list(shape), F32, kind="ExternalInput")
    x = ein("x", [S, D])
    rel_bias = ein("rel_bias", [32, 16])
    mix_norm = ein("mix_norm", [2, D])
    mlp_norm = ein("mlp_norm", [2, D])
    even_w_in = ein("even_w_in", [D, EVEN_IN])
    even_w_out = ein("even_w_out", [D, D])
    cmp_pos_k = ein("cmp_pos_k", [32, 64])
    cmp_pos_v = ein("cmp_pos_v", [32, 64])
    cmp_k_w1 = ein("cmp_k_w1", [2048, 256])
    cmp_k_w2 = ein("cmp_k_w2", [256, 64])
    cmp_v_w1 = ein("cmp_v_w1", [2048, 256])
    cmp_v_w2 = ein("cmp_v_w2", [256, 64])
    odd_w_in = ein("odd_w_in", [D, ODD_IN])
    odd_b_forget = ein("odd_b_forget", [16, 1])
    odd_w_out = ein("odd_w_out", [D, D])
    mlp_w1 = ein("mlp_w1", [2 * D, DFF])
    mlp_w2 = ein("mlp_w2", [2 * DFF, D])
    final_norm = ein("final_norm", [1, D])
    c_ident = ein("c_ident", [128, 128])
    c_ohc = ein("c_ohc", [33, LS])
    c_ohw = ein("c_ohw", [33, LS])
    c_ohcmp = ein("c_ohcmp", [33, LC])
    c_cm16 = ein("c_cm16", [128, 512])
    c_own16 = ein("c_own16", [128, 512])
    c_force = ein("c_force", [128, 2048])
    c_overlap = ein("c_overlap", [256, 64])
    c_ohb64 = ein("c_ohb64", [64, S])
    c_ohb16 = ein("c_ohb16", [16, S])
    c_tri = ein("c_tri", [128, 128])
    c_ones = ein("c_ones", [128, S])

    out = nc.dram_tensor("out", [S, D], F32, kind="ExternalOutput")
    dbgset = dbg if isinstance(dbg, (set, list, tuple)) else None
    def scr(name, shape, dt):
        ext = (name in dbgset) if dbgset is not None else bool(dbg)
        return nc.dram_tensor(name, list(shape), dt, kind="ExternalOutput" if ext else "Internal")
    FT0 = scr("FT0", [16 * 128, S], BF16)
    TM0 = scr("TM0", [S, 768], BF16)
    STRC = scr("STRC", [16 * 128, LS], BF16)
    STRW = scr("STRW", [8 * 128, LS], BF16)
    STRK = scr("STRK", [8 * 128, LC], BF16)
    PENM = scr("PENM", [8 * 16, S], BF16)
    PENS = scr("PENS", [2 * 64, S], BF16)
    KCMP = scr("KCMP", [2 * 64, 256], BF16)
    VCMP = scr("VCMP", [2 * 256, 64], BF16)
    OATT = scr("OATT", [S, D], BF16)
    H1 = scr("H1", [S, D], F32)
    H2 = scr("H2", [S, D], F32)
    FT1 = scr("FT1", [16 * 128, S], BF16)
    TM1 = scr("TM1", [S, D], BF16)
    CUMP = scr("CUMP", [6 * 16, S], BF16)
    H3 = scr("H3", [S, D], F32)
    B_OHB64 = scr("B_OHB64", [64, S], BF16)
    B_OHB16 = scr("B_OHB16", [16, S], BF16)
    B_ONES = scr("B_ONES", [8, S], BF16)
    B_OVL = scr("B_OVL", [256, 64], BF16)

    gsb = lambda name, shape, dt: st.enter_context(nc.sbuf_tensor(name, list(shape), dt))
    banks = [st.enter_context(nc.psum_tensor(f"bank{i}", [128, 512], F32)) for i in range(8)]
    bres = RL(8)
    identf = gsb("identf", [128, 128], F32)
    identb = gsb("identb", [128, 128], BF16)
    gates = gsb("gates", [128, NT, 24], F32)
    r_const = Res()
    r_gates = Res()

    C.dma("sp", identf[:], c_ident.ap(), writes=[r_const])
    C.dve(lambda: nc.vector.tensor_copy(out=identb[:], in_=identf[:]), [r_const], [r_const])
    C.dma("pool", B_OHB64.ap(), c_ohb64.ap())
    C.dma("pool", B_OHB16.ap(), c_ohb16.ap())
    C.dma("pool", B_ONES.ap(), c_ones.ap()[0:8, :])
    C.dma("pool", B_OVL.ap(), c_overlap.ap())

    def bankbf(i):
        return banks[i][:, :].bitcast(BF16)

    def load_w_bf16(ph, name, src, r0, nk, ncols):
        w = ph.sb(name, [128, nk, ncols], BF16)
        res = Res()
        issue_w(w, src, r0, nk, ncols, res)
        return w, res

    def issue_w(w, src, r0, nk, ncols, res):
        for k in range(nk):
            c0 = 0
            while c0 < ncols:
                c1 = min(ncols, c0 + 2048)
                C.dma("pool", w[:, k, c0:c1], src.ap()[r0 + k * 128:r0 + (k + 1) * 128, c0:c1], writes=[res])
                c0 = c1
        return w, res

    def bcast_row(ph, name, src, row):
        g = ph.sb(name, [128, D], F32)
        res = Res()
        C.dma("sp", g[:], bass.AP(src, row * D, [[0, 128], [1, D]]), writes=[res])
        return g, res

    def rmsnorm_T(ph, tag, xt, r_x, g, r_g, hn, r_hn, small, r_small, hnT_dst, r_dst, bank_i):
        junk, ss, rstd = small
        C.dve(lambda: nc.vector.scalar_tensor_tensor(out=junk[:], in0=xt, scalar=1.0, in1=xt, op0=ALU.mult,
                                                     op1=ALU.mult, accum_out=ss[:]), [r_x], [r_small])
        C.dve(lambda: nc.vector.tensor_scalar(out=rstd[:], in0=ss[:], scalar1=1.0 / D, scalar2=EPS, op0=ALU.mult,
                                              op1=ALU.add), [r_small], [r_small])
        C.act(rstd[:], rstd[:], AF.Sqrt, [r_small], [r_small])
        C.dve(lambda: nc.vector.reciprocal(out=rstd[:], in_=rstd[:]), [r_small], [r_small])
        C.dve(lambda: nc.vector.scalar_tensor_tensor(out=hn[:], in0=xt, scalar=rstd[:, 0:1], in1=g[:], op0=ALU.mult,
                                                     op1=ALU.mult), [r_x, r_small, r_g], [r_hn])
        bb = bankbf(bank_i)
        for k in range(8):
            C.tr(bb[:, k * 128:(k + 1) * 128], hn[:, k * 128:(k + 1) * 128], identb[:], [r_hn, r_const],
                 [bres[bank_i]])
        C.evac(hnT_dst, bb.rearrange("p (k t) -> p k t", k=8), [bres[bank_i]], [r_dst])

    if upto >= 0:
        with Phase(C) as ph:
            tba = ph.sb("tba", [65, 16], F32)
            tbh = ph.sb("tbh", [65, 16], BF16)
            tbrep = ph.sb("tbrep", [65, 16, 128], BF16)
            ones65 = ph.sb("ones65", [65, 128], BF16)
            ohc = ph.sb("ohc", [65, LS], BF16)
            ohw = ph.sb("ohw", [65, LS], BF16)
            ohk = ph.sb("ohk", [65, LC], BF16)
            r0 = Res()
            rt = Res()
            C.dve(lambda: nc.vector.memset(tbh[:], -PEN), [], [rt])
            C.dve(lambda: nc.vector.memset(ones65[:], 1.0), [], [rt])
            C.dma("sp", tba[0:32, :], rel_bias.ap(), writes=[rt])
            C.dma("sp", tba[32:64, :], rel_bias.ap(), writes=[rt])
            for (dst_t, src_t, L) in [(ohc, c_ohc, LS), (ohw, c_ohw, LS), (ohk, c_ohcmp, LC)]:
                c0 = 0
                while c0 < L:
                    c1 = min(L, c0 + 2048)
                    C.dma("pool", dst_t[0:32, c0:c1], src_t.ap()[0:32, c0:c1], writes=[r0])
                    C.dma("pool", dst_t[32:64, c0:c1], src_t.ap()[0:32, c0:c1], writes=[r0])
                    C.dma("pool", dst_t[64:65, c0:c1], src_t.ap()[32:33, c0:c1], writes=[r0])
                    c0 = c1
            C.dve(lambda: nc.vector.tensor_copy(out=tbh[0:64, :], in_=tba[0:64, :]), [rt], [rt])
            C.dve(lambda: nc.vector.tensor_tensor(out=tba[32:64, :], in0=tba[32:64, :], in1=tbh[32:64, :],
                                                  op=ALU.subtract), [rt], [rt])
            C.dve(lambda: nc.vector.tensor_copy(out=tbh[32:64, :], in_=tba[32:64, :]), [rt], [rt])
            r1 = Res()
            for h in range(16):
                C.dve(lambda h=h: nc.vector.tensor_scalar(out=tbrep[:, h, :], in0=ones65[:], scalar1=tbh[:, h:h + 1],
                                                          scalar2=None, op0=ALU.mult), [rt], [r1])
            stg = [ph.sb(f"stg{i}", [128, LC], BF16) for i in range(2)]
            rstg = RL(2)
            jobs = [(ohc, LS, h, STRC, h) for h in range(16)]
            jobs += [(ohw, LS, 8 + hn, STRW, hn) for hn in range(8)]
            jobs += [(ohk, LC, 8 + hn, STRK, hn) for hn in range(8)]
            bi = 0
            for ji, (oh, L, trow, dst, di) in enumerate(jobs):
                sg, rs = stg[ji % 2], rstg[ji % 2]
                c0 = 0
                while c0 < L:
                    c1 = min(L, c0 + 512)
                    b = bi % 4
                    bi += 1
                    C.mm(banks[b][:, 0:c1 - c0], tbrep[:, trow, :], oh[:, c0:c1], True, True, [r0, r1], [bres[b]])
                    C.act(sg[:, c0:c1], banks[b][:, 0:c1 - c0], AF.Exp, [bres[b]], [rs])
                    c0 = c1
                C.dma("pool", dst.ap()[di * 128:(di + 1) * 128, :], sg[:, 0:L], reads=[rs])
    if upto < 1:
        return _finish(C, nc)

    def proj_phase(layer, src_h, w_in_t, ncols, gsrc, fm_list, tm_groups, fm_special, tm_handler, pre=None):
        with Phase(C) as ph:
            win, r_win = load_w_bf16(ph, "win", w_in_t, 0, 8, ncols)
            g, r_g = bcast_row(ph, "gmix", gsrc, layer)
            xts = [ph.sb(f"xt{i}", [128, D], F32) for i in range(3)]
            r_xt = RL(3)
            hns = [ph.sb(f"hn{i}", [128, D], BF16) for i in range(2)]
            r_hn = RL(2)
            junk = ph.sb("junk", [128, D], BF16)
            smalls = [(junk, ph.sb(f"ss{i}", [128, 1], F32), ph.sb(f"rstd{i}", [128, 1], F32)) for i in range(2)]
            r_sm = RL(2)
            hnT = [ph.sb(f"hnT{i}", [128, 8, 512], BF16) for i in range(2)]
            r_hnT = [RL(4) for _ in range(2)]
            fsb = [ph.sb(f"fsb{i}", [128, 512], BF16) for i in range(4)]
            r_fsb = RL(4)
            tsb = [ph.sb(f"tsb{i}", [128, 1024], BF16) for i in range(2)]
            r_tsb = RL(2)
            env = dict(ph=ph)
            if pre is not None:
                pre(env)

            def ld(tt):
                i = tt % 3
                C.dma("sp", xts[i][:], src_h.ap()[tt * 128:(tt + 1) * 128, :], writes=[r_xt[i]])

            ld(0)
            ld(1)
            fi = 0
            for c in range(8):
                hb = c % 2
                for i in range(4):
                    tt = 4 * c + i
                    if tt + 2 < NT:
                        ld(tt + 2)
                    rmsnorm_T(ph, "p", xts[tt % 3][:], r_xt[tt % 3], g, r_g, hns[tt % 2], r_hn[tt % 2],
                              smalls[tt % 2], r_sm[tt % 2], hnT[hb][:, :, i * 128:(i + 1) * 128], r_hnT[hb][i],
                              6 + (tt % 2))
                for m, (col0, width, scale, dst) in enumerate(fm_list):
                    b = m % 3
                    for k in range(8):
                        C.mm(banks[b][0:width, :], win[:, k, col0:col0 + width], hnT[hb][:, k, :], k == 0, k == 7,
                             [r_win] + r_hnT[hb], [bres[b]])
                    if dst is None:
                        fm_special(env, c, banks[b], bres[b])
                        continue
                    f = fi % 4
                    fi += 1
                    C.evac(fsb[f][0:width, :], banks[b][0:width, :], [bres[b]], [r_fsb[f]], scale=scale)
                    C.dma("pool", dst[:, c * 512:(c + 1) * 512], fsb[f][0:width, :], reads=[r_fsb[f]])
                for i in range(4):
                    tt = 4 * c + i
                    for gi, grp in enumerate(tm_groups):
                        b = 3 + (2 * i + gi) % 3
                        o = 0
                        for (col0, width) in grp:
                            for k in range(8):
                                C.mm(banks[b][:, o:o + width], hnT[hb][:, k, i * 128:(i + 1) * 128],
                                     win[:, k, col0:col0 + width], k == 0, k == 7, [r_win, r_hnT[hb][i]], [bres[b]])
                            o += width
                        tm_handler(env, tt, gi, banks[b], bres[b], tsb[tt % 2], r_tsb[tt % 2])

    fm0_cols = [0, 128, 256, 384, 512, 640, 768, 896, 1536, 1664, 1792, 1920, 2048, 2176, 2304, 2560]
    fm0 = []
    for m, col0 in enumerate(fm0_cols):
        isq = m < 4 or 8 <= m < 12
        fm0.append((col0, 128, 0.125 if isq else None, FT0.ap()[m * 128:(m + 1) * 128, :]))
    tm0_groups = [[(1024, 512)], [(2432, 128), (2688, 128), (2816, 24)]]
    gtmp = gsb("gtmp", [128, 24], F32)
    r_gtmp = Res()

    def tm0_handler(env, tt, gi, bank, rb, tsb, r_tsb):
        if gi == 0:
            C.evac(tsb[:, 0:512], bank[:, 0:512], [rb], [r_tsb])
        else:
            C.evac(tsb[:, 512:768], bank[:, 0:256], [rb], [r_tsb])
            C.act(gates[:, tt, :], bank[:, 256:280], AF.Sigmoid, [rb], [r_gates])
            C.dma("pool", TM0.ap()[tt * 128:(tt + 1) * 128, :], tsb[:, 0:768], reads=[r_tsb])

    proj_phase(0, x, even_w_in, EVEN_IN, mix_norm, fm0, tm0_groups, None, tm0_handler)
    if upto < 2:
        return _finish(C, nc)

    def alloc_attn(ph, nkmax, vwmax, wlen):
        sets = []
        for i in range(2):
            sets.append(dict(QA=ph.sb(f"QA{i}", [128, S], BF16), KA=ph.sb(f"KA{i}", [128, nkmax], BF16),
                             VA=ph.sb(f"VA{i}", [128, nkmax // 128, vwmax], BF16),
                             W=ph.sb(f"W{i}", [128, wlen], BF16), r=Res(),
                             fb=ph.sb(f"fb{i}", [128, 2], F32), rfb=Res()))
        A = dict(sets=sets, P=[ph.sb(f"P{i}", [128, 512], BF16) for i in range(6)], rP=RL(6), pi=0,
                 den=[ph.sb(f"den{i}", [128, 2], F32) for i in range(8)], rden=RL(8), di=0, defer=[], mi=0)
        return A

    def load_job(aset, job):
        r = aset["r"]
        deps = job.get("deps", [])
        for (row0, nrows, src) in job["qa"]:
            C.dma("sp", aset["QA"][row0:row0 + nrows, :], src, reads=deps, writes=[r])
        for (row0, nrows, ncols, src) in job["ka"]:
            C.dma("sp", aset["KA"][row0:row0 + nrows, 0:ncols], src, reads=deps, writes=[r])
        vsrc, nkt, vcols = job["va"]
        C.dma("sp", aset["VA"][:, 0:nkt, 0:vcols], vsrc.rearrange("(t p) c -> p t c", p=128), reads=deps, writes=[r])
        for (col0, ncols, tensor, off, pstep) in job["w"]:
            C.dma("sp", aset["W"][:, col0:col0 + ncols], bass.AP(tensor, off, [[pstep, 128], [1, ncols]]), writes=[r])

    def compute_job(A, aset, job):
        KQ, VW = job["KQ"], job["VW"]
        r = aset["r"]
        QA, KA, VA = aset["QA"], aset["KA"], aset["VA"]
        P, rP = A["P"], A["rP"]
        if job.get("far"):
            C.act(aset["fb"][:, 0:1], aset["W"][:, 4479:4480], AF.Ln, [r], [aset["rfb"]])
        for qc in range(8):
            tiles = job["tiles"](qc)
            first, last = {}, {}
            for idx, (kt, jlo, jhi) in enumerate(tiles):
                for j in range(jlo, jhi + 1):
                    first.setdefault(j, idx)
                    last[j] = idx

            def emit_pv(pd):
                idx, kt, jlo, jhi, p = pd
                for j in range(jlo, jhi + 1):
                    C.mm(banks[3 + j][:, 0:VW], P[p][:, j * 128:(j + 1) * 128], VA[:, kt, 0:VW], first[j] == idx,
                         last[j] == idx, [rP[p], r], [bres[3 + j]])

            pend = []
            first_pv = [True]

            def do_pv(pd):
                if first_pv[0]:
                    first_pv[0] = False
                    for f in A["defer"]:
                        f()
                    A["defer"] = []
                emit_pv(pd)

            for idx, (kt, jlo, jhi) in enumerate(tiles):
                sbi = idx % 3
                c0, c1 = jlo * 128, (jhi + 1) * 128
                C.mm(banks[sbi][:, c0:c1], KA[0:KQ, kt * 128:(kt + 1) * 128], QA[0:KQ, qc * 512 + c0:qc * 512 + c1],
                     True, True, [r], [bres[sbi]])
                p = A["pi"] % 6
                A["pi"] += 1
                if job.get("far") and 512 * qc - 128 * kt >= 1152:
                    C.act(P[p][:, c0:c1], banks[sbi][:, c0:c1], AF.Exp, [bres[sbi], aset["rfb"]], [rP[p]],
                          bias=aset["fb"][:, 0:1])
                else:
                    C.act(P[p][:, c0:c1], banks[sbi][:, c0:c1], AF.Exp, [bres[sbi]], [rP[p]])
                    for (a0, a1, wap, rw) in job["wmul"](aset, kt, qc, jlo, jhi):
                        A["mi"] += 1
                        if job.get("pool_share") and A["mi"] % 3 == 0:
                            C.op("pool", lambda a0=a0, a1=a1, wap=wap, p=p: nc.gpsimd.tensor_tensor(
                                out=P[p][:, a0:a1], in0=P[p][:, a0:a1], in1=wap, op=ALU.mult), [rP[p], rw], [rP[p]])
                        else:
                            C.dve(lambda a0=a0, a1=a1, wap=wap, p=p: nc.vector.tensor_tensor(
                                out=P[p][:, a0:a1], in0=P[p][:, a0:a1], in1=wap, op=ALU.mult), [rP[p], rw], [rP[p]])
                if len(pend) >= 2:
                    do_pv(pend.pop(0))
                pend.append((idx, kt, jlo, jhi, p))
            for pd in pend:
                do_pv(pd)
            A["defer"].append(lambda qc=qc: job["fin"](A, [(qc * 4 + j, banks[3 + j], bres[3 + j]) for j in range(4)]))
            if job.get("hook"):
                job["hook"](qc)
        for f in A["defer"]:
            f()
        A["defer"] = []

    def run_jobs(A, jobs):
        for i, job in enumerate(jobs):
            if i == 0 or job.get("late"):
                load_job(A["sets"][i % 2], job)
            if i + 1 < len(jobs) and not jobs[i + 1].get("late"):
                load_job(A["sets"][(i + 1) % 2], jobs[i + 1])
            compute_job(A, A["sets"][i % 2], job)
            if job.get("after"):
                job["after"]()

    def rden_multi(A, items, col):
        outs = []
        for (qt, bank, rb) in items:
            d = A["di"] % 8
            A["di"] += 1
            den, rd = A["den"][d], A["rden"][d]
            C.dve(lambda den=den, bank=bank: nc.vector.tensor_scalar(out=den[:, 0:1], in0=bank[:, col:col + 1],
                                                                     scalar1=1e-30, scalar2=None, op0=ALU.max),
                  [rb], [rd])
            outs.append((den, rd))
        for (den, rd) in outs:
            C.dve(lambda den=den: nc.vector.reciprocal(out=den[:, 0:1], in_=den[:, 0:1]), [rd], [rd])
        return outs

    def causal_tiles(qc):
        return [(kt, max(0, kt - 4 * qc), 3) for kt in range(4 * qc + 4)]

    def window_tiles(qc):
        out_ = []
        for kt in range(max(0, 4 * qc - 4), 4 * qc + 4):
            rel = kt - 4 * qc
            out_.append((kt, max(0, rel), min(3, rel + 4)))
        return out_

    def strip_wmul(aset, kt, qc, jlo, jhi):
        j0 = 512 * qc - 128 * kt + 384
        c0, c1 = jlo * 128, (jhi + 1) * 128
        return [(c0, c1, aset["W"][:, j0 + c0:j0 + c1], aset["r"])]

    def ft_rows(FT, row0, n=64):
        return FT.ap()[row0:row0 + n, :]

    if upto >= 2:
        with Phase(C) as ph:
            A = alloc_attn(ph, S, 65, 4480)
            for aset in A["sets"]:
                C.dve(lambda aset=aset: nc.vector.memset(aset["VA"][:, :, 64:65], 1.0), [], [aset["r"]])
            cm = ph.sb("cm", [128, 512], F32)
            own = ph.sb("own", [128, 512], F32)
            r_c2 = Res()
            C.dma("sp", cm[:], c_cm16.ap(), writes=[r_c2])
            C.dma("sp", own[:], c_own16.ap(), writes=[r_c2])
            kTp = [ph.sb(f"kTp{i}", [64, S], BF16) for i in range(2)]
            qTp = [ph.sb(f"qTp{i}", [64, S], BF16) for i in range(2)]
            r_kq = RL(2)
            kmf = ph.sb("kmf", [64, 16], F32)
            kmb = ph.sb("kmb", [64, 16], BF16)
            rm = ph.sb("rm", [128, 512], F32)
            sel = ph.sb("sel", [128, 512], F32)
            penf = ph.sb("penf", [128, 512], F32)
            m8 = ph.sb("m8", [128, NT, 8], F32)
            penT = [ph.sb(f"penT{i}", [16, S], BF16) for i in range(2)]
            r_penT = RL(2)
            r_prep = Res()
            r_rm, r_m8, r_selm = Res(), Res(), Res()
            r_penm = RL(8)

            def prep_load(h):
                i = h % 2
                C.dma("sp", kTp[i][:], ft_rows(FT0, (4 + h // 2) * 128 + (h % 2) * 64), writes=[r_kq[i]])
                C.dma("sp", qTp[i][:], ft_rows(FT0, (h // 2) * 128 + (h % 2) * 64), writes=[r_kq[i]])

            prep_load(0)
            for h in range(8):
                i = h % 2
                if h + 1 < 8:
                    prep_load(h + 1)
                C.dve(lambda: nc.vector.tensor_reduce(out=kmf[:], in_=kTp[i][:, :].rearrange("p (n k) -> p n k", k=256),
                                                      axis=AX.X, op=ALU.add), [r_kq[i]], [r_prep])
                C.dve(lambda: nc.vector.tensor_scalar(out=kmb[:], in0=kmf[:], scalar1=1.0 / 256, scalar2=None,
                                                      op0=ALU.mult), [r_prep], [r_prep])
                for qt in range(NT):
                    C.mm(banks[6][:, qt * 16:(qt + 1) * 16], qTp[i][:, qt * 128:(qt + 1) * 128], kmb[:], True, True,
                         [r_kq[i], r_prep], [bres[6]])
                C.dve(lambda: nc.vector.tensor_tensor(out=rm[:], in0=banks[6][:, :], in1=cm[:], op=ALU.add),
                      [bres[6], r_c2], [r_rm])
                for qt in range(NT):
                    C.dve(lambda qt=qt: nc.vector.max(out=m8[:, qt, :], in_=rm[:, qt * 16:(qt + 1) * 16]),
                          [r_rm], [r_m8])
                for qt in range(NT):
                    C.dve(lambda qt=qt: nc.vector.tensor_scalar(out=sel[:, qt * 16:(qt + 1) * 16],
                                                                in0=rm[:, qt * 16:(qt + 1) * 16],
                                                                scalar1=m8[:, qt, 2:3], scalar2=None, op0=ALU.is_ge),
                          [r_rm, r_m8], [r_selm])
                C.dve(lambda: nc.vector.tensor_tensor(out=sel[:], in0=sel[:], in1=own[:], op=ALU.max),
                      [r_selm, r_c2], [r_prep])
                C.dve(lambda: nc.vector.tensor_scalar(out=penf[:], in0=sel[:], scalar1=PEN, scalar2=-PEN, op0=ALU.mult,
                                                      op1=ALU.add), [r_prep], [r_prep])
                for g4 in range(8):
                    for u in range(4):
                        qt = 4 * g4 + u
                        C.tr(banks[7][0:16, u * 128:(u + 1) * 128], penf[:, qt * 16:(qt + 1) * 16], identf[:],
                             [r_prep, r_const], [bres[7]])
                    C.evac(penT[i][:, g4 * 512:(g4 + 1) * 512], banks[7][0:16, :], [bres[7]], [r_penT[i]])
                C.dma("pool", PENM.ap()[h * 16:(h + 1) * 16, :], penT[i][:], reads=[r_penT[i]], writes=[r_penm[h]])

            osb = [ph.sb(f"osb{i}", [128, NT, 64], BF16) for i in range(2)]
            r_osb = RL(2)

            def simple_fin(oi):
                def fin(A, items):
                    dens = rden_multi(A, items, 64)
                    for (qt, bank, rb), (den, rd) in zip(items, dens):
                        C.act(osb[oi][:, qt, :], bank[:, 0:64], AF.Copy, [rb, rd], [r_osb[oi]], scale=den[:, 0:1])
                return fin

            def simple_after(oi, dst, col0):
                def after():
                    C.dma("pool", dst.ap()[:, col0:col0 + 64].rearrange("(t p) c -> p t c", p=128), osb[oi][:],
                          reads=[r_osb[oi]])
                return after

            jobs = []
            for h in range(8):
                jobs.append(dict(
                    qa=[(0, 64, ft_rows(FT0, (h // 2) * 128 + (h % 2) * 64)), (64, 16, PENM.ap()[h * 16:(h + 1) * 16, :])],
                    ka=[(0, 64, S, ft_rows(FT0, (4 + h // 2) * 128 + (h % 2) * 64)), (64, 16, S, B_OHB16.ap())],
                    va=(TM0.ap()[:, h * 64:(h + 1) * 64], NT, 64),
                    w=[(0, 4480, STRC, h * 128 * LS + 128, LS - 1)],
                    deps=[r_penm[h]], KQ=80, VW=65, tiles=causal_tiles, wmul=strip_wmul, far=True, pool_share=True,
                    fin=simple_fin(h % 2), after=simple_after(h % 2, OATT, h * 64)))
            run_jobs(A, jobs)
    if upto < 3:
        return _finish(C, nc)

    with Phase(C) as ph:
        w1s = ph.sb("w1s", [64, 32, 256], BF16)
        w2s = ph.sb("w2s", [128, 2, 64], BF16)
        posf = ph.sb("posf", [64, 32], F32)
        pos2 = ph.sb("pos2", [64, 32, 2], BF16)
        rawT = ph.sb("rawT", [64, S], BF16)
        cb = ph.sb("cb", [128, 2], F32)
        HT = ph.sb("HT", [128, 2, 256], BF16)
        kcs = ph.sb("kcs", [64, 256], BF16)
        vcs = ph.sb("vcs", [128, 2, 64], BF16)
        r_w, r_raw, r_cb, r_HT, r_o = Res(), Res(), Res(), Res(), Res()
        C.dve(lambda: nc.vector.memset(HT[:], 0.0), [], [r_HT])
        for kv in range(2):
            w1src = [cmp_k_w1, cmp_v_w1][kv]
            w2src = [cmp_k_w2, cmp_v_w2][kv]
            possrc = [cmp_pos_k, cmp_pos_v][kv]
            for l0 in range(0, 32, 8):
                C.dma("pool", w1s[:, l0:l0 + 8, :],
                      w1src.ap()[l0 * 64:(l0 + 8) * 64, :].rearrange("(l d) h -> d l h", d=64), writes=[r_w])
            C.dma("pool", w2s[:], w2src.ap().rearrange("(t p) c -> p t c", p=128), writes=[r_w])
            C.dma("sp", posf[:], possrc.ap().rearrange("l d -> d l"), writes=[r_w], allow_slow_non_contiguous=True)
            for u in range(2):
                C.dve(lambda u=u: nc.vector.tensor_copy(out=pos2[:, :, u], in_=posf[:]), [r_w], [r_w])
            for ht in range(2):
                for l in range(32):
                    C.mm(banks[6][:, ht * 2:ht * 2 + 2], w1s[:, l, ht * 128:(ht + 1) * 128], pos2[:, l, :], l == 0,
                         l == 31, [r_w], [bres[6]])
            C.evac(cb[:, 0:1], banks[6][:, 0:1], [bres[6]], [r_cb], eng="dve")
            C.evac(cb[:, 1:2], banks[6][:, 2:3], [bres[6]], [r_cb], eng="dve")
            for g in range(2):
                C.dma("sp", rawT[:], ft_rows(FT0, (12 + kv) * 128 + g * 64), writes=[r_raw])
                rv = rawT[:, :].rearrange("p (n s) -> p n s", s=16)
                for ht in range(2):
                    for l in range(32):
                        rhs = rv[:, 0:255, l] if l < 16 else rv[:, 1:256, l - 16]
                        C.mm(banks[ht][:, 0:255], w1s[:, l, ht * 128:(ht + 1) * 128], rhs, l == 0, l == 31,
                             [r_w, r_raw], [bres[ht]])
                    C.act(HT[:, ht, 0:255], banks[ht][:, 0:255], AF.Silu, [bres[ht], r_cb], [r_HT],
                          bias=cb[:, ht:ht + 1])
                if kv == 0:
                    for ht in range(2):
                        C.mm(banks[2][0:64, 0:256], w2s[:, ht, :], HT[:, ht, :], ht == 0, ht == 1, [r_w, r_HT],
                             [bres[2]])
                    C.evac(kcs[:], banks[2][0:64, 0:256], [bres[2]], [r_o])
                    C.dma("pool", KCMP.ap()[g * 64:(g + 1) * 64, :], kcs[:], reads=[r_o])
                else:
                    for nt in range(2):
                        for ht in range(2):
                            C.mm(banks[3 + nt][:, 0:64], HT[:, ht, nt * 128:(nt + 1) * 128], w2s[:, ht, :], ht == 0,
                                 ht == 1, [r_w, r_HT], [bres[3 + nt]])
                        C.evac(vcs[:, nt, :], banks[3 + nt][:, 0:64], [bres[3 + nt]], [r_o])
                    C.dma("pool", VCMP.ap()[g * 256:(g + 1) * 256, :].rearrange("(t p) c -> p t c", p=128), vcs[:],
                          reads=[r_o])
    if upto < 4:
        return _finish(C, nc)

    with Phase(C) as ph:
        A = alloc_attn(ph, S, 129, 8192)
        for aset in A["sets"]:
            C.dve(lambda aset=aset: nc.vector.memset(aset["VA"][:, :, 64:65], 1.0), [], [aset["r"]])
            C.dma("sp", aset["VA"][:, 0:2, 65:129], B_OVL.ap().rearrange("(t p) c -> p t c", p=128),
                  writes=[aset["r"]])
        oacc = ph.sb("oacc", [128, NT, 512], F32)
        imps = [ph.sb(f"imp{g}", [128, NT, 64], F32) for g in range(2)]
        r_imps = RL(2)
        force = ph.sb("force", [128, NT, 64], F32)
        r_oacc, r_force = Res(), Res()
        C.dma("sp", force[:].rearrange("p t m -> p (t m)"), c_force.ap(), writes=[r_force])
        sg = [ph.sb(f"sg{i}", [128, 2], F32) for i in range(8)]
        r_sg = RL(8)
        sgi = [0]
        r_pens = RL(2)

        def nsa_fin(hn, branch):
            def fin(A, items):
                dens = rden_multi(A, items, 64)
                imp, r_imp = imps[hn // 4], r_imps[hn // 4]
                ks = []
                for (qt, bank, rb), (den, rd) in zip(items, dens):
                    k = sgi[0] % 8
                    sgi[0] += 1
                    ks.append(k)
                    C.dve(lambda k=k, den=den, qt=qt: nc.vector.tensor_tensor(
                        out=sg[k][:, 0:1], in0=den[:, 0:1], in1=gates[:, qt, 3 * hn + branch:3 * hn + branch + 1],
                        op=ALU.mult), [rd, r_gates], [r_sg[k]])
                for (qt, bank, rb), (den, rd), k in zip(items, dens, ks):
                    oslice = oacc[:, qt, hn * 64:(hn + 1) * 64]
                    if branch == 0:
                        C.act(oslice, bank[:, 0:64], AF.Copy, [rb, r_sg[k]], [r_oacc], scale=sg[k][:, 0:1])
                    else:
                        C.dve(lambda oslice=oslice, bank=bank, k=k: nc.vector.scalar_tensor_tensor(
                            out=oslice, in0=bank[:, 0:64], scalar=sg[k][:, 0:1], in1=oslice, op0=ALU.mult,
                            op1=ALU.add), [rb, r_sg[k], r_oacc], [r_oacc])
                if branch == 0:
                    for (qt, bank, rb), (den, rd) in zip(items, dens):
                        if hn % 4 == 0:
                            C.act(imp[:, qt, :], bank[:, 65:129], AF.Copy, [rb, rd], [r_imp], scale=den[:, 0:1])
                        else:
                            C.dve(lambda qt=qt, bank=bank, den=den: nc.vector.scalar_tensor_tensor(
                                out=imp[:, qt, :], in0=bank[:, 65:129], scalar=den[:, 0:1], in1=imp[:, qt, :],
                                op0=ALU.mult, op1=ALU.add), [rb, rd, r_imp], [r_imp])
            return fin

        def cmp_tiles(qc):
            return [(0, 0, 3)] + ([(1, 0, 3)] if qc >= 4 else [])

        def cmp_wmul(aset, kt, qc, jlo, jhi):
            return [(0, 512, aset["W"][:, kt * 4096 + qc * 512:kt * 4096 + (qc + 1) * 512], aset["r"])]

        i2 = ph.sb("i2", [128, 4, 64], F32)
        i3 = ph.sb("i3", [128, 4, 64], F32)
        m8a = ph.sb("m8a", [128, 4, 8], F32)
        m8b = ph.sb("m8b", [128, 4, 8], F32)
        sel2 = ph.sb("sel2", [128, 4, 64], F32)
        pnf = [ph.sb(f"pnf{i}", [128, 4, 64], F32) for i in range(2)]
        r_i2, r_i3, r_m8a, r_m8b, r_s2 = Res(), Res(), Res(), Res(), Res()
        r_pnf = RL(2)
        penTs = ph.sb("penTs", [64, S], BF16)
        r_penTs = Res()

        def sel_piece(g, k):
            pb = k % 2
            C.dve(lambda: nc.vector.tensor_tensor(out=i2[:], in0=imps[g][:, 4 * k:4 * k + 4, :],
                                                  in1=force[:, 4 * k:4 * k + 4, :], op=ALU.add),
                  [r_imps[g], r_force], [r_i2])
            for u in range(4):
                C.dve(lambda u=u: nc.vector.max(out=m8a[:, u, :], in_=i2[:, u, :]), [r_i2], [r_m8a])
            for u in range(4):
                C.dve(lambda u=u: nc.vector.match_replace(out=i3[:, u, :], in_to_replace=m8a[:, u, :],
                                                          in_values=i2[:, u, :], imm_value=-3.0e38),
                      [r_i2, r_m8a], [r_i3])
            for u in range(4):
                C.dve(lambda u=u: nc.vector.max(out=m8b[:, u, :], in_=i3[:, u, :]), [r_i3], [r_m8b])
            for u in range(4):
                C.dve(lambda u=u: nc.vector.tensor_scalar(out=sel2[:, u, :], in0=i2[:, u, :], scalar1=m8b[:, u, 7:8],
                                                          scalar2=None, op0=ALU.is_ge), [r_i2, r_m8b], [r_s2])
            C.dve(lambda: nc.vector.tensor_scalar(out=pnf[pb][:], in0=sel2[:], scalar1=PEN, scalar2=-PEN, op0=ALU.mult,
                                                  op1=ALU.add), [r_s2], [r_pnf[pb]])
            for u in range(4):
                C.tr(banks[7][0:64, u * 128:(u + 1) * 128], pnf[pb][:, u, :], identf[:], [r_pnf[pb], r_const],
                     [bres[7]])
            C.evac(penTs[:, k * 512:(k + 1) * 512], banks[7][0:64, :], [bres[7]], [r_penTs])
            if k == 7:
                C.dma("pool", PENS.ap()[g * 64:(g + 1) * 64, :], penTs[:], reads=[r_penTs], writes=[r_pens[g]])

        def win_hook(g, j):
            def hook(qc):
                if qc == 3:
                    sel_piece(g, 2 * j)
                elif qc == 7:
                    sel_piece(g, 2 * j + 1)
            return hook

        jobs_cmp, jobs_win, jobs_slc = [], [], []
        for g in range(2):
            for j in range(4):
                hn = 4 * g + j
                qrows = ft_rows(FT0, (8 + hn // 2) * 128 + (hn % 2) * 64)
                jobs_cmp.append(dict(
                    qa=[(0, 64, qrows)], ka=[(0, 64, 256, KCMP.ap()[g * 64:(g + 1) * 64, :])],
                    va=(VCMP.ap()[g * 256:(g + 1) * 256, :], 2, 64),
                    w=[(nt * 4096, 4096, STRK, hn * 128 * LC + 4081 - 2048 * nt, LC - 16) for nt in range(2)],
                    KQ=64, VW=129, tiles=cmp_tiles, wmul=cmp_wmul, fin=nsa_fin(hn, 0)))
                jobs_win.append(dict(
                    qa=[(0, 64, qrows)], ka=[(0, 64, S, ft_rows(FT0, 15 * 128 + g * 64))],
                    va=(TM0.ap()[:, 640 + g * 64:640 + (g + 1) * 64], NT, 64),
                    w=[(0, 4480, STRW, hn * 128 * LS + 128, LS - 1)],
                    KQ=64, VW=65, tiles=window_tiles, wmul=strip_wmul, fin=nsa_fin(hn, 2), hook=win_hook(g, j),
                    pool_share=True))
                jobs_slc.append(dict(
                    qa=[(0, 64, qrows), (64, 64, PENS.ap()[g * 64:(g + 1) * 64, :])],
                    ka=[(0, 64, S, ft_rows(FT0, 14 * 128 + g * 64)), (64, 64, S, B_OHB64.ap())],
                    va=(TM0.ap()[:, 512 + g * 64:512 + (g + 1) * 64], NT, 64),
                    w=[(0, 4480, STRC, (8 + hn) * 128 * LS + 128, LS - 1)],
                    deps=[r_pens[g]], KQ=128, VW=65, tiles=causal_tiles, wmul=strip_wmul, fin=nsa_fin(hn, 1),
                    far=True, pool_share=True))
        run_jobs(A, jobs_cmp + jobs_win + jobs_slc)
        ob = [ph.sb(f"ob{i}", [128, 4, 512], BF16) for i in range(2)]
        r_ob = RL(2)
        for q8 in range(8):
            i = q8 % 2
            C.evac(ob[i][:], oacc[:, q8 * 4:(q8 + 1) * 4, :], [r_oacc], [r_ob[i]])
            C.dma("pool", OATT.ap()[q8 * 512:(q8 + 1) * 512, 512:1024].rearrange("(t p) c -> p t c", p=128), ob[i][:],
                  reads=[r_ob[i]])
    if upto < 5:
        return _finish(C, nc)

    def outproj_phase(wsrc, hsrc, hdst, after_wload=None):
        with Phase(C) as ph:
            wout, r_wout = load_w_bf16(ph, "wout", wsrc, 0, 8, D)
            if after_wload is not None:
                after_wload()
            ot = [ph.sb(f"ot{i}", [128, D], BF16) for i in range(2)]
            ht = [ph.sb(f"ht{i}", [128, D], F32) for i in range(3)]
            oT = [ph.sb(f"oT{i}", [128, 8, 128], BF16) for i in range(2)]
            r_ot, r_ht, r_oT = RL(2), RL(3), RL(2)

            def ld(tt):
                C.dma("sp", ot[tt % 2][:], OATT.ap()[tt * 128:(tt + 1) * 128, :], writes=[r_ot[tt % 2]])
                C.dma("sp", ht[tt % 3][:], hsrc.ap()[tt * 128:(tt + 1) * 128, :], writes=[r_ht[tt % 3]])

            ld(0)
            for tt in range(NT):
                if tt + 1 < NT:
                    ld(tt + 1)
                i = tt % 2
                bb = bankbf(6 + i)
                for k in range(8):
                    C.tr(bb[:, k * 128:(k + 1) * 128], ot[i][:, k * 128:(k + 1) * 128], identb[:], [r_ot[i], r_const],
                         [bres[6 + i]])
                C.evac(oT[i][:], bb.rearrange("p (k t) -> p k t", k=8), [bres[6 + i]], [r_oT[i]])
                for half in range(2):
                    b = 2 * i + half
                    for k in range(8):
                        C.mm(banks[b][:, :], oT[i][:, k, :], wout[:, k, half * 512:(half + 1) * 512], k == 0, k == 7,
                             [r_oT[i], r_wout], [bres[b]])
                    hs = ht[tt % 3][:, half * 512:(half + 1) * 512]
                    C.dve(lambda hs=hs, b=b: nc.vector.tensor_tensor(out=hs, in0=banks[b][:, :], in1=hs, op=ALU.add),
                          [bres[b], r_ht[tt % 3]], [r_ht[tt % 3]])
                C.dma("sp", hdst.ap()[tt * 128:(tt + 1) * 128, :], ht[tt % 3][:], reads=[r_ht[tt % 3]])

    def outproj_mlp(wsrc, hsrc, hmid, layer, hdst, final):
        with Phase(C) as ph:
            w1 = ph.sb("w1", [128, 8, DFF], BF16)
            w2 = ph.sb("w2", [128, 32, D], BF16)
            r_w1, r_w2 = Res(), Res()

            def issue():
                issue_w(w1, mlp_w1, layer * D, 8, DFF, r_w1)
                issue_w(w2, mlp_w2, layer * DFF, 32, D, r_w2)

            outproj_phase(wsrc, hsrc, hmid, after_wload=issue)
            mlp_body(ph, layer, hmid, hdst, final, w1, r_w1, w2, r_w2)

    def mlp_body(ph, layer, hsrc, hdst, final, w1, r_w1, w2, r_w2):
        if True:
            g, r_g = bcast_row(ph, "gmlp", mlp_norm, layer)
            if final:
                gf, r_gf = bcast_row(ph, "gfin", final_norm, 0)
            ht = [ph.sb(f"mh{i}", [128, D], F32) for i in range(3)]
            r_ht = RL(3)
            hn = [ph.sb(f"mhn{i}", [128, D], BF16) for i in range(2)]
            r_hn = RL(2)
            junk = ph.sb("mjunk", [128, D], BF16)
            smalls = [(junk, ph.sb(f"mss{i}", [128, 1], F32), ph.sb(f"mrs{i}", [128, 1], F32)) for i in range(2)]
            r_sm = RL(2)
            hnT = [ph.sb(f"mhnT{i}", [128, 8, 128], BF16) for i in range(2)]
            r_hnT = RL(2)
            aT = ph.sb("aT", [128, 32, 128], BF16)
            r_aT = RL(32)
            rl = [ph.sb(f"rl{i}", [128, 128], F32) for i in range(4)]
            r_rl = RL(4)
            if final:
                fo = [ph.sb(f"fo{i}", [128, D], F32) for i in range(2)]
                r_fo = RL(2)
                fss = [ph.sb(f"fss{i}", [128, 2], F32) for i in range(2)]
                r_fss = RL(2)

            def ld(tt):
                C.dma("sp", ht[tt % 3][:], hsrc.ap()[tt * 128:(tt + 1) * 128, :], writes=[r_ht[tt % 3]])

            ld(0)
            ld(1)
            for tt in range(NT):
                if tt + 2 < NT:
                    ld(tt + 2)
                i = tt % 2
                h3 = tt % 3
                rmsnorm_T(ph, "m", ht[h3][:], r_ht[h3], g, r_g, hn[i], r_hn[i], smalls[i], r_sm[i], hnT[i][:],
                          r_hnT[i], 6 + i)
                for f in range(32):
                    b = f % 4
                    for k in range(8):
                        C.mm(banks[b][:, 0:128], w1[:, k, f * 128:(f + 1) * 128], hnT[i][:, k, :], k == 0, k == 7,
                             [r_w1, r_hnT[i]], [bres[b]])
                    C.act(rl[b][:], banks[b][:, 0:128], AF.Relu, [bres[b]], [r_rl[b]])
                    C.dve(lambda f=f, b=b: nc.vector.tensor_tensor(out=aT[:, f, :], in0=rl[b][:], in1=rl[b][:],
                                                                   op=ALU.mult), [r_rl[b]], [r_aT[f]])
                for half in range(2):
                    b = 4 + half
                    for f in range(32):
                        C.mm(banks[b][:, :], aT[:, f, :], w2[:, f, half * 512:(half + 1) * 512], f == 0, f == 31,
                             [r_aT[f], r_w2], [bres[b]])
                    hs = ht[h3][:, half * 512:(half + 1) * 512]
                    C.dve(lambda hs=hs, b=b: nc.vector.tensor_tensor(out=hs, in0=banks[b][:, :], in1=hs, op=ALU.add),
                          [bres[b], r_ht[h3]], [r_ht[h3]])
                if not final:
                    C.dma("pool", hdst.ap()[tt * 128:(tt + 1) * 128, :], ht[h3][:], reads=[r_ht[h3]])
                else:
                    ss, rstd = fss[i][:, 0:1], fss[i][:, 1:2]
                    C.dve(lambda: nc.vector.scalar_tensor_tensor(out=junk[:], in0=ht[h3][:], scalar=1.0, in1=ht[h3][:],
                                                                 op0=ALU.mult, op1=ALU.mult, accum_out=ss),
                          [r_ht[h3]], [r_fss[i]])
                    C.dve(lambda: nc.vector.tensor_scalar(out=rstd, in0=ss, scalar1=1.0 / D, scalar2=EPS, op0=ALU.mult,
                                                          op1=ALU.add), [r_fss[i]], [r_fss[i]])
                    C.act(rstd, rstd, AF.Sqrt, [r_fss[i]], [r_fss[i]])
                    C.dve(lambda: nc.vector.reciprocal(out=rstd, in_=rstd), [r_fss[i]], [r_fss[i]])
                    C.dve(lambda: nc.vector.scalar_tensor_tensor(out=fo[i][:], in0=ht[h3][:], scalar=rstd, in1=gf[:],
                                                                 op0=ALU.mult, op1=ALU.mult),
                          [r_ht[h3], r_fss[i], r_gf], [r_fo[i]])
                    C.dma("pool", hdst.ap()[tt * 128:(tt + 1) * 128, :], fo[i][:], reads=[r_fo[i]])

    outproj_mlp(even_w_out, x, H1, 0, H2, False)
    if upto < 7:
        return _finish(C, nc)

    SPD = nc.dram_tensor("SPD", [16, S], F32, kind="ExternalOutput" if (dbgset is None and dbg) or (dbgset and "SPD" in dbgset) else "Internal")
    fm1 = []
    for m in range(8):
        fm1.append((m * 128, 128, 0.125, FT1.ap()[m * 128:(m + 1) * 128, :]))
    for m in range(8):
        fm1.append((1024 + m * 128, 128, None, FT1.ap()[1024 + m * 128:1024 + (m + 1) * 128, :]))
    fm1.append((3072, 16, None, None))
    tm1_groups = [[(2048, 512)], [(2560, 512)]]

    def pre1(env):
        ph = env["ph"]
        env["nb"] = ph.sb("nb", [16, 1], F32)
        env["e16"] = ph.sb("e16", [16, 512], F32)
        env["sp16"] = [ph.sb(f"sp16_{i}", [16, 512], F32) for i in range(2)]
        env["r_nb"], env["r_e"], env["r_sp"] = Res(), Res(), RL(2)
        C.dma("sp", env["nb"][:], odd_b_forget.ap(), writes=[env["r_nb"]])
        C.dve(lambda: nc.vector.tensor_scalar(out=env["nb"][:], in0=env["nb"][:], scalar1=-1.0, scalar2=None,
                                              op0=ALU.mult), [env["r_nb"]], [env["r_nb"]])

    def fm1_special(env, c, bank, rb):
        i = c % 2
        C.act(env["e16"][:], bank[0:16, :], AF.Exp, [rb, env["r_nb"]], [env["r_e"]], scale=-1.0, bias=env["nb"][:, 0:1])
        C.act(env["sp16"][i][:], env["e16"][:], AF.Ln, [env["r_e"]], [env["r_sp"][i]], bias=1.0)
        C.dma("pool", SPD.ap()[:, c * 512:(c + 1) * 512], env["sp16"][i][:], reads=[env["r_sp"][i]])

    def tm1_handler(env, tt, gi, bank, rb, tsb, r_tsb):
        C.evac(tsb[:, gi * 512:(gi + 1) * 512], bank[:, 0:512], [rb], [r_tsb])
        if gi == 1:
            C.dma("pool", TM1.ap()[tt * 128:(tt + 1) * 128, :], tsb[:, 0:1024], reads=[r_tsb])

    proj_phase(1, H2, odd_w_in, ODD_IN, mix_norm, fm1, tm1_groups, fm1_special, tm1_handler, pre=pre1)
    with Phase(C) as ph:
        Cm = ph.sb("Cm", [16, S], F32)
        spb = ph.sb("spb", [16, S], F32)
        ones16 = ph.sb("ones16", [16, S], F32)
        parts = ph.sb("parts", [16, 6, S], BF16)
        r_c = Res()
        C.dma("sp", spb[:], SPD.ap(), writes=[r_c])
        C.dve(lambda: nc.vector.memset(ones16[:], 1.0), [], [r_c])
        C.dve(lambda: nc.vector.tensor_tensor_scan(out=Cm[:], data0=ones16[:], data1=spb[:], initial=0.0, op0=ALU.mult,
                                                   op1=ALU.add), [r_c], [r_c])
        C.dve(lambda: nc.vector.tensor_copy(out=parts[:, 0, :], in_=Cm[:]), [r_c], [r_c])
        C.dve(lambda: nc.vector.tensor_tensor(out=spb[:], in0=Cm[:], in1=parts[:, 0, :], op=ALU.subtract), [r_c], [r_c])
        C.dve(lambda: nc.vector.tensor_copy(out=parts[:, 1, :], in_=spb[:]), [r_c], [r_c])
        C.dve(lambda: nc.vector.tensor_tensor(out=spb[:], in0=spb[:], in1=parts[:, 1, :], op=ALU.subtract), [r_c], [r_c])
        C.dve(lambda: nc.vector.tensor_copy(out=parts[:, 2, :], in_=spb[:]), [r_c], [r_c])
        for u in range(3):
            C.dve(lambda u=u: nc.vector.tensor_scalar(out=parts[:, 3 + u, :], in0=parts[:, u, :], scalar1=-1.0,
                                                      scalar2=None, op0=ALU.mult), [r_c], [r_c])
        for u in range(6):
            C.dma("pool", CUMP.ap()[u * 16:(u + 1) * 16, :], parts[:, u, :], reads=[r_c])
    if upto < 8:
        return _finish(C, nc)

    with Phase(C) as ph:
        A = alloc_attn(ph, S, 65, 128)
        tri = ph.sb("tri", [128, 128], BF16)
        r_tri = Res()
        C.dma("pool", tri[:], c_tri.ap(), writes=[r_tri])
        for aset in A["sets"]:
            C.dve(lambda aset=aset: nc.vector.memset(aset["VA"][:, :, 64:65], 1.0), [], [aset["r"]])
        osb = [ph.sb(f"fosb{i}", [128, NT, 64], BF16) for i in range(2)]
        r_osb = RL(2)

        def fox_fin(oi):
            def fin(A, items):
                dens = rden_multi(A, items, 64)
                for (qt, bank, rb), (den, rd) in zip(items, dens):
                    C.act(osb[oi][:, qt, :], bank[:, 0:64], AF.Copy, [rb, rd], [r_osb[oi]], scale=den[:, 0:1])
            return fin

        def fox_after(oi, col0):
            def after():
                C.dma("pool", OATT.ap()[:, col0:col0 + 64].rearrange("(t p) c -> p t c", p=128), osb[oi][:],
                      reads=[r_osb[oi]])
            return after

        def fox_wmul(aset, kt, qc, jlo, jhi):
            if kt >= 4 * qc:
                return [(jlo * 128, jlo * 128 + 128, tri[:], r_tri)]
            return []

        jobs = []
        for h in range(16):
            jobs.append(dict(
                qa=[(0, 64, ft_rows(FT1, h * 64)), (64, 3, bass.AP(CUMP, (48 + h) * S, [[16 * S, 3], [1, S]])),
                    (67, 3, B_ONES.ap()[0:3, :])],
                ka=[(0, 64, S, ft_rows(FT1, 1024 + h * 64)), (64, 3, S, B_ONES.ap()[0:3, :]),
                    (67, 3, S, bass.AP(CUMP, h * S, [[16 * S, 3], [1, S]]))],
                va=(TM1.ap()[:, h * 64:(h + 1) * 64], NT, 64), w=[],
                KQ=70, VW=65, tiles=causal_tiles, wmul=fox_wmul, fin=fox_fin(h % 2), after=fox_after(h % 2, h * 64)))
        run_jobs(A, jobs)
    if upto < 9:
        return _finish(C, nc)
    outproj_mlp(odd_w_out, H2, H3, 1, out, True)
    _finish(C, nc)


def _finish(C, nc):
    C.barrier()


def rel_bucket_np(n):
    n = np.maximum(n, 0)
    nf = np.maximum(n, 1).astype(np.float32)
    large = 16 + (np.log(nf / np.float32(16)) / np.float32(np.log(1024 / 16)) * np.float32(16)).astype(np.int32)
    large = np.minimum(large, 31)
    return np.where(n < 16, n, large)


def make_consts():
    c = {}
    c["c_ident"] = np.eye(128, dtype=np.float32)
    r = np.arange(LS) - 512
    oh = np.zeros((33, LS), np.float32)
    b = rel_bucket_np(r)
    valid = r >= 0
    oh[b[valid], np.nonzero(valid)[0]] = 1.0
    oh[32, ~valid] = 1.0
    c["c_ohc"] = oh
    ohw = np.zeros((33, LS), np.float32)
    validw = (r >= 0) & (r < 512)
    ohw[b[validw], np.nonzero(validw)[0]] = 1.0
    ohw[32, ~validw] = 1.0
    c["c_ohw"] = ohw
    r2 = np.arange(LC) - 4112
    b2 = rel_bucket_np(r2)
    ohk = np.zeros((33, LC), np.float32)
    v2 = r2 >= 0
    ohk[b2[v2], np.nonzero(v2)[0]] = 1.0
    ohk[32, ~v2] = 1.0
    c["c_ohcmp"] = ohk
    n = np.arange(16)[None, :]
    blk = (np.arange(32) // 2)[:, None]
    cm = np.where(n < blk, 0.0, -1e30).astype(np.float32)
    own = (n >= blk).astype(np.float32)
    c["c_cm16"] = np.tile(cm.reshape(1, 512), (128, 1))
    c["c_own16"] = np.tile(own.reshape(1, 512), (128, 1))
    t = (np.arange(32)[None, :, None] * 128 + np.arange(128)[:, None, None])
    sblk = t // 64
    m = np.arange(64)[None, None, :]
    forced = (m == 0) | (m == sblk) | (m == sblk - 1)
    f = np.where(forced, 1e6, np.where(m <= sblk, 0.0, -1e30)).astype(np.float32)
    c["c_force"] = f.reshape(128, 2048)
    ci = np.arange(256)[:, None] * 16
    sj = np.arange(64)[None, :] * 64
    ov = ((ci < sj + 64) & (ci + 32 > sj)).astype(np.float32)
    ov[255, :] = 0.0
    c["c_overlap"] = ov
    tok = np.arange(S)[None, :]
    c["c_ohb64"] = (tok // 64 == np.arange(64)[:, None]).astype(np.float32)
    c["c_ohb16"] = (tok // 256 == np.arange(16)[:, None]).astype(np.float32)
    c["c_tri"] = (np.arange(128)[None, :] >= np.arange(128)[:, None]).astype(np.float32)
    c["c_ones"] = np.ones((128, S), np.float32)
    return c


_CACHE = {}


def make_in_maps(inputs, cores):
    consts = make_consts()
    f = lambda a: np.ascontiguousarray(np.asarray(a, dtype=np.float32))
    shared = {
        "rel_bias": f(inputs["rel_bias"]),
        "mix_norm": f(inputs["mix_norm"]),
        "mlp_norm": f(inputs["mlp_norm"]),
        "even_w_in": f(inputs["even_w_in"][0]),
        "even_w_out": f(inputs["even_w_out"][0]),
        "cmp_pos_k": f(inputs["cmp_pos_k"][0]),
        "cmp_pos_v": f(inputs["cmp_pos_v"][0]),
        "cmp_k_w1": f(inputs["cmp_k_w1"][0]),
        "cmp_k_w2": f(inputs["cmp_k_w2"][0]),
        "cmp_v_w1": f(inputs["cmp_v_w1"][0]),
        "cmp_v_w2": f(inputs["cmp_v_w2"][0]),
        "odd_w_in": f(inputs["odd_w_in"][0]),
        "odd_b_forget": f(np.asarray(inputs["odd_b_forget"]).reshape(16, 1)),
        "odd_w_out": f(inputs["odd_w_out"][0]),
        "mlp_w1": f(np.asarray(inputs["mlp_w1"]).reshape(2 * D, DFF)),
        "mlp_w2": f(np.asarray(inputs["mlp_w2"]).reshape(2 * DFF, D)),
        "final_norm": f(np.asarray(inputs["final_norm"]).reshape(1, D)),
    }
    shared.update(consts)
    xs = np.asarray(inputs["x"], dtype=np.float32)
    maps = []
    for b in cores:
        m = dict(shared)
        m["x"] = np.ascontiguousarray(xs[b])
        maps.append(m)
    return maps


def kernel(**inputs):
    if "nc" not in _CACHE:
        nc = bass.Bass("TRN2", target_bir_lowering=False)
        build(nc)
        _CACHE["nc"] = nc
    nc = _CACHE["nc"]
    maps = make_in_maps(inputs, list(range(8)))
    res = run_bass_kernel_spmd(nc, maps, core_ids=list(range(8)))
    return np.stack([np.asarray(r["out"], dtype=np.float32) for r in res.results], axis=0)
```

```python
import contextlib
import numpy as np
import concourse.bass as bass
import concourse.mybir as mybir
from concourse.bass_utils import run_bass_kernel_spmd

F32 = mybir.dt.float32
BF16 = mybir.dt.bfloat16
AF = mybir.ActivationFunctionType
ALU = mybir.AluOpType
AX = mybir.AxisListType

D = 1024
S = 4096
NT = 32
DFF = 4096
EVEN_IN = 2840
ODD_IN = 3088
LS = 4608
LC = 8320
PEN = 30000.0
EPS = 1e-5


class Res:
    __slots__ = ("w", "r")

    def __init__(self):
        self.w = {}
        self.r = {}


def RL(n):
    return [Res() for _ in range(n)]


class Ctx:
    def __init__(self, nc, st):
        self.nc = nc
        self.engs = {"pe": nc.tensor, "act": nc.scalar, "dve": nc.vector, "pool": nc.gpsimd, "sp": nc.sync}
        self.sem = {k: st.enter_context(nc.semaphore("sem_" + k)) for k in ["pe", "act", "dve", "pool"]}
        self.cnt = {k: 0 for k in self.sem}
        self.seen = {k: {} for k in self.engs}
        self.hist = {}
        self.rings = {}
        for q, n in [("sp", 12), ("pool", 8), ("act", 4)]:
            self.rings[q] = dict(sems=[st.enter_context(nc.semaphore(f"dq_{q}_{i}")) for i in range(n)], i=0)
        self.last = {}
        self.flip = 0
        self.ninst = 0

    def _semfor(self, key):
        if isinstance(key, tuple):
            return self.rings[key[1]]["sems"][key[2]]
        return self.sem[key]

    def _wait(self, e, key, val):
        s = self.seen[e]
        if s.get(key, 0) >= val:
            return
        self.engs[e].wait_ge(self._semfor(key), val)
        self.ninst += 1
        s[key] = val
        snap = self.hist.get((key, val))
        if snap:
            for k2, v2 in snap.items():
                if s.get(k2, 0) < v2:
                    s[k2] = v2

    def _deps(self, e, reads, writes):
        for r in reads:
            for key, val in list(r.w.items()):
                self._wait(e, key, val)
        for w in writes:
            for key, val in list(w.w.items()):
                if key != e:
                    self._wait(e, key, val)
            for key, val in list(w.r.items()):
                if key != e:
                    self._wait(e, key, val)

    def op(self, e, fn, reads=(), writes=()):
        self._deps(e, reads, writes)
        inst = fn()
        self.cnt[e] += 1
        c = self.cnt[e]
        inst.then_inc(self.sem[e], 1)
        self.ninst += 1
        self.hist[(e, c)] = dict(self.seen[e])
        for r in reads:
            r.r[e] = c
        for w in writes:
            w.w = {e: c}
            w.r = {}
        return inst

    def dma(self, q, out, in_, reads=(), writes=(), **kw):
        ring = self.rings[q]
        i = ring["i"]
        n = len(ring["sems"])
        slot = i % n
        val = 16 * (i // n + 1)
        key = ("d", q, slot)
        if val > 16:
            self._wait(q, key, val - 16)
        for r in reads:
            for k2, v2 in list(r.w.items()):
                self._wait(q, k2, v2)
        for w in writes:
            for k2, v2 in list(w.w.items()):
                if k2 != q and not (isinstance(k2, tuple) and k2[1] == q):
                    self._wait(q, k2, v2)
            for k2, v2 in list(w.r.items()):
                if k2 != q:
                    self._wait(q, k2, v2)
        inst = self.engs[q].dma_start(out=out, in_=in_, **kw)
        inst.then_inc(ring["sems"][slot], 16)
        self.ninst += 1
        ring["i"] += 1
        self.hist[(key, val)] = dict(self.seen[q])
        self.last[key] = val
        for r in reads:
            r.r[key] = val
        for w in writes:
            w.w[key] = val
            w.r = {}

    def barrier(self):
        targets = [(k, self.cnt[k]) for k in self.sem if self.cnt[k] > 0] + list(self.last.items())
        for e in self.engs:
            for key, val in targets:
                if key == e:
                    continue
                self._wait(e, key, val)

    def mm(self, out, lhsT, rhs, start, stop, reads, writes):
        return self.op("pe", lambda: self.nc.tensor.matmul(out, lhsT, rhs, start=start, stop=stop), reads, writes)

    def tr(self, out, in_, ident, reads, writes):
        return self.op("pe", lambda: self.nc.tensor.transpose(out, in_, ident), reads, writes)

    def act(self, out, in_, func, reads, writes, **kw):
        return self.op("act", lambda: self.nc.scalar.activation(out=out, in_=in_, func=func, **kw), reads, writes)

    def evac(self, out, in_, reads, writes, scale=None, eng=None):
        if eng is None:
            self.flip ^= 1
            eng = "act" if self.flip else "dve"
        if eng == "act":
            if scale is None:
                return self.op("act", lambda: self.nc.scalar.copy(out=out, in_=in_), reads, writes)
            return self.op("act", lambda: self.nc.scalar.mul(out=out, in_=in_, mul=scale), reads, writes)
        if scale is None:
            return self.op("dve", lambda: self.nc.vector.tensor_copy(out=out, in_=in_), reads, writes)
        return self.op("dve", lambda: self.nc.vector.tensor_scalar(out=out, in0=in_, scalar1=scale, scalar2=None,
                                                                   op0=ALU.mult), reads, writes)

    def dve(self, fn, reads, writes):
        return self.op("dve", fn, reads, writes)


class Phase:
    uid = 0

    def __init__(self, ctx):
        self.ctx = ctx
        self.st = contextlib.ExitStack()

    def __enter__(self):
        self.st.__enter__()
        return self

    def __exit__(self, *a):
        self.ctx.barrier()
        return self.st.__exit__(*a)

    def sb(self, name, shape, dt):
        Phase.uid += 1
        return self.st.enter_context(self.ctx.nc.sbuf_tensor(f"{name}_u{Phase.uid}", list(shape), dt))


def dram_rows(t, r0, nr, ncols, c0=0, rowlen=None):
    return t.ap()[r0:r0 + nr, c0:c0 + ncols]


def build(nc, upto=99, dbg=False):
    st = contextlib.ExitStack()
    with st:
        _build(nc, st, upto, dbg)
    return nc


def _build(nc, st, upto, dbg):
    C = Ctx(nc, st)
    ein = lambda name, shape: nc.dram_tensor(name, list(shape), F32, kind="ExternalInput")
    x = ein("x", [S, D])
    rel_bias = ein("rel_bias", [32, 16])
    mix_norm = ein("mix_norm", [2, D])
    mlp_norm = ein("mlp_norm", [2, D])
    even_w_in = ein("even_w_in", [D, EVEN_IN])
    even_w_out = ein("even_w_out", [D, D])
    cmp_pos_k = ein("cmp_pos_k", [32, 64])
    cmp_pos_v = ein("cmp_pos_v", [32, 64])
    cmp_k_w1 = ein("cmp_k_w1", [2048, 256])
    cmp_k_w2 = ein("cmp_k_w2", [256, 64])
    cmp_v_w1 = ein("cmp_v_w1", [2048, 256])
    cmp_v_w2 = ein("cmp_v_w2", [256, 64])
    odd_w_in = ein("odd_w_in", [D, ODD_IN])
    odd_b_forget = ein("odd_b_forget", [16, 1])
    odd_w_out = ein("odd_w_out", [D, D])
    mlp_w1 = ein("mlp_w1", [2 * D, DFF])
    mlp_w2 = ein("mlp_w2", [2 * DFF, D])
    final_norm = ein("final_norm", [1, D])
    c_ident = ein("c_ident", [128, 128])
    c_ohc = ein("c_ohc", [33, LS])
    c_ohw = ein("c_ohw", [33, LS])
    c_ohcmp = ein("c_ohcmp", [33, LC])
    c_cm16 = ein("c_cm16", [128, 512])
    c_own16 = ein("c_own16", [128, 512])
    c_force = ein("c_force", [128, 2048])
    c_overlap = ein("c_overlap", [256, 64])
    c_ohb64 = ein("c_ohb64", [64, S])
    c_ohb16 = ein("c_ohb16", [16, S])
    c_tri = ein("c_tri", [128, 128])
    c_ones = ein("c_ones", [128, S])

    out = nc.dram_tensor("out", [S, D], F32, kind="ExternalOutput")
    dbgset = dbg if isinstance(dbg, (set, list, tuple)) else None
    def scr(name, shape, dt):
        ext = (name in dbgset) if dbgset is not None else bool(dbg)
        return nc.dram_tensor(name, list(shape), dt, kind="ExternalOutput" if ext else "Internal")
    FT0 = scr("FT0", [16 * 128, S], BF16)
    TM0 = scr("TM0", [S, 768], BF16)
    STRC = scr("STRC", [16 * 128, LS], BF16)
    STRW = scr("STRW", [8 * 128, LS], BF16)
    STRK = scr("STRK", [8 * 128, LC], BF16)
    PENM = scr("PENM", [8 * 16, S], BF16)
    PENS = scr("PENS", [2 * 64, S], BF16)
    KCMP = scr("KCMP", [2 * 64, 256], BF16)
    VCMP = scr("VCMP", [2 * 256, 64], BF16)
    OATT = scr("OATT", [S, D], BF16)
    H1 = scr("H1", [S, D], F32)
    H2 = scr("H2", [S, D], F32)
    FT1 = scr("FT1", [16 * 128, S], BF16)
    TM1 = scr("TM1", [S, D], BF16)
    CUMP = scr("CUMP", [6 * 16, S], BF16)
    H3 = scr("H3", [S, D], F32)
    B_OHB64 = scr("B_OHB64", [64, S], BF16)
    B_OHB16 = scr("B_OHB16", [16, S], BF16)
    B_ONES = scr("B_ONES", [8, S], BF16)
    B_OVL = scr("B_OVL", [256, 64], BF16)

    gsb = lambda name, shape, dt: st.enter_context(nc.sbuf_tensor(name, list(shape), dt))
    banks = [st.enter_context(nc.psum_tensor(f"bank{i}", [128, 512], F32)) for i in range(8)]
    bres = RL(8)
    identf = gsb("identf", [128, 128], F32)
    identb = gsb("identb", [128, 128], BF16)
    gates = gsb("gates", [128, NT, 24], F32)
    r_const = Res()
    r_gates = Res()

    C.dma("sp", identf[:], c_ident.ap(), writes=[r_const])
    C.dve(lambda: nc.vector.tensor_copy(out=identb[:], in_=identf[:]), [r_const], [r_const])
    C.dma("pool", B_OHB64.ap(), c_ohb64.ap())
    C.dma("pool", B_OHB16.ap(), c_ohb16.ap())
    C.dma("pool", B_ONES.ap(), c_ones.ap()[0:8, :])
    C.dma("pool", B_OVL.ap(), c_overlap.ap())

    def bankbf(i):
        return banks[i][:, :].bitcast(BF16)

    def load_w_bf16(ph, name, src, r0, nk, ncols):
        w = ph.sb(name, [128, nk, ncols], BF16)
        res = Res()
        issue_w(w, src, r0, nk, ncols, res)
        return w, res

    def issue_w(w, src, r0, nk, ncols, res):
        for k in range(nk):
            c0 = 0
            while c0 < ncols:
                c1 = min(ncols, c0 + 2048)
                C.dma("pool", w[:, k, c0:c1], src.ap()[r0 + k * 128:r0 + (k + 1) * 128, c0:c1], writes=[res])
                c0 = c1
        return w, res

    def bcast_row(ph, name, src, row):
        g = ph.sb(name, [128, D], F32)
        res = Res()
        C.dma("sp", g[:], bass.AP(src, row * D, [[0, 128], [1, D]]), writes=[res])
        return g, res

    def rmsnorm_T(ph, tag, xt, r_x, g, r_g, hn, r_hn, small, r_small, hnT_dst, r_dst, bank_i):
        junk, ss, rstd = small
        C.dve(lambda: nc.vector.scalar_tensor_tensor(out=junk[:], in0=xt, scalar=1.0, in1=xt, op0=ALU.mult,
                                                     op1=ALU.mult, accum_out=ss[:]), [r_x], [r_small])
        C.dve(lambda: nc.vector.tensor_scalar(out=rstd[:], in0=ss[:], scalar1=1.0 / D, scalar2=EPS, op0=ALU.mult,
                                              op1=ALU.add), [r_small], [r_small])
        C.act(rstd[:], rstd[:], AF.Sqrt, [r_small], [r_small])
        C.dve(lambda: nc.vector.reciprocal(out=rstd[:], in_=rstd[:]), [r_small], [r_small])
        C.dve(lambda: nc.vector.scalar_tensor_tensor(out=hn[:], in0=xt, scalar=rstd[:, 0:1], in1=g[:], op0=ALU.mult,
                                                     op1=ALU.mult), [r_x, r_small, r_g], [r_hn])
        bb = bankbf(bank_i)
        for k in range(8):
            C.tr(bb[:, k * 128:(k + 1) * 128], hn[:, k * 128:(k + 1) * 128], identb[:], [r_hn, r_const],
                 [bres[bank_i]])
        C.evac(hnT_dst, bb.rearrange("p (k t) -> p k t", k=8), [bres[bank_i]], [r_dst])

    if upto >= 0:
        with Phase(C) as ph:
            tba = ph.sb("tba", [65, 16], F32)
            tbh = ph.sb("tbh", [65, 16], BF16)
            tbrep = ph.sb("tbrep", [65, 16, 128], BF16)
            ones65 = ph.sb("ones65", [65, 128], BF16)
            ohc = ph.sb("ohc", [65, LS], BF16)
            ohw = ph.sb("ohw", [65, LS], BF16)
            ohk = ph.sb("ohk", [65, LC], BF16)
            r0 = Res()
            rt = Res()
            C.dve(lambda: nc.vector.memset(tbh[:], -PEN), [], [rt])
            C.dve(lambda: nc.vector.memset(ones65[:], 1.0), [], [rt])
            C.dma("sp", tba[0:32, :], rel_bias.ap(), writes=[rt])
            C.dma("sp", tba[32:64, :], rel_bias.ap(), writes=[rt])
            for (dst_t, src_t, L) in [(ohc, c_ohc, LS), (ohw, c_ohw, LS), (ohk, c_ohcmp, LC)]:
                c0 = 0
                while c0 < L:
                    c1 = min(L, c0 + 2048)
                    C.dma("pool", dst_t[0:32, c0:c1], src_t.ap()[0:32, c0:c1], writes=[r0])
                    C.dma("pool", dst_t[32:64, c0:c1], src_t.ap()[0:32, c0:c1], writes=[r0])
                    C.dma("pool", dst_t[64:65, c0:c1], src_t.ap()[32:33, c0:c1], writes=[r0])
                    c0 = c1
            C.dve(lambda: nc.vector.tensor_copy(out=tbh[0:64, :], in_=tba[0:64, :]), [rt], [rt])
            C.dve(lambda: nc.vector.tensor_tensor(out=tba[32:64, :], in0=tba[32:64, :], in1=tbh[32:64, :],
                                                  op=ALU.subtract), [rt], [rt])
            C.dve(lambda: nc.vector.tensor_copy(out=tbh[32:64, :], in_=tba[32:64, :]), [rt], [rt])
            r1 = Res()
            for h in range(16):
                C.dve(lambda h=h: nc.vector.tensor_scalar(out=tbrep[:, h, :], in0=ones65[:], scalar1=tbh[:, h:h + 1],
                                                          scalar2=None, op0=ALU.mult), [rt], [r1])
            stg = [ph.sb(f"stg{i}", [128, LC], BF16) for i in range(2)]
            rstg = RL(2)
            jobs = [(ohc, LS, h, STRC, h) for h in range(16)]
            jobs += [(ohw, LS, 8 + hn, STRW, hn) for hn in range(8)]
            jobs += [(ohk, LC, 8 + hn, STRK, hn) for hn in range(8)]
            bi = 0
            for ji, (oh, L, trow, dst, di) in enumerate(jobs):
                sg, rs = stg[ji % 2], rstg[ji % 2]
                c0 = 0
                while c0 < L:
                    c1 = min(L, c0 + 512)
                    b = bi % 4
                    bi += 1
                    C.mm(banks[b][:, 0:c1 - c0], tbrep[:, trow, :], oh[:, c0:c1], True, True, [r0, r1], [bres[b]])
                    C.act(sg[:, c0:c1], banks[b][:, 0:c1 - c0], AF.Exp, [bres[b]], [rs])
                    c0 = c1
                C.dma("pool", dst.ap()[di * 128:(di + 1) * 128, :], sg[:, 0:L], reads=[rs])
    if upto < 1:
        return _finish(C, nc)

    def proj_phase(layer, src_h, w_in_t, ncols, gsrc, fm_list, tm_groups, fm_special, tm_handler, pre=None):
        with Phase(C) as ph:
            win, r_win = load_w_bf16(ph, "win", w_in_t, 0, 8, ncols)
            g, r_g = bcast_row(ph, "gmix", gsrc, layer)
            xts = [ph.sb(f"xt{i}", [128, D], F32) for i in range(3)]
            r_xt = RL(3)
            hns = [ph.sb(f"hn{i}", [128, D], BF16) for i in range(2)]
            r_hn = RL(2)
            junk = ph.sb("junk", [128, D], BF16)
            smalls = [(junk, ph.sb(f"ss{i}", [128, 1], F32), ph.sb(f"rstd{i}", [128, 1], F32)) for i in range(2)]
            r_sm = RL(2)
            hnT = [ph.sb(f"hnT{i}", [128, 8, 512], BF16) for i in range(2)]
            r_hnT = [RL(4) for _ in range(2)]
            fsb = [ph.sb(f"fsb{i}", [128, 512], BF16) for i in range(4)]
            r_fsb = RL(4)
            tsb = [ph.sb(f"tsb{i}", [128, 1024], BF16) for i in range(2)]
            r_tsb = RL(2)
            env = dict(ph=ph)
            if pre is not None:
                pre(env)

            def ld(tt):
                i = tt % 3
                C.dma("sp", xts[i][:], src_h.ap()[tt * 128:(tt + 1) * 128, :], writes=[r_xt[i]])

            ld(0)
            ld(1)
            fi = 0
            for c in range(8):
                hb = c % 2
                for i in range(4):
                    tt = 4 * c + i
                    if tt + 2 < NT:
                        ld(tt + 2)
                    rmsnorm_T(ph, "p", xts[tt % 3][:], r_xt[tt % 3], g, r_g, hns[tt % 2], r_hn[tt % 2],
                              smalls[tt % 2], r_sm[tt % 2], hnT[hb][:, :, i * 128:(i + 1) * 128], r_hnT[hb][i],
                              6 + (tt % 2))
                for m, (col0, width, scale, dst) in enumerate(fm_list):
                    b = m % 3
                    for k in range(8):
                        C.mm(banks[b][0:width, :], win[:, k, col0:col0 + width], hnT[hb][:, k, :], k == 0, k == 7,
                             [r_win] + r_hnT[hb], [bres[b]])
                    if dst is None:
                        fm_special(env, c, banks[b], bres[b])
                        continue
                    f = fi % 4
                    fi += 1
                    C.evac(fsb[f][0:width, :], banks[b][0:width, :], [bres[b]], [r_fsb[f]], scale=scale)
                    C.dma("pool", dst[:, c * 512:(c + 1) * 512], fsb[f][0:width, :], reads=[r_fsb[f]])
                for i in range(4):
                    tt = 4 * c + i
                    for gi, grp in enumerate(tm_groups):
                        b = 3 + (2 * i + gi) % 3
                        o = 0
                        for (col0, width) in grp:
                            for k in range(8):
                                C.mm(banks[b][:, o:o + width], hnT[hb][:, k, i * 128:(i + 1) * 128],
                                     win[:, k, col0:col0 + width], k == 0, k == 7, [r_win, r_hnT[hb][i]], [bres[b]])
                            o += width
                        tm_handler(env, tt, gi, banks[b], bres[b], tsb[tt % 2], r_tsb[tt % 2])

    fm0_cols = [0, 128, 256, 384, 512, 640, 768, 896, 1536, 1664, 1792, 1920, 2048, 2176, 2304, 2560]
    fm0 = []
    for m, col0 in enumerate(fm0_cols):
        isq = m < 4 or 8 <= m < 12
        fm0.append((col0, 128, 0.125 if isq else None, FT0.ap()[m * 128:(m + 1) * 128, :]))
    tm0_groups = [[(1024, 512)], [(2432, 128), (2688, 128), (2816, 24)]]
    gtmp = gsb("gtmp", [128, 24], F32)
    r_gtmp = Res()

    def tm0_handler(env, tt, gi, bank, rb, tsb, r_tsb):
        if gi == 0:
            C.evac(tsb[:, 0:512], bank[:, 0:512], [rb], [r_tsb])
        else:
            C.evac(tsb[:, 512:768], bank[:, 0:256], [rb], [r_tsb])
            C.act(gates[:, tt, :], bank[:, 256:280], AF.Sigmoid, [rb], [r_gates])
            C.dma("pool", TM0.ap()[tt * 128:(tt + 1) * 128, :], tsb[:, 0:768], reads=[r_tsb])

    proj_phase(0, x, even_w_in, EVEN_IN, mix_norm, fm0, tm0_groups, None, tm0_handler)
    if upto < 2:
        return _finish(C, nc)

    def alloc_attn(ph, nkmax, vwmax, wlen):
        sets = []
        for i in range(2):
            sets.append(dict(QA=ph.sb(f"QA{i}", [128, S], BF16), KA=ph.sb(f"KA{i}", [128, nkmax], BF16),
                             VA=ph.sb(f"VA{i}", [128, nkmax // 128, vwmax], BF16),
                             W=ph.sb(f"W{i}", [128, wlen], BF16), r=Res(),
                             fb=ph.sb(f"fb{i}", [128, 2], F32), rfb=Res()))
        A = dict(sets=sets, P=[ph.sb(f"P{i}", [128, 512], BF16) for i in range(6)], rP=RL(6), pi=0,
                 den=[ph.sb(f"den{i}", [128, 2], F32) for i in range(8)], rden=RL(8), di=0, defer=[], mi=0)
        return A

    def load_job(aset, job):
        r = aset["r"]
        deps = job.get("deps", [])
        for (row0, nrows, src) in job["qa"]:
            C.dma("sp", aset["QA"][row0:row0 + nrows, :], src, reads=deps, writes=[r])
        for (row0, nrows, ncols, src) in job["ka"]:
            C.dma("sp", aset["KA"][row0:row0 + nrows, 0:ncols], src, reads=deps, writes=[r])
        vsrc, nkt, vcols = job["va"]
        C.dma("sp", aset["VA"][:, 0:nkt, 0:vcols], vsrc.rearrange("(t p) c -> p t c", p=128), reads=deps, writes=[r])
        for (col0, ncols, tensor, off, pstep) in job["w"]:
            C.dma("sp", aset["W"][:, col0:col0 + ncols], bass.AP(tensor, off, [[pstep, 128], [1, ncols]]), writes=[r])

    def compute_job(A, aset, job):
        KQ, VW = job["KQ"], job["VW"]
        r = aset["r"]
        QA, KA, VA = aset["QA"], aset["KA"], aset["VA"]
        P, rP = A["P"], A["rP"]
        if job.get("far"):
            C.act(aset["fb"][:, 0:1], aset["W"][:, 4479:4480], AF.Ln, [r], [aset["rfb"]])
        for qc in range(8):
            tiles = job["tiles"](qc)
            first, last = {}, {}
            for idx, (kt, jlo, jhi) in enumerate(tiles):
                for j in range(jlo, jhi + 1):
                    first.setdefault(j, idx)
                    last[j] = idx

            def emit_pv(pd):
                idx, kt, jlo, jhi, p = pd
                for j in range(jlo, jhi + 1):
                    C.mm(banks[3 + j][:, 0:VW], P[p][:, j * 128:(j + 1) * 128], VA[:, kt, 0:VW], first[j] == idx,
                         last[j] == idx, [rP[p], r], [bres[3 + j]])

            pend = []
            first_pv = [True]

            def do_pv(pd):
                if first_pv[0]:
                    first_pv[0] = False
                    for f in A["defer"]:
                        f()
                    A["defer"] = []
                emit_pv(pd)

            for idx, (kt, jlo, jhi) in enumerate(tiles):
                sbi = idx % 3
                c0, c1 = jlo * 128, (jhi + 1) * 128
                C.mm(banks[sbi][:, c0:c1], KA[0:KQ, kt * 128:(kt + 1) * 128], QA[0:KQ, qc * 512 + c0:qc * 512 + c1],
                     True, True, [r], [bres[sbi]])
                p = A["pi"] % 6
                A["pi"] += 1
                if job.get("far") and 512 * qc - 128 * kt >= 1152:
                    C.act(P[p][:, c0:c1], banks[sbi][:, c0:c1], AF.Exp, [bres[sbi], aset["rfb"]], [rP[p]],
                          bias=aset["fb"][:, 0:1])
                else:
                    C.act(P[p][:, c0:c1], banks[sbi][:, c0:c1], AF.Exp, [bres[sbi]], [rP[p]])
                    for (a0, a1, wap, rw) in job["wmul"](aset, kt, qc, jlo, jhi):
                        A["mi"] += 1
                        if job.get("pool_share") and A["mi"] % 3 == 0:
                            C.op("pool", lambda a0=a0, a1=a1, wap=wap, p=p: nc.gpsimd.tensor_tensor(
                                out=P[p][:, a0:a1], in0=P[p][:, a0:a1], in1=wap, op=ALU.mult), [rP[p], rw], [rP[p]])
                        else:
                            C.dve(lambda a0=a0, a1=a1, wap=wap, p=p: nc.vector.tensor_tensor(
                                out=P[p][:, a0:a1], in0=P[p][:, a0:a1], in1=wap, op=ALU.mult), [rP[p], rw], [rP[p]])
                if len(pend) >= 2:
                    do_pv(pend.pop(0))
                pend.append((idx, kt, jlo, jhi, p))
            for pd in pend:
                do_pv(pd)
            A["defer"].append(lambda qc=qc: job["fin"](A, [(qc * 4 + j, banks[3 + j], bres[3 + j]) for j in range(4)]))
            if job.get("hook"):
                job["hook"](qc)
        for f in A["defer"]:
            f()
        A["defer"] = []

    def run_jobs(A, jobs):
        for i, job in enumerate(jobs):
            if i == 0 or job.get("late"):
                load_job(A["sets"][i % 2], job)
            if i + 1 < len(jobs) and not jobs[i + 1].get("late"):
                load_job(A["sets"][(i + 1) % 2], jobs[i + 1])
            compute_job(A, A["sets"][i % 2], job)
            if job.get("after"):
                job["after"]()

    def rden_multi(A, items, col):
        outs = []
        for (qt, bank, rb) in items:
            d = A["di"] % 8
            A["di"] += 1
            den, rd = A["den"][d], A["rden"][d]
            C.dve(lambda den=den, bank=bank: nc.vector.tensor_scalar(out=den[:, 0:1], in0=bank[:, col:col + 1],
                                                                     scalar1=1e-30, scalar2=None, op0=ALU.max),
                  [rb], [rd])
            outs.append((den, rd))
        for (den, rd) in outs:
            C.dve(lambda den=den: nc.vector.reciprocal(out=den[:, 0:1], in_=den[:, 0:1]), [rd], [rd])
        return outs

    def causal_tiles(qc):
        return [(kt, max(0, kt - 4 * qc), 3) for kt in range(4 * qc + 4)]

    def window_tiles(qc):
        out_ = []
        for kt in range(max(0, 4 * qc - 4), 4 * qc + 4):
            rel = kt - 4 * qc
            out_.append((kt, max(0, rel), min(3, rel + 4)))
        return out_

    def strip_wmul(aset, kt, qc, jlo, jhi):
        j0 = 512 * qc - 128 * kt + 384
        c0, c1 = jlo * 128, (jhi + 1) * 128
        return [(c0, c1, aset["W"][:, j0 + c0:j0 + c1], aset["r"])]

    def ft_rows(FT, row0, n=64):
        return FT.ap()[row0:row0 + n, :]

    def compress_setup(ph):
        w1s = ph.sb("w1s", [64, 32, 256], BF16)
        w2s = ph.sb("w2s", [128, 2, 64], BF16)
        posf = ph.sb("posf", [64, 32], F32)
        pos2 = ph.sb("pos2", [64, 32, 2], BF16)
        rawT = ph.sb("rawT", [64, S], BF16)
        cb = ph.sb("cb", [128, 2], F32)
        HT = ph.sb("HT", [128, 2, 256], BF16)
        kcs = ph.sb("kcs", [64, 256], BF16)
        vcs = ph.sb("vcs", [128, 2, 64], BF16)
        r_w, r_raw, r_cb, r_HT, r_o = Res(), Res(), Res(), Res(), Res()
        C.dve(lambda: nc.vector.memset(HT[:], 0.0), [], [r_HT])

        def loads(piece):
            kv, g = piece // 2, piece % 2
            if g == 0:
                w1src = [cmp_k_w1, cmp_v_w1][kv]
                w2src = [cmp_k_w2, cmp_v_w2][kv]
                possrc = [cmp_pos_k, cmp_pos_v][kv]
                for l0 in range(0, 32, 8):
                    C.dma("pool", w1s[:, l0:l0 + 8, :],
                          w1src.ap()[l0 * 64:(l0 + 8) * 64, :].rearrange("(l d) h -> d l h", d=64), writes=[r_w])
                C.dma("pool", w2s[:], w2src.ap().rearrange("(t p) c -> p t c", p=128), writes=[r_w])
                C.dma("sp", posf[:], possrc.ap().rearrange("l d -> d l"), writes=[r_w], allow_slow_non_contiguous=True)
            C.dma("sp", rawT[:], ft_rows(FT0, (12 + kv) * 128 + g * 64), writes=[r_raw])

        def compute(piece):
            kv, g = piece // 2, piece % 2
            if g == 0:
                for u in range(2):
                    C.dve(lambda u=u: nc.vector.tensor_copy(out=pos2[:, :, u], in_=posf[:]), [r_w], [r_w])
                for ht in range(2):
                    for l in range(32):
                        C.mm(banks[7][:, ht * 2:ht * 2 + 2], w1s[:, l, ht * 128:(ht + 1) * 128], pos2[:, l, :], l == 0,
                             l == 31, [r_w], [bres[7]])
                C.evac(cb[:, 0:1], banks[7][:, 0:1], [bres[7]], [r_cb], eng="dve")
                C.evac(cb[:, 1:2], banks[7][:, 2:3], [bres[7]], [r_cb], eng="dve")
            rv = rawT[:, :].rearrange("p (n s) -> p n s", s=16)
            for ht in range(2):
                for l in range(32):
                    rhs = rv[:, 0:255, l] if l < 16 else rv[:, 1:256, l - 16]
                    C.mm(banks[7][:, 0:255], w1s[:, l, ht * 128:(ht + 1) * 128], rhs, l == 0, l == 31,
                         [r_w, r_raw], [bres[7]])
                C.act(HT[:, ht, 0:255], banks[7][:, 0:255], AF.Silu, [bres[7], r_cb], [r_HT], bias=cb[:, ht:ht + 1])
            if kv == 0:
                for ht in range(2):
                    C.mm(banks[7][0:64, 0:256], w2s[:, ht, :], HT[:, ht, :], ht == 0, ht == 1, [r_w, r_HT], [bres[7]])
                C.evac(kcs[:], banks[7][0:64, 0:256], [bres[7]], [r_o])
                C.dma("pool", KCMP.ap()[g * 64:(g + 1) * 64, :], kcs[:], reads=[r_o])
            else:
                for nt in range(2):
                    for ht in range(2):
                        C.mm(banks[7][:, 0:64], HT[:, ht, nt * 128:(nt + 1) * 128], w2s[:, ht, :], ht == 0, ht == 1,
                             [r_w, r_HT], [bres[7]])
                    C.evac(vcs[:, nt, :], banks[7][:, 0:64], [bres[7]], [r_o])
                C.dma("pool", VCMP.ap()[g * 256:(g + 1) * 256, :].rearrange("(t p) c -> p t c", p=128), vcs[:],
                      reads=[r_o])

        return loads, compute

    if upto >= 2:
        with Phase(C) as ph:
            A = alloc_attn(ph, S, 65, 4480)
            for aset in A["sets"]:
                C.dve(lambda aset=aset: nc.vector.memset(aset["VA"][:, :, 64:65], 1.0), [], [aset["r"]])
            cm = ph.sb("cm", [128, 512], F32)
            own = ph.sb("own", [128, 512], F32)
            r_c2 = Res()
            C.dma("sp", cm[:], c_cm16.ap(), writes=[r_c2])
            C.dma("sp", own[:], c_own16.ap(), writes=[r_c2])
            kTp = [ph.sb(f"kTp{i}", [64, S], BF16) for i in range(2)]
            qTp = [ph.sb(f"qTp{i}", [64, S], BF16) for i in range(2)]
            r_kq = RL(2)
            kmf = ph.sb("kmf", [64, 16], F32)
            kmb = ph.sb("kmb", [64, 16], BF16)
            rm = ph.sb("rm", [128, 512], F32)
            sel = ph.sb("sel", [128, 512], F32)
            penf = ph.sb("penf", [128, 512], F32)
            m8 = ph.sb("m8", [128, NT, 8], F32)
            penT = [ph.sb(f"penT{i}", [16, S], BF16) for i in range(2)]
            r_penT = RL(2)
            r_prep = Res()
            r_rm, r_m8, r_selm = Res(), Res(), Res()
            r_penm = RL(8)

            def prep_load(h):
                i = h % 2
                C.dma("sp", kTp[i][:], ft_rows(FT0, (4 + h // 2) * 128 + (h % 2) * 64), writes=[r_kq[i]])
                C.dma("sp", qTp[i][:], ft_rows(FT0, (h // 2) * 128 + (h % 2) * 64), writes=[r_kq[i]])

            prep_load(0)
            for h in range(8):
                i = h % 2
                if h + 1 < 8:
                    prep_load(h + 1)
                C.dve(lambda: nc.vector.tensor_reduce(out=kmf[:], in_=kTp[i][:, :].rearrange("p (n k) -> p n k", k=256),
                                                      axis=AX.X, op=ALU.add), [r_kq[i]], [r_prep])
                C.dve(lambda: nc.vector.tensor_scalar(out=kmb[:], in0=kmf[:], scalar1=1.0 / 256, scalar2=None,
                                                      op0=ALU.mult), [r_prep], [r_prep])
                for qt in range(NT):
                    C.mm(banks[6][:, qt * 16:(qt + 1) * 16], qTp[i][:, qt * 128:(qt + 1) * 128], kmb[:], True, True,
                         [r_kq[i], r_prep], [bres[6]])
                C.dve(lambda: nc.vector.tensor_tensor(out=rm[:], in0=banks[6][:, :], in1=cm[:], op=ALU.add),
                      [bres[6], r_c2], [r_rm])
                for qt in range(NT):
                    C.dve(lambda qt=qt: nc.vector.max(out=m8[:, qt, :], in_=rm[:, qt * 16:(qt + 1) * 16]),
                          [r_rm], [r_m8])
                for qt in range(NT):
                    C.dve(lambda qt=qt: nc.vector.tensor_scalar(out=sel[:, qt * 16:(qt + 1) * 16],
                                                                in0=rm[:, qt * 16:(qt + 1) * 16],
                                                                scalar1=m8[:, qt, 2:3], scalar2=None, op0=ALU.is_ge),
                          [r_rm, r_m8], [r_selm])
                C.dve(lambda: nc.vector.tensor_tensor(out=sel[:], in0=sel[:], in1=own[:], op=ALU.max),
                      [r_selm, r_c2], [r_prep])
                C.dve(lambda: nc.vector.tensor_scalar(out=penf[:], in0=sel[:], scalar1=PEN, scalar2=-PEN, op0=ALU.mult,
                                                      op1=ALU.add), [r_prep], [r_prep])
                for g4 in range(8):
                    for u in range(4):
                        qt = 4 * g4 + u
                        C.tr(banks[7][0:16, u * 128:(u + 1) * 128], penf[:, qt * 16:(qt + 1) * 16], identf[:],
                             [r_prep, r_const], [bres[7]])
                    C.evac(penT[i][:, g4 * 512:(g4 + 1) * 512], banks[7][0:16, :], [bres[7]], [r_penT[i]])
                C.dma("pool", PENM.ap()[h * 16:(h + 1) * 16, :], penT[i][:], reads=[r_penT[i]], writes=[r_penm[h]])

            osb = [ph.sb(f"osb{i}", [128, NT, 64], BF16) for i in range(2)]
            r_osb = RL(2)

            def simple_fin(oi):
                def fin(A, items):
                    dens = rden_multi(A, items, 64)
                    for (qt, bank, rb), (den, rd) in zip(items, dens):
                        C.act(osb[oi][:, qt, :], bank[:, 0:64], AF.Copy, [rb, rd], [r_osb[oi]], scale=den[:, 0:1])
                return fin

            def simple_after(oi, dst, col0):
                def after():
                    C.dma("pool", dst.ap()[:, col0:col0 + 64].rearrange("(t p) c -> p t c", p=128), osb[oi][:],
                          reads=[r_osb[oi]])
                return after

            cmp_loads, cmp_compute = compress_setup(ph)

            def moba_hook(h):
                def hook(qc):
                    if h < 4 and qc == 1:
                        cmp_loads(h)
                    if h < 4 and qc == 6:
                        cmp_compute(h)
                return hook

            jobs = []
            for h in range(8):
                jobs.append(dict(
                    hook=moba_hook(h),
                    qa=[(0, 64, ft_rows(FT0, (h // 2) * 128 + (h % 2) * 64)), (64, 16, PENM.ap()[h * 16:(h + 1) * 16, :])],
                    ka=[(0, 64, S, ft_rows(FT0, (4 + h // 2) * 128 + (h % 2) * 64)), (64, 16, S, B_OHB16.ap())],
                    va=(TM0.ap()[:, h * 64:(h + 1) * 64], NT, 64),
                    w=[(0, 4480, STRC, h * 128 * LS + 128, LS - 1)],
                    deps=[r_penm[h]], KQ=80, VW=65, tiles=causal_tiles, wmul=strip_wmul, far=True, pool_share=True,
                    fin=simple_fin(h % 2), after=simple_after(h % 2, OATT, h * 64)))
            run_jobs(A, jobs)
    if upto < 3:
        return _finish(C, nc)

    if upto < 4:
        return _finish(C, nc)

    with Phase(C) as ph:
        A = alloc_attn(ph, S, 129, 8192)
        for aset in A["sets"]:
            C.dve(lambda aset=aset: nc.vector.memset(aset["VA"][:, :, 64:65], 1.0), [], [aset["r"]])
            C.dma("sp", aset["VA"][:, 0:2, 65:129], B_OVL.ap().rearrange("(t p) c -> p t c", p=128),
                  writes=[aset["r"]])
        oacc = ph.sb("oacc", [128, NT, 512], F32)
        imps = [ph.sb(f"imp{g}", [128, NT, 64], F32) for g in range(2)]
        r_imps = RL(2)
        force = ph.sb("force", [128, NT, 64], F32)
        r_oacc, r_force = Res(), Res()
        C.dma("sp", force[:].rearrange("p t m -> p (t m)"), c_force.ap(), writes=[r_force])
        sg = [ph.sb(f"sg{i}", [128, 2], F32) for i in range(8)]
        r_sg = RL(8)
        sgi = [0]
        r_pens = RL(2)

        def nsa_fin(hn, branch):
            def fin(A, items):
                dens = rden_multi(A, items, 64)
                imp, r_imp = imps[hn // 4], r_imps[hn // 4]
                ks = []
                for (qt, bank, rb), (den, rd) in zip(items, dens):
                    k = sgi[0] % 8
                    sgi[0] += 1
                    ks.append(k)
                    C.dve(lambda k=k, den=den, qt=qt: nc.vector.tensor_tensor(
                        out=sg[k][:, 0:1], in0=den[:, 0:1], in1=gates[:, qt, 3 * hn + branch:3 * hn + branch + 1],
                        op=ALU.mult), [rd, r_gates], [r_sg[k]])
                for (qt, bank, rb), (den, rd), k in zip(items, dens, ks):
                    oslice = oacc[:, qt, hn * 64:(hn + 1) * 64]
                    if branch == 0:
                        C.act(oslice, bank[:, 0:64], AF.Copy, [rb, r_sg[k]], [r_oacc], scale=sg[k][:, 0:1])
                    else:
                        C.dve(lambda oslice=oslice, bank=bank, k=k: nc.vector.scalar_tensor_tensor(
                            out=oslice, in0=bank[:, 0:64], scalar=sg[k][:, 0:1], in1=oslice, op0=ALU.mult,
                            op1=ALU.add), [rb, r_sg[k], r_oacc], [r_oacc])
                if branch == 0:
                    for (qt, bank, rb), (den, rd) in zip(items, dens):
                        if hn % 4 == 0:
                            C.act(imp[:, qt, :], bank[:, 65:129], AF.Copy, [rb, rd], [r_imp], scale=den[:, 0:1])
                        else:
                            C.dve(lambda qt=qt, bank=bank, den=den: nc.vector.scalar_tensor_tensor(
                                out=imp[:, qt, :], in0=bank[:, 65:129], scalar=den[:, 0:1], in1=imp[:, qt, :],
                                op0=ALU.mult, op1=ALU.add), [rb, rd, r_imp], [r_imp])
            return fin

        def cmp_tiles(qc):
            return [(0, 0, 3)] + ([(1, 0, 3)] if qc >= 4 else [])

        def cmp_wmul(aset, kt, qc, jlo, jhi):
            return [(0, 512, aset["W"][:, kt * 4096 + qc * 512:kt * 4096 + (qc + 1) * 512], aset["r"])]

        i2 = ph.sb("i2", [128, 4, 64], F32)
        i3 = ph.sb("i3", [128, 4, 64], F32)
        m8a = ph.sb("m8a", [128, 4, 8], F32)
        m8b = ph.sb("m8b", [128, 4, 8], F32)
        sel2 = ph.sb("sel2", [128, 4, 64], F32)
        pnf = [ph.sb(f"pnf{i}", [128, 4, 64], F32) for i in range(2)]
        r_i2, r_i3, r_m8a, r_m8b, r_s2 = Res(), Res(), Res(), Res(), Res()
        r_pnf = RL(2)
        penTs = ph.sb("penTs", [64, S], BF16)
        r_penTs = Res()

        def sel_piece(g, k):
            pb = k % 2
            C.dve(lambda: nc.vector.tensor_tensor(out=i2[:], in0=imps[g][:, 4 * k:4 * k + 4, :],
                                                  in1=force[:, 4 * k:4 * k + 4, :], op=ALU.add),
                  [r_imps[g], r_force], [r_i2])
            for u in range(4):
                C.dve(lambda u=u: nc.vector.max(out=m8a[:, u, :], in_=i2[:, u, :]), [r_i2], [r_m8a])
            for u in range(4):
                C.dve(lambda u=u: nc.vector.match_replace(out=i3[:, u, :], in_to_replace=m8a[:, u, :],
                                                          in_values=i2[:, u, :], imm_value=-3.0e38),
                      [r_i2, r_m8a], [r_i3])
            for u in range(4):
                C.dve(lambda u=u: nc.vector.max(out=m8b[:, u, :], in_=i3[:, u, :]), [r_i3], [r_m8b])
            for u in range(4):
                C.dve(lambda u=u: nc.vector.tensor_scalar(out=sel2[:, u, :], in0=i2[:, u, :], scalar1=m8b[:, u, 7:8],
                                                          scalar2=None, op0=ALU.is_ge), [r_i2, r_m8b], [r_s2])
            C.dve(lambda: nc.vector.tensor_scalar(out=pnf[pb][:], in0=sel2[:], scalar1=PEN, scalar2=-PEN, op0=ALU.mult,
                                                  op1=ALU.add), [r_s2], [r_pnf[pb]])
            for u in range(4):
                C.tr(banks[7][0:64, u * 128:(u + 1) * 128], pnf[pb][:, u, :], identf[:], [r_pnf[pb], r_const],
                     [bres[7]])
            C.evac(penTs[:, k * 512:(k + 1) * 512], banks[7][0:64, :], [bres[7]], [r_penTs])
            if k == 7:
                C.dma("pool", PENS.ap()[g * 64:(g + 1) * 64, :], penTs[:], reads=[r_penTs], writes=[r_pens[g]])

        def win_hook(g, j):
            def hook(qc):
                if qc == 3:
                    sel_piece(g, 2 * j)
                elif qc == 7:
                    sel_piece(g, 2 * j + 1)
            return hook

        jobs_cmp, jobs_win, jobs_slc = [], [], []
        for g in range(2):
            for j in range(4):
                hn = 4 * g + j
                qrows = ft_rows(FT0, (8 + hn // 2) * 128 + (hn % 2) * 64)
                jobs_cmp.append(dict(
                    qa=[(0, 64, qrows)], ka=[(0, 64, 256, KCMP.ap()[g * 64:(g + 1) * 64, :])],
                    va=(VCMP.ap()[g * 256:(g + 1) * 256, :], 2, 64),
                    w=[(nt * 4096, 4096, STRK, hn * 128 * LC + 4081 - 2048 * nt, LC - 16) for nt in range(2)],
                    KQ=64, VW=129, tiles=cmp_tiles, wmul=cmp_wmul, fin=nsa_fin(hn, 0)))
                jobs_win.append(dict(
                    qa=[(0, 64, qrows)], ka=[(0, 64, S, ft_rows(FT0, 15 * 128 + g * 64))],
                    va=(TM0.ap()[:, 640 + g * 64:640 + (g + 1) * 64], NT, 64),
                    w=[(0, 4480, STRW, hn * 128 * LS + 128, LS - 1)],
                    KQ=64, VW=65, tiles=window_tiles, wmul=strip_wmul, fin=nsa_fin(hn, 2), hook=win_hook(g, j),
                    pool_share=True))
                jobs_slc.append(dict(
                    qa=[(0, 64, qrows), (64, 64, PENS.ap()[g * 64:(g + 1) * 64, :])],
                    ka=[(0, 64, S, ft_rows(FT0, 14 * 128 + g * 64)), (64, 64, S, B_OHB64.ap())],
                    va=(TM0.ap()[:, 512 + g * 64:512 + (g + 1) * 64], NT, 64),
                    w=[(0, 4480, STRC, (8 + hn) * 128 * LS + 128, LS - 1)],
                    deps=[r_pens[g]], KQ=128, VW=65, tiles=causal_tiles, wmul=strip_wmul, fin=nsa_fin(hn, 1),
                    far=True, pool_share=True))
        run_jobs(A, jobs_cmp + jobs_win + jobs_slc)
        ob = [ph.sb(f"ob{i}", [128, 4, 512], BF16) for i in range(2)]
        r_ob = RL(2)
        for q8 in range(8):
            i = q8 % 2
            C.evac(ob[i][:], oacc[:, q8 * 4:(q8 + 1) * 4, :], [r_oacc], [r_ob[i]])
            C.dma("pool", OATT.ap()[q8 * 512:(q8 + 1) * 512, 512:1024].rearrange("(t p) c -> p t c", p=128), ob[i][:],
                  reads=[r_ob[i]])
    if upto < 5:
        return _finish(C, nc)

    def outproj_phase(wsrc, hsrc, hdst, after_wload=None):
        with Phase(C) as ph:
            wout, r_wout = load_w_bf16(ph, "wout", wsrc, 0, 8, D)
            if after_wload is not None:
                after_wload()
            ot = [ph.sb(f"ot{i}", [128, D], BF16) for i in range(2)]
            ht = [ph.sb(f"ht{i}", [128, D], F32) for i in range(3)]
            oT = [ph.sb(f"oT{i}", [128, 8, 128], BF16) for i in range(2)]
            r_ot, r_ht, r_oT = RL(2), RL(3), RL(2)

            def ld(tt):
                C.dma("sp", ot[tt % 2][:], OATT.ap()[tt * 128:(tt + 1) * 128, :], writes=[r_ot[tt % 2]])
                C.dma("sp", ht[tt % 3][:], hsrc.ap()[tt * 128:(tt + 1) * 128, :], writes=[r_ht[tt % 3]])

            ld(0)
            for tt in range(NT):
                if tt + 1 < NT:
                    ld(tt + 1)
                i = tt % 2
                bb = bankbf(6 + i)
                for k in range(8):
                    C.tr(bb[:, k * 128:(k + 1) * 128], ot[i][:, k * 128:(k + 1) * 128], identb[:], [r_ot[i], r_const],
                         [bres[6 + i]])
                C.evac(oT[i][:], bb.rearrange("p (k t) -> p k t", k=8), [bres[6 + i]], [r_oT[i]])
                for half in range(2):
                    b = 2 * i + half
                    for k in range(8):
                        C.mm(banks[b][:, :], oT[i][:, k, :], wout[:, k, half * 512:(half + 1) * 512], k == 0, k == 7,
                             [r_oT[i], r_wout], [bres[b]])
                    hs = ht[tt % 3][:, half * 512:(half + 1) * 512]
                    C.dve(lambda hs=hs, b=b: nc.vector.tensor_tensor(out=hs, in0=banks[b][:, :], in1=hs, op=ALU.add),
                          [bres[b], r_ht[tt % 3]], [r_ht[tt % 3]])
                C.dma("sp", hdst.ap()[tt * 128:(tt + 1) * 128, :], ht[tt % 3][:], reads=[r_ht[tt % 3]])

    def outproj_mlp(wsrc, hsrc, hmid, layer, hdst, final):
        with Phase(C) as ph:
            w1 = ph.sb("w1", [128, 8, DFF], BF16)
            w2 = ph.sb("w2", [128, 32, D], BF16)
            r_w1, r_w2 = Res(), Res()

            def issue():
                issue_w(w1, mlp_w1, layer * D, 8, DFF, r_w1)
                issue_w(w2, mlp_w2, layer * DFF, 32, D, r_w2)

            outproj_phase(wsrc, hsrc, hmid, after_wload=issue)
            mlp_body(ph, layer, hmid, hdst, final, w1, r_w1, w2, r_w2)

    def mlp_body(ph, layer, hsrc, hdst, final, w1, r_w1, w2, r_w2):
        if True:
            g, r_g = bcast_row(ph, "gmlp", mlp_norm, layer)
            if final:
                gf, r_gf = bcast_row(ph, "gfin", final_norm, 0)
            ht = [ph.sb(f"mh{i}", [128, D], F32) for i in range(3)]
            r_ht = RL(3)
            hn = [ph.sb(f"mhn{i}", [128, D], BF16) for i in range(2)]
            r_hn = RL(2)
            junk = ph.sb("mjunk", [128, D], BF16)
            smalls = [(junk, ph.sb(f"mss{i}", [128, 1], F32), ph.sb(f"mrs{i}", [128, 1], F32)) for i in range(2)]
            r_sm = RL(2)
            hnT = [ph.sb(f"mhnT{i}", [128, 8, 128], BF16) for i in range(2)]
            r_hnT = RL(2)
            aT = ph.sb("aT", [128, 32, 128], BF16)
            r_aT = RL(32)
            rl = [ph.sb(f"rl{i}", [128, 128], F32) for i in range(4)]
            r_rl = RL(4)
            if final:
                fo = [ph.sb(f"fo{i}", [128, D], F32) for i in range(2)]
                r_fo = RL(2)
                fss = [ph.sb(f"fss{i}", [128, 2], F32) for i in range(2)]
                r_fss = RL(2)

            def ld(tt):
                C.dma("sp", ht[tt % 3][:], hsrc.ap()[tt * 128:(tt + 1) * 128, :], writes=[r_ht[tt % 3]])

            ld(0)
            ld(1)
            for tt in range(NT):
                if tt + 2 < NT:
                    ld(tt + 2)
                i = tt % 2
                h3 = tt % 3
                rmsnorm_T(ph, "m", ht[h3][:], r_ht[h3], g, r_g, hn[i], r_hn[i], smalls[i], r_sm[i], hnT[i][:],
                          r_hnT[i], 6 + i)
                for f in range(32):
                    b = f % 4
                    for k in range(8):
                        C.mm(banks[b][:, 0:128], w1[:, k, f * 128:(f + 1) * 128], hnT[i][:, k, :], k == 0, k == 7,
                             [r_w1, r_hnT[i]], [bres[b]])
                    C.act(rl[b][:], banks[b][:, 0:128], AF.Relu, [bres[b]], [r_rl[b]])
                    C.dve(lambda f=f, b=b: nc.vector.tensor_tensor(out=aT[:, f, :], in0=rl[b][:], in1=rl[b][:],
                                                                   op=ALU.mult), [r_rl[b]], [r_aT[f]])
                for half in range(2):
                    b = 4 + half
                    for f in range(32):
                        C.mm(banks[b][:, :], aT[:, f, :], w2[:, f, half * 512:(half + 1) * 512], f == 0, f == 31,
                             [r_aT[f], r_w2], [bres[b]])
                    hs = ht[h3][:, half * 512:(half + 1) * 512]
                    C.dve(lambda hs=hs, b=b: nc.vector.tensor_tensor(out=hs, in0=banks[b][:, :], in1=hs, op=ALU.add),
                          [bres[b], r_ht[h3]], [r_ht[h3]])
                if not final:
                    C.dma("pool", hdst.ap()[tt * 128:(tt + 1) * 128, :], ht[h3][:], reads=[r_ht[h3]])
                else:
                    ss, rstd = fss[i][:, 0:1], fss[i][:, 1:2]
                    C.dve(lambda: nc.vector.scalar_tensor_tensor(out=junk[:], in0=ht[h3][:], scalar=1.0, in1=ht[h3][:],
                                                                 op0=ALU.mult, op1=ALU.mult, accum_out=ss),
                          [r_ht[h3]], [r_fss[i]])
                    C.dve(lambda: nc.vector.tensor_scalar(out=rstd, in0=ss, scalar1=1.0 / D, scalar2=EPS, op0=ALU.mult,
                                                          op1=ALU.add), [r_fss[i]], [r_fss[i]])
                    C.act(rstd, rstd, AF.Sqrt, [r_fss[i]], [r_fss[i]])
                    C.dve(lambda: nc.vector.reciprocal(out=rstd, in_=rstd), [r_fss[i]], [r_fss[i]])
                    C.dve(lambda: nc.vector.scalar_tensor_tensor(out=fo[i][:], in0=ht[h3][:], scalar=rstd, in1=gf[:],
                                                                 op0=ALU.mult, op1=ALU.mult),
                          [r_ht[h3], r_fss[i], r_gf], [r_fo[i]])
                    C.dma("pool", hdst.ap()[tt * 128:(tt + 1) * 128, :], fo[i][:], reads=[r_fo[i]])

    outproj_mlp(even_w_out, x, H1, 0, H2, False)
    if upto < 7:
        return _finish(C, nc)

    SPD = nc.dram_tensor("SPD", [16, S], F32, kind="ExternalOutput" if (dbgset is None and dbg) or (dbgset and "SPD" in dbgset) else "Internal")
    fm1 = []
    for m in range(8):
        fm1.append((m * 128, 128, 0.125, FT1.ap()[m * 128:(m + 1) * 128, :]))
    for m in range(8):
        fm1.append((1024 + m * 128, 128, None, FT1.ap()[1024 + m * 128:1024 + (m + 1) * 128, :]))
    fm1.append((3072, 16, None, None))
    tm1_groups = [[(2048, 512)], [(2560, 512)]]

    def pre1(env):
        ph = env["ph"]
        env["nb"] = ph.sb("nb", [16, 1], F32)
        env["e16"] = ph.sb("e16", [16, 512], F32)
        env["sp16"] = [ph.sb(f"sp16_{i}", [16, 512], F32) for i in range(2)]
        env["r_nb"], env["r_e"], env["r_sp"] = Res(), Res(), RL(2)
        C.dma("sp", env["nb"][:], odd_b_forget.ap(), writes=[env["r_nb"]])
        C.dve(lambda: nc.vector.tensor_scalar(out=env["nb"][:], in0=env["nb"][:], scalar1=-1.0, scalar2=None,
                                              op0=ALU.mult), [env["r_nb"]], [env["r_nb"]])

    def fm1_special(env, c, bank, rb):
        i = c % 2
        C.act(env["e16"][:], bank[0:16, :], AF.Exp, [rb, env["r_nb"]], [env["r_e"]], scale=-1.0, bias=env["nb"][:, 0:1])
        C.act(env["sp16"][i][:], env["e16"][:], AF.Ln, [env["r_e"]], [env["r_sp"][i]], bias=1.0)
        C.dma("pool", SPD.ap()[:, c * 512:(c + 1) * 512], env["sp16"][i][:], reads=[env["r_sp"][i]])

    def tm1_handler(env, tt, gi, bank, rb, tsb, r_tsb):
        C.evac(tsb[:, gi * 512:(gi + 1) * 512], bank[:, 0:512], [rb], [r_tsb])
        if gi == 1:
            C.dma("pool", TM1.ap()[tt * 128:(tt + 1) * 128, :], tsb[:, 0:1024], reads=[r_tsb])

    proj_phase(1, H2, odd_w_in, ODD_IN, mix_norm, fm1, tm1_groups, fm1_special, tm1_handler, pre=pre1)
    with Phase(C) as ph:
        Cm = ph.sb("Cm", [16, S], F32)
        spb = ph.sb("spb", [16, S], F32)
        ones16 = ph.sb("ones16", [16, S], F32)
        parts = ph.sb("parts", [16, 6, S], BF16)
        r_c = Res()
        C.dma("sp", spb[:], SPD.ap(), writes=[r_c])
        C.dve(lambda: nc.vector.memset(ones16[:], 1.0), [], [r_c])
        C.dve(lambda: nc.vector.tensor_tensor_scan(out=Cm[:], data0=ones16[:], data1=spb[:], initial=0.0, op0=ALU.mult,
                                                   op1=ALU.add), [r_c], [r_c])
        C.dve(lambda: nc.vector.tensor_copy(out=parts[:, 0, :], in_=Cm[:]), [r_c], [r_c])
        C.dve(lambda: nc.vector.tensor_tensor(out=spb[:], in0=Cm[:], in1=parts[:, 0, :], op=ALU.subtract), [r_c], [r_c])
        C.dve(lambda: nc.vector.tensor_copy(out=parts[:, 1, :], in_=spb[:]), [r_c], [r_c])
        C.dve(lambda: nc.vector.tensor_tensor(out=spb[:], in0=spb[:], in1=parts[:, 1, :], op=ALU.subtract), [r_c], [r_c])
        C.dve(lambda: nc.vector.tensor_copy(out=parts[:, 2, :], in_=spb[:]), [r_c], [r_c])
        for u in range(3):
            C.dve(lambda u=u: nc.vector.tensor_scalar(out=parts[:, 3 + u, :], in0=parts[:, u, :], scalar1=-1.0,
                                                      scalar2=None, op0=ALU.mult), [r_c], [r_c])
        for u in range(6):
            C.dma("pool", CUMP.ap()[u * 16:(u + 1) * 16, :], parts[:, u, :], reads=[r_c])
    if upto < 8:
        return _finish(C, nc)

    with Phase(C) as ph:
        A = alloc_attn(ph, S, 65, 128)
        tri = ph.sb("tri", [128, 128], BF16)
        r_tri = Res()
        C.dma("pool", tri[:], c_tri.ap(), writes=[r_tri])
        for aset in A["sets"]:
            C.dve(lambda aset=aset: nc.vector.memset(aset["VA"][:, :, 64:65], 1.0), [], [aset["r"]])
        osb = [ph.sb(f"fosb{i}", [128, NT, 64], BF16) for i in range(2)]
        r_osb = RL(2)

        def fox_fin(oi):
            def fin(A, items):
                dens = rden_multi(A, items, 64)
                for (qt, bank, rb), (den, rd) in zip(items, dens):
                    C.act(osb[oi][:, qt, :], bank[:, 0:64], AF.Copy, [rb, rd], [r_osb[oi]], scale=den[:, 0:1])
            return fin

        def fox_after(oi, col0):
            def after():
                C.dma("pool", OATT.ap()[:, col0:col0 + 64].rearrange("(t p) c -> p t c", p=128), osb[oi][:],
                      reads=[r_osb[oi]])
            return after

        def fox_wmul(aset, kt, qc, jlo, jhi):
            if kt >= 4 * qc:
                return [(jlo * 128, jlo * 128 + 128, tri[:], r_tri)]
            return []

        jobs = []
        for h in range(16):
            jobs.append(dict(
                qa=[(0, 64, ft_rows(FT1, h * 64)), (64, 3, bass.AP(CUMP, (48 + h) * S, [[16 * S, 3], [1, S]])),
                    (67, 3, B_ONES.ap()[0:3, :])],
                ka=[(0, 64, S, ft_rows(FT1, 1024 + h * 64)), (64, 3, S, B_ONES.ap()[0:3, :]),
                    (67, 3, S, bass.AP(CUMP, h * S, [[16 * S, 3], [1, S]]))],
                va=(TM1.ap()[:, h * 64:(h + 1) * 64], NT, 64), w=[],
                KQ=70, VW=65, tiles=causal_tiles, wmul=fox_wmul, fin=fox_fin(h % 2), after=fox_after(h % 2, h * 64)))
        run_jobs(A, jobs)
    if upto < 9:
        return _finish(C, nc)
    outproj_mlp(odd_w_out, H2, H3, 1, out, True)
    _finish(C, nc)


def _finish(C, nc):
    C.barrier()


def rel_bucket_np(n):
    n = np.maximum(n, 0)
    nf = np.maximum(n, 1).astype(np.float32)
    large = 16 + (np.log(nf / np.float32(16)) / np.float32(np.log(1024 / 16)) * np.float32(16)).astype(np.int32)
    large = np.minimum(large, 31)
    return np.where(n < 16, n, large)


def make_consts():
    c = {}
    c["c_ident"] = np.eye(128, dtype=np.float32)
    r = np.arange(LS) - 512
    oh = np.zeros((33, LS), np.float32)
    b = rel_bucket_np(r)
    valid = r >= 0
    oh[b[valid], np.nonzero(valid)[0]] = 1.0
    oh[32, ~valid] = 1.0
    c["c_ohc"] = oh
    ohw = np.zeros((33, LS), np.float32)
    validw = (r >= 0) & (r < 512)
    ohw[b[validw], np.nonzero(validw)[0]] = 1.0
    ohw[32, ~validw] = 1.0
    c["c_ohw"] = ohw
    r2 = np.arange(LC) - 4112
    b2 = rel_bucket_np(r2)
    ohk = np.zeros((33, LC), np.float32)
    v2 = r2 >= 0
    ohk[b2[v2], np.nonzero(v2)[0]] = 1.0
    ohk[32, ~v2] = 1.0
    c["c_ohcmp"] = ohk
    n = np.arange(16)[None, :]
    blk = (np.arange(32) // 2)[:, None]
    cm = np.where(n < blk, 0.0, -1e30).astype(np.float32)
    own = (n >= blk).astype(np.float32)
    c["c_cm16"] = np.tile(cm.reshape(1, 512), (128, 1))
    c["c_own16"] = np.tile(own.reshape(1, 512), (128, 1))
    t = (np.arange(32)[None, :, None] * 128 + np.arange(128)[:, None, None])
    sblk = t // 64
    m = np.arange(64)[None, None, :]
    forced = (m == 0) | (m == sblk) | (m == sblk - 1)
    f = np.where(forced, 1e6, np.where(m <= sblk, 0.0, -1e30)).astype(np.float32)
    c["c_force"] = f.reshape(128, 2048)
    ci = np.arange(256)[:, None] * 16
    sj = np.arange(64)[None, :] * 64
    ov = ((ci < sj + 64) & (ci + 32 > sj)).astype(np.float32)
    ov[255, :] = 0.0
    c["c_overlap"] = ov
    tok = np.arange(S)[None, :]
    c["c_ohb64"] = (tok // 64 == np.arange(64)[:, None]).astype(np.float32)
    c["c_ohb16"] = (tok // 256 == np.arange(16)[:, None]).astype(np.float32)
    c["c_tri"] = (np.arange(128)[None, :] >= np.arange(128)[:, None]).astype(np.float32)
    c["c_ones"] = np.ones((128, S), np.float32)
    return c


_CACHE = {}


def make_in_maps(inputs, cores):
    consts = make_consts()
    f = lambda a: np.ascontiguousarray(np.asarray(a, dtype=np.float32))
    shared = {
        "rel_bias": f(inputs["rel_bias"]),
        "mix_norm": f(inputs["mix_norm"]),
        "mlp_norm": f(inputs["mlp_norm"]),
        "even_w_in": f(inputs["even_w_in"][0]),
        "even_w_out": f(inputs["even_w_out"][0]),
        "cmp_pos_k": f(inputs["cmp_pos_k"][0]),
        "cmp_pos_v": f(inputs["cmp_pos_v"][0]),
        "cmp_k_w1": f(inputs["cmp_k_w1"][0]),
        "cmp_k_w2": f(inputs["cmp_k_w2"][0]),
        "cmp_v_w1": f(inputs["cmp_v_w1"][0]),
        "cmp_v_w2": f(inputs["cmp_v_w2"][0]),
        "odd_w_in": f(inputs["odd_w_in"][0]),
        "odd_b_forget": f(np.asarray(inputs["odd_b_forget"]).reshape(16, 1)),
        "odd_w_out": f(inputs["odd_w_out"][0]),
        "mlp_w1": f(np.asarray(inputs["mlp_w1"]).reshape(2 * D, DFF)),
        "mlp_w2": f(np.asarray(inputs["mlp_w2"]).reshape(2 * DFF, D)),
        "final_norm": f(np.asarray(inputs["final_norm"]).reshape(1, D)),
    }
    shared.update(consts)
    xs = np.asarray(inputs["x"], dtype=np.float32)
    maps = []
    for b in cores:
        m = dict(shared)
        m["x"] = np.ascontiguousarray(xs[b])
        maps.append(m)
    return maps


def kernel(**inputs):
    if "nc" not in _CACHE:
        nc = bass.Bass("TRN2", target_bir_lowering=False)
        build(nc)
        _CACHE["nc"] = nc
    nc = _CACHE["nc"]
    maps = make_in_maps(inputs, list(range(8)))
    res = run_bass_kernel_spmd(nc, maps, core_ids=list(range(8)))
    return np.stack([np.asarray(r["out"], dtype=np.float32) for r in res.results], axis=0)
```
